# Optimizing a Trainium2 kernel written in Bass

```python
import math
import jax
import jax.numpy as jnp
from jax import lax
import numpy as np

D_MODEL = 1024
BATCH = 4
SEQ = 8192
DEPTH = 2

N_MIXERS = 2
N_CONV_LAYERS = (DEPTH + 1) // 2
N_ATTN_LAYERS = DEPTH // 2
CONV_WIDTH = 3
N_HEADS = 16
HEAD_DIM = D_MODEL // N_HEADS
MOBA_BLOCK = 256
MOBA_TOPK = 3
Q_CHUNK = 64
ROPE_THETA = 10000.0
PEER_HEADS = 8
PEER_NKEYS = 128
PEER_EXPERTS = PEER_NKEYS * PEER_NKEYS
PEER_QDIM = 128
PEER_HALF = PEER_QDIM // 2
PEER_TOPK = 16
PEER_CHUNK = 128
RMS_EPS = 1e-6

kernel_name = "hybrid_shortconv_moba_peer"


def rms_norm(x, g):
    xf = x.astype(jnp.float32)
    y = xf * lax.rsqrt(jnp.mean(xf * xf, axis=-1, keepdims=True) + RMS_EPS)
    return (y * g.astype(jnp.float32)).astype(x.dtype)


def short_conv_mixer(h, w_in, conv_w, w_out):
    bcz = h @ w_in
    b_gate, c_gate, z = jnp.split(bcz, 3, axis=-1)
    u = c_gate * z
    u_conv = lax.conv_general_dilated(
        u, conv_w[:, None, :].astype(u.dtype), window_strides=(1,),
        padding=[(CONV_WIDTH - 1, 0)], dimension_numbers=('NWC', 'WIO', 'NWC'),
        feature_group_count=D_MODEL)
    return (b_gate * u_conv) @ w_out


def rope(x, positions):
    half = HEAD_DIM // 2
    inv = ROPE_THETA ** (-jnp.arange(half, dtype=jnp.float32) / half)
    ang = positions.astype(jnp.float32)[:, None] * inv[None, :]
    cos, sin = jnp.cos(ang), jnp.sin(ang)
    xf = x.astype(jnp.float32)
    x1, x2 = xf[..., :half], xf[..., half:]
    return jnp.concatenate([x1 * cos - x2 * sin, x2 * cos + x1 * sin], axis=-1).astype(x.dtype)


def moba_mixer(h, w_qkv, w_o):
    B, S, _ = h.shape
    qkv = (h @ w_qkv).reshape(B, S, 3, N_HEADS, HEAD_DIM)
    q = jnp.transpose(qkv[:, :, 0], (0, 2, 1, 3))
    k = jnp.transpose(qkv[:, :, 1], (0, 2, 1, 3))
    v = jnp.transpose(qkv[:, :, 2], (0, 2, 1, 3))
    pos = jnp.arange(S)
    q = rope(q, pos)
    k = rope(k, pos)
    nb = -(-S // MOBA_BLOCK)
    pad = nb * MOBA_BLOCK - S
    k_blk = jnp.pad(k, ((0, 0), (0, 0), (0, pad), (0, 0))).reshape(B, N_HEADS, nb, MOBA_BLOCK, HEAD_DIM)
    v_blk = jnp.pad(v, ((0, 0), (0, 0), (0, pad), (0, 0))).reshape(B, N_HEADS, nb, MOBA_BLOCK, HEAD_DIM)
    k_mean = jnp.mean(k_blk.astype(jnp.float32), axis=3)
    n_sel = min(MOBA_TOPK, nb)
    nc = S // Q_CHUNK
    q_c = q.reshape(B, N_HEADS, nc, Q_CHUNK, HEAD_DIM)
    scale = HEAD_DIM ** -0.5
    head_ix = jnp.arange(N_HEADS)[:, None, None]
    blk_ids = jnp.arange(nb)

    def chunk(idx):
        b = idx // nc
        c = idx % nc
        qb = q_c[b, :, c]
        kb = k_blk[b]
        vb = v_blk[b]
        j = (c * Q_CHUNK) // MOBA_BLOCK
        q_pos = c * Q_CHUNK + jnp.arange(Q_CHUNK)
        gate = jnp.einsum('hqd,hnd->hqn', qb.astype(jnp.float32), k_mean[b])
        gate = jnp.where((blk_ids < j)[None, None, :], gate, -jnp.inf)
        _, sel = lax.top_k(gate, n_sel)
        sel_valid = sel < j
        k_sel = kb[head_ix, sel]
        v_sel = vb[head_ix, sel]
        s_sel = jnp.einsum('hqd,hqntd->hqnt', qb, k_sel, preferred_element_type=jnp.float32) * scale
        s_sel = jnp.where(sel_valid[..., None], s_sel, -jnp.inf).reshape(N_HEADS, Q_CHUNK, n_sel * MOBA_BLOCK)
        k_own = lax.dynamic_index_in_dim(kb, j, axis=1, keepdims=False)
        v_own = lax.dynamic_index_in_dim(vb, j, axis=1, keepdims=False)
        s_own = jnp.einsum('hqd,htd->hqt', qb, k_own, preferred_element_type=jnp.float32) * scale
        k_pos = j * MOBA_BLOCK + jnp.arange(MOBA_BLOCK)
        s_own = jnp.where((k_pos[None, :] <= q_pos[:, None])[None], s_own, -jnp.inf)
        p = jax.nn.softmax(jnp.concatenate([s_sel, s_own], axis=-1), axis=-1).astype(vb.dtype)
        p_sel = p[..., :n_sel * MOBA_BLOCK].reshape(N_HEADS, Q_CHUNK, n_sel, MOBA_BLOCK)
        p_own = p[..., n_sel * MOBA_BLOCK:]
        return (jnp.einsum('hqnt,hqntd->hqd', p_sel, v_sel)
                + jnp.einsum('hqt,htd->hqd', p_own, v_own))

    out = lax.map(chunk, jnp.arange(B * nc))
    out = out.reshape(B, nc, N_HEADS, Q_CHUNK, HEAD_DIM).transpose(0, 1, 3, 2, 4)
    out = out.reshape(B, S, N_HEADS * HEAD_DIM)
    return out @ w_o


def peer_ffn(h, w_q, sub_k1, sub_k2, u_emb, v_emb):
    B, S, D = h.shape
    T = B * S
    xs = h.reshape(T // PEER_CHUNK, PEER_CHUNK, D)

    def chunk(xc):
        q = (xc @ w_q).reshape(PEER_CHUNK, PEER_HEADS, 2, PEER_HALF).astype(jnp.float32)
        s1 = jnp.einsum('thd,nd->thn', q[:, :, 0], sub_k1.astype(jnp.float32))
        s2 = jnp.einsum('thd,nd->thn', q[:, :, 1], sub_k2.astype(jnp.float32))
        v1, i1 = lax.top_k(s1, PEER_TOPK)
        v2, i2 = lax.top_k(s2, PEER_TOPK)
        cand = (v1[..., :, None] + v2[..., None, :]).reshape(PEER_CHUNK, PEER_HEADS, PEER_TOPK * PEER_TOPK)
        cidx = (i1[..., :, None] * PEER_NKEYS + i2[..., None, :]).reshape(PEER_CHUNK, PEER_HEADS, PEER_TOPK * PEER_TOPK)
        top_s, top_pos = lax.top_k(cand, PEER_TOPK)
        e_idx = jnp.take_along_axis(cidx, top_pos, axis=-1)
        g = jax.nn.softmax(top_s, axis=-1)
        u = u_emb[e_idx]
        a = jax.nn.gelu(jnp.einsum('td,thkd->thk', xc, u, preferred_element_type=jnp.float32), approximate=False)
        w = (g * a).astype(xc.dtype)
        return jnp.einsum('thk,thkd->td', w, v_emb[e_idx])

    return lax.map(chunk, xs).reshape(B, S, D)


def setup_inputs(seed: int = 0) -> dict:
    key = jax.random.key(seed)
    ks = jax.random.split(key, 16)
    D = D_MODEL
    f32 = jnp.float32
    x = jax.random.normal(ks[0], (BATCH, SEQ, D), f32)
    norm_mix = 1.0 + 0.01 * jax.random.normal(ks[1], (DEPTH, D), f32)
    norm_ffn = 1.0 + 0.01 * jax.random.normal(ks[2], (DEPTH, D), f32)
    conv_w_in = jax.random.normal(ks[3], (N_CONV_LAYERS, D, 3 * D), f32) * D ** -0.5
    conv_w = jax.random.normal(ks[4], (N_CONV_LAYERS, CONV_WIDTH, D), f32) * 0.5
    conv_w_out = jax.random.normal(ks[5], (N_CONV_LAYERS, D, D), f32) * D ** -0.5
    attn_w_qkv = jax.random.normal(ks[6], (N_ATTN_LAYERS, D, 3 * N_HEADS * HEAD_DIM), f32) * D ** -0.5
    attn_w_o = jax.random.normal(ks[7], (N_ATTN_LAYERS, N_HEADS * HEAD_DIM, D), f32) * (N_HEADS * HEAD_DIM) ** -0.5
    peer_w_q = jax.random.normal(ks[8], (DEPTH, D, PEER_HEADS * PEER_QDIM), f32) * D ** -0.5
    peer_k1 = jax.random.normal(ks[9], (DEPTH, PEER_NKEYS, PEER_HALF), f32) * PEER_HALF ** -0.5
    peer_k2 = jax.random.normal(ks[10], (DEPTH, PEER_NKEYS, PEER_HALF), f32) * PEER_HALF ** -0.5
    peer_u = jax.random.normal(ks[11], (DEPTH, PEER_EXPERTS, D), f32) * D ** -0.5
    peer_v = jax.random.normal(ks[12], (DEPTH, PEER_EXPERTS, D), f32) * D ** -0.5
    norm_final = 1.0 + 0.01 * jax.random.normal(ks[13], (D,), f32)
    return {"x": x, "norm_mix": norm_mix, "norm_ffn": norm_ffn,
            "conv_w_in": conv_w_in, "conv_w": conv_w, "conv_w_out": conv_w_out,
            "attn_w_qkv": attn_w_qkv, "attn_w_o": attn_w_o,
            "peer_w_q": peer_w_q, "peer_k1": peer_k1, "peer_k2": peer_k2,
            "peer_u": peer_u, "peer_v": peer_v, "norm_final": norm_final}


def reference(x, norm_mix, norm_ffn, conv_w_in, conv_w, conv_w_out, attn_w_qkv, attn_w_o,
              peer_w_q, peer_k1, peer_k2, peer_u, peer_v, norm_final):
    for i in range(DEPTH):
        hn = rms_norm(x, norm_mix[i])
        j = i // N_MIXERS
        if i % N_MIXERS == 0:
            x = x + short_conv_mixer(hn, conv_w_in[j], conv_w[j], conv_w_out[j])
        else:
            x = x + moba_mixer(hn, attn_w_qkv[j], attn_w_o[j])
        x = x + peer_ffn(rms_norm(x, norm_ffn[i]), peer_w_q[i], peer_k1[i], peer_k2[i], peer_u[i], peer_v[i])
    return rms_norm(x, norm_final)
```

```python
import numpy as np
import concourse.bass as bass
import concourse.mybir as mybir
from contextlib import ExitStack

F32 = mybir.dt.float32
BF16 = mybir.dt.bfloat16
I32 = mybir.dt.int32
U32 = mybir.dt.uint32
ALU = mybir.AluOpType
AF = mybir.ActivationFunctionType
AX = mybir.AxisListType

ENGS = ("pe", "act", "dve", "pool", "sp")


class Buf:
    def __init__(self, prog, t, name):
        self.p = prog
        self.t = t
        self.name = name
        self.last_w = None
        self.readers = {}
        self.dsem = None
        self.dcount = 0

    def __getitem__(self, idx):
        return self.t[idx]


class Prog:
    def __init__(self, nc, es, direct=True):
        self.nc = nc
        self.es = es
        self.direct = direct
        self.eng = {"pe": nc.tensor, "act": nc.scalar, "dve": nc.vector, "pool": nc.gpsimd, "sp": nc.sync}
        self.streams = {e: [] for e in ENGS}
        self.count = {e: 0 for e in ENGS}
        self.known = {e: {} for e in ENGS}
        self.sems = {}
        for e in ENGS:
            self.sems["E_" + e] = es.enter_context(nc.semaphore("E_" + e))
        self.free_dsems = []
        self.ndsem = 0
        self.dsem_count = {}
        self.all_bufs = []

    def sbuf(self, es, name, shape, dt):
        self.uid = getattr(self, "uid", 0) + 1
        t = es.enter_context(self.nc.sbuf_tensor("%s_%d" % (name, self.uid), list(shape), dt))
        b = Buf(self, t, name)
        return b

    def psum(self, es, name, shape, dt):
        self.uid = getattr(self, "uid", 0) + 1
        t = es.enter_context(self.nc.psum_tensor("%s_%d" % (name, self.uid), list(shape), dt))
        b = Buf(self, t, name)
        return b

    def _get_dsem(self, b):
        if b.dsem is None:
            if self.free_dsems:
                k = self.free_dsems.pop()
            else:
                k = "D_%d" % self.ndsem
                self.ndsem += 1
                self.sems[k] = self.es.enter_context(self.nc.semaphore(k))
                self.dsem_count[k] = 0
            b.dsem = k
            b.dcount = self.dsem_count[k]
            self.all_bufs.append(b)
        return b.dsem

    def release(self, bufs):
        for b in bufs:
            if b.dsem is not None:
                self.dsem_count[b.dsem] = b.dcount
                self.free_dsems.append(b.dsem)
                b.dsem = None

    def _collect(self, eng, reads, writes):
        deps = {}

        def add(ev):
            if ev is None:
                return
            k, v = ev
            if deps.get(k, 0) < v:
                deps[k] = v

        for b in reads:
            add(b.last_w)
        for b in writes:
            add(b.last_w)
            for k, v in b.readers.items():
                add((k, v))
        waits = []
        for k, v in deps.items():
            if eng == "pe" and k == "E_pe":
                continue
            if self.known[eng].get(k, 0) < v:
                self.known[eng][k] = v
                waits.append((k, v))
        return waits

    def op(self, eng, fn, reads=(), writes=()):
        waits = self._collect(eng, reads, writes)
        self.count[eng] += 1
        ev = ("E_" + eng, self.count[eng])
        self._push(eng, (waits, fn, ("E_" + eng, 1)))
        for b in reads:
            b.readers[ev[0]] = ev[1]
        for b in writes:
            b.last_w = ev
            b.readers = {}
        return ev

    def dma(self, q, out, in_, buf, load, **kw):
        if load:
            waits = self._collect(q, (), (buf,))
        else:
            waits = self._collect(q, (buf,), ())
        k = self._get_dsem(buf)
        buf.dcount += 16
        ev = (k, buf.dcount)
        self._push(q, (waits, lambda e: e.dma_start(out=out, in_=in_, **kw), (k, 16)))
        if load:
            buf.last_w = ev
            buf.readers = {}
        else:
            buf.readers[k] = buf.dcount
        return ev

    def dma2(self, q, out, in_, dst_buf, src_buf, **kw):
        raise NotImplementedError

    def barrier(self):
        waits = []
        for b in self.all_bufs:
            if b.dsem is not None and self.known["sp"].get(b.dsem, 0) < b.dcount:
                self.known["sp"][b.dsem] = b.dcount
                waits.append((b.dsem, b.dcount))
        for e in ENGS:
            if e == "sp":
                continue
            k = "E_" + e
            if self.known["sp"].get(k, 0) < self.count[e]:
                self.known["sp"][k] = self.count[e]
                waits.append((k, self.count[e]))
        self.count["sp"] += 1
        v = self.count["sp"]
        self._push("sp", (waits, lambda e, s=self.sems["E_sp"]: e.sem_inc(s, 1), None))
        for e in ENGS:
            if e == "sp":
                continue
            self.known[e]["E_sp"] = v
            self._push(e, ([("E_sp", v)], None, None))
            for k2, v2 in self.known["sp"].items():
                if self.known[e].get(k2, 0) < v2:
                    self.known[e][k2] = v2
        self.release(self.all_bufs)
        self.all_bufs = [b for b in self.all_bufs if False]

    def wait_all_dma(self, eng="sp"):
        waits = []
        for b in self.all_bufs:
            if b.dsem is not None and self.known[eng].get(b.dsem, 0) < b.dcount:
                self.known[eng][b.dsem] = b.dcount
                waits.append((b.dsem, b.dcount))
        self._push(eng, (waits, None, None))

    def _push(self, e, item):
        if not self.direct:
            self.streams[e].append(item)
            return
        waits, fn, inc = item
        eng = self.eng[e]
        for k, v in waits:
            eng.wait_ge(self.sems[k], v)
        if fn is not None:
            ins = fn(eng)
            if inc is not None:
                ins.then_inc(self.sems[inc[0]], inc[1])

    def emit(self):
        if self.direct:
            return
        nc = self.nc
        engmap = {"pe": "tensor", "act": "scalar", "dve": "vector", "pool": "gpsimd", "sp": "sync"}
        with nc.Block() as block:
            for e in ENGS:
                stream = self.streams[e]

                def body(eng, stream=stream):
                    for waits, fn, inc in stream:
                        for k, v in waits:
                            eng.wait_ge(self.sems[k], v)
                        if fn is not None:
                            ins = fn(eng)
                            if inc is not None:
                                ins.then_inc(self.sems[inc[0]], inc[1])

                getattr(block, engmap[e])(body)


D = 1024
NTOK = 4096
BLK = 256
NBLK = NTOK // BLK
EPS = 1e-6


def bcast_rows(ap1d, nparts):
    n = ap1d.shape[0]
    return ap1d.rearrange("(o n) -> o n", o=1).to_broadcast([nparts, n])


class NormScratch:
    def __init__(self, P, es, tag):
        self.junk = P.sbuf(es, "nj" + tag, [128, D], BF16)
        self.ss = P.sbuf(es, "nss" + tag, [128, 1], F32)
        self.ms = P.sbuf(es, "nms" + tag, [128, 1], F32)
        self.rstd = P.sbuf(es, "nrs" + tag, [128, 1], F32)
        self.hn = P.sbuf(es, "nhn" + tag, [128, D], BF16)
        self.pt = P.psum(es, "npt" + tag, [128, 8, 128], BF16)


def emit_rstd(P, xt, rows, S):
    P.op("act", lambda e: e.activation(out=S.junk[0:rows, :], in_=xt[0:rows, :], func=AF.Square,
                                       accum_out=S.ss[0:rows, :]), [xt], [S.junk, S.ss])
    P.op("dve", lambda e: e.tensor_scalar(out=S.ms[0:rows, :], in0=S.ss[0:rows, :], scalar1=1.0 / D, scalar2=EPS,
                                          op0=ALU.mult, op1=ALU.add), [S.ss], [S.ms])
    P.op("act", lambda e: e.activation(out=S.ms[0:rows, :], in_=S.ms[0:rows, :], func=AF.Sqrt), [S.ms], [S.ms])
    P.op("dve", lambda e: e.reciprocal(out=S.rstd[0:rows, :], in_=S.ms[0:rows, :]), [S.ms], [S.rstd])


def emit_norm_T(P, xt, rows, gt, idb, hnT, col0, S, evac="act"):
    emit_rstd(P, xt, rows, S)
    P.op("dve", lambda e: e.scalar_tensor_tensor(out=S.hn[0:rows, :], in0=xt[0:rows, :], scalar=S.rstd[0:rows, 0:1],
                                                 in1=gt[0:rows, :], op0=ALU.mult, op1=ALU.mult),
         [xt, S.rstd, gt], [S.hn])
    for dk in range(8):
        P.op("pe", lambda e, dk=dk: e.transpose(out=S.pt[:, dk, 0:rows], in_=S.hn[0:rows, dk * 128:(dk + 1) * 128],
                                                identity=idb[0:rows, 0:rows]), [S.hn, idb], [S.pt])
    if evac == "act":
        P.op("act", lambda e: e.copy(out=hnT[:, :, col0:col0 + rows], in_=S.pt[:, :, 0:rows]), [S.pt], [hnT])
    else:
        P.op("dve", lambda e: e.tensor_copy(out=hnT[:, :, col0:col0 + rows], in_=S.pt[:, :, 0:rows]), [S.pt], [hnT])


def phase_conv(P, nc, x0, xh, g_ap, w_in, conv_w, w_out, ident, x1, nblk=NBLK, side_jobs=None):
    with ExitStack() as es:
        side = prep_gen(P, es, side_jobs, ident) if side_jobs else None
        side_steps = (128 * len(side_jobs) + nblk - 1) // nblk if side_jobs else 0
        idb = P.sbuf(es, "c_idb", [128, 128], BF16)
        gt = P.sbuf(es, "c_gt", [128, D], F32)
        win = P.sbuf(es, "c_win", [128, 8, 3 * D], BF16)
        wout = P.sbuf(es, "c_wout", [128, 8, D], BF16)
        cw = P.sbuf(es, "c_cw", [128, 8, 3], F32)
        P.dma("pool", idb[:], ident, idb, True)
        P.dma("sp", gt[:], bcast_rows(g_ap, 128), gt, True)
        w_in_v = w_in.rearrange("(dk p) f -> p dk f", p=128)
        for dk in range(8):
            P.dma("pool", win[:, dk, :], w_in_v[:, dk, :], win, True)
        w_out_v = w_out.rearrange("(dk p) f -> p dk f", p=128)
        for dk in range(0, 8, 4):
            P.dma("pool", wout[:, dk:dk + 4, :], w_out_v[:, dk:dk + 4, :], wout, True)
        for kk in range(3):
            P.dma("sp", cw[:, :, kk], conv_w[kk, :].rearrange("(fc p) -> p fc", p=128), cw, True,
                  allow_slow_non_contiguous=True)
        sets = []
        for s in range(2):
            t = str(s)
            st = dict(
                xt=[P.sbuf(es, "c_xt%d_%s" % (i, t), [128, D], F32) for i in range(2)],
                xh=P.sbuf(es, "c_xh" + t, [2, D], F32),
                hnT=P.sbuf(es, "c_hnT" + t, [128, 8, BLK + 2], BF16),
                gT=P.sbuf(es, "c_gT" + t, [128, 8, BLK], BF16),
                S=NormScratch(P, es, "c" + t),
            )
            sets.append(st)
        csb = [P.sbuf(es, "c_csb%d" % i, [128, BLK + 2], F32) for i in range(2)]
        u = [P.sbuf(es, "c_u%d" % i, [128, BLK + 2], F32) for i in range(2)]
        acc = [P.sbuf(es, "c_acc%d" % i, [128, BLK], F32) for i in range(2)]
        pb = [P.psum(es, "c_pb%d" % i, [128, 512], F32) for i in range(3)]
        py = [P.psum(es, "c_py%d" % i, [128, 512], F32) for i in range(2)]
        N = BLK + 2
        nbuf = 0
        npy = 0
        for blk in range(nblk):
            st = sets[blk % 2]
            S = st["S"]
            hnT = st["hnT"]
            gT = st["gT"]
            P.dma("sp", st["xh"][:], xh[blk * 2:blk * 2 + 2, :], st["xh"], True)
            for i in range(2):
                P.dma("sp", st["xt"][i][:], x0[blk * BLK + i * 128: blk * BLK + (i + 1) * 128, :], st["xt"][i], True)
            emit_norm_T(P, st["xh"], 2, gt, idb, hnT, 0, S)
            for i in range(2):
                emit_norm_T(P, st["xt"][i], 128, gt, idb, hnT, 2 + i * 128, S)
            for fc in range(8):
                for j in range(3):
                    col = j * D + fc * 128
                    for dk in range(8):
                        P.op("pe", lambda e, j=j, dk=dk, col=col: e.matmul(
                            pb[j][:, 0:N], lhsT=win[:, dk, col:col + 128], rhs=hnT[:, dk, 0:N],
                            start=(dk == 0), stop=(dk == 7)), [win, hnT], [pb[j]])
                k = nbuf % 2
                nbuf += 1
                P.op("act", lambda e, k=k: e.copy(out=csb[k][:, 0:N], in_=pb[1][:, 0:N]), [pb[1]], [csb[k]])
                P.op("dve", lambda e, k=k: e.tensor_tensor(out=u[k][:, 0:N], in0=csb[k][:, 0:N], in1=pb[2][:, 0:N],
                                                           op=ALU.mult), [csb[k], pb[2]], [u[k]])
                P.op("dve", lambda e, k=k, fc=fc: e.tensor_scalar(out=acc[k][:], in0=u[k][:, 0:BLK],
                                                                   scalar1=cw[:, fc, 0:1], scalar2=None, op0=ALU.mult),
                     [u[k], cw], [acc[k]])
                for kk in (1, 2):
                    P.op("dve", lambda e, k=k, fc=fc, kk=kk: e.scalar_tensor_tensor(
                        out=acc[k][:], in0=u[k][:, kk:kk + BLK], scalar=cw[:, fc, kk:kk + 1], in1=acc[k][:],
                        op0=ALU.mult, op1=ALU.add), [u[k], cw, acc[k]], [acc[k]])
                P.op("dve", lambda e, k=k, fc=fc: e.tensor_tensor(out=gT[:, fc, :], in0=acc[k][:], in1=pb[0][:, 2:N],
                                                                   op=ALU.mult), [acc[k], pb[0]], [gT])
            for _ in range(side_steps):
                next(side, None)
            for i in range(2):
                xt = st["xt"][i]
                for fh in range(2):
                    pyk = py[npy % 2]
                    npy += 1
                    for fc in range(8):
                        P.op("pe", lambda e, fc=fc, i=i, fh=fh, pyk=pyk: e.matmul(
                            pyk[:, :], lhsT=gT[:, fc, i * 128:(i + 1) * 128], rhs=wout[:, fc, fh * 512:(fh + 1) * 512],
                            start=(fc == 0), stop=(fc == 7)), [gT, wout], [pyk])
                    P.op("dve", lambda e, xt=xt, fh=fh, pyk=pyk: e.tensor_tensor(
                        out=xt[:, fh * 512:(fh + 1) * 512], in0=xt[:, fh * 512:(fh + 1) * 512], in1=pyk[:, :],
                        op=ALU.add), [xt, pyk], [xt])
                P.dma("sp", x1[blk * BLK + i * 128: blk * BLK + (i + 1) * 128, :], xt[:], xt, False)
        P.barrier()


def block_of(core, jl):
    return core // 2, 2 * jl + (core % 2)


def shard_tokens(x, core):
    b = core // 2
    xs = np.empty((NTOK, x.shape[2]), x.dtype)
    xh = np.zeros((NBLK * 2, x.shape[2]), x.dtype)
    for jl in range(NBLK):
        _, j = block_of(core, jl)
        xs[jl * BLK:(jl + 1) * BLK] = x[b, j * BLK:(j + 1) * BLK]
        if j > 0:
            xh[jl * 2:jl * 2 + 2] = x[b, j * BLK - 2:j * BLK]
    return xs, xh


def unshard_tokens(outs, B, S):
    full = np.empty((B, S, outs[0].shape[1]), outs[0].dtype)
    for core, o in enumerate(outs):
        for jl in range(NBLK):
            b, j = block_of(core, jl)
            full[b, j * BLK:(j + 1) * BLK] = o[jl * BLK:(jl + 1) * BLK]
    return full


NH = 8
NKEY = 128
TOPK = 16
GRP = 256
NGRP = NTOK // GRP
NEG = -1.0e30


def fap(a, dims):
    return bass.AP(a.tensor, a.offset, [list(a.ap[0])] + [list(d) for d in dims])


def view(P, buf, name):
    b = Buf(P, buf.t, name)
    return b


def prep_gen(P, es, jobs, ident):
    idb = P.sbuf(es, "pg_idb", [128, 128], BF16)
    P.dma("pool", idb[:], ident, idb, True)
    ub = [P.sbuf(es, "pg_ub%d" % i, [128, D], BF16) for i in range(3)]
    ut = [P.sbuf(es, "pg_ut%d" % i, [128, 8, 128], BF16) for i in range(3)]
    vb = [P.sbuf(es, "pg_vb%d" % i, [128, D], BF16) for i in range(3)]
    pt = P.psum(es, "pg_pt", [128, 8, 128], BF16)
    n = 0
    for (u_ap, v_ap, UT_d, V_d) in jobs:
        for c in range(128):
            k = n % 3
            n += 1
            P.dma("pool", ub[k][:], u_ap[c * 128:(c + 1) * 128, :], ub[k], True)
            for dk in range(8):
                P.op("pe", lambda e, dk=dk, k=k: e.transpose(out=pt[:, dk, :], in_=ub[k][:, dk * 128:(dk + 1) * 128],
                                                           identity=idb[:]), [ub[k], idb], [pt])
            P.op("act", lambda e, k=k: e.copy(out=ut[k][:], in_=pt[:]), [pt], [ut[k]])
            P.dma("sp", UT_d[c].rearrange("p (dk q) -> p dk q", q=128), ut[k][:], ut[k], False)
            P.dma("pool", vb[k][:], v_ap[c * 128:(c + 1) * 128, :], vb[k], True)
            P.dma("sp", V_d[c * 128:(c + 1) * 128, :], vb[k][:], vb[k], False)
            yield


def phase_peer_prep(P, nc, u_ap, v_ap, ident, UT_d, V_d):
    with ExitStack() as es:
        idb = P.sbuf(es, "pp_idb", [128, 128], BF16)
        P.dma("pool", idb[:], ident, idb, True)
        ub = [P.sbuf(es, "pp_ub%d" % i, [128, D], BF16) for i in range(3)]
        ut = [P.sbuf(es, "pp_ut%d" % i, [128, 8, 128], BF16) for i in range(3)]
        vb = [P.sbuf(es, "pp_vb%d" % i, [128, D], BF16) for i in range(3)]
        pt = [P.psum(es, "pp_pt%d" % i, [128, 8, 128], BF16) for i in range(2)]
        for c in range(128):
            k = c % 3
            P.dma("pool", ub[k][:], u_ap[c * 128:(c + 1) * 128, :], ub[k], True)
            for dk in range(8):
                P.op("pe", lambda e, dk=dk, k=k, c=c: e.transpose(out=pt[c % 2][:, dk, :], in_=ub[k][:, dk * 128:(dk + 1) * 128],
                                                                 identity=idb[:]), [ub[k], idb], [pt[c % 2]])
            if c % 2:
                P.op("act", lambda e, k=k, c=c: e.copy(out=ut[k][:], in_=pt[c % 2][:]), [pt[c % 2]], [ut[k]])
            else:
                P.op("dve", lambda e, k=k, c=c: e.tensor_copy(out=ut[k][:], in_=pt[c % 2][:]), [pt[c % 2]], [ut[k]])
            P.dma("sp", UT_d[c].rearrange("p (dk q) -> p dk q", q=128), ut[k][:], ut[k], False)
            P.dma("pool", vb[k][:], v_ap[c * 128:(c + 1) * 128, :], vb[k], True)
            P.dma("sp", V_d[c * 128:(c + 1) * 128, :], vb[k][:], vb[k], False)
        P.barrier()


def phase_peer_route(P, nc, x_d, g_ap, wq_ap, k1_ap, k2_ap, ident, iota_ap, hnT_d, tab_d, ntiles=NTOK // 128, stop=99):
    with ExitStack() as es:
        idb = P.sbuf(es, "pr_idb", [128, 128], BF16)
        idf = P.sbuf(es, "pr_idf", [128, 128], F32)
        iot = P.sbuf(es, "pr_iot", [128, 128], F32)
        gt = P.sbuf(es, "pr_gt", [128, D], F32)
        wq = P.sbuf(es, "pr_wq", [128, 8, D], BF16)
        kcat = P.sbuf(es, "pr_kcat", [128, 128], BF16)
        kT = P.sbuf(es, "pr_kT", [128, 128], BF16)
        P.dma("pool", idb[:], ident, idb, True)
        P.dma("sp", idf[:], ident, idf, True)
        P.dma("sp", iot[:], iota_ap, iot, True)
        P.dma("sp", gt[:], bcast_rows(g_ap, 128), gt, True)
        wq_v = wq_ap.rearrange("(dk p) f -> p dk f", p=128)
        for dk in range(0, 8, 4):
            P.dma("pool", wq[:, dk:dk + 4, :], wq_v[:, dk:dk + 4, :], wq, True)
        P.dma("pool", kcat[:, 0:64], k1_ap, kcat, True)
        P.dma("pool", kcat[:, 64:128], k2_ap, kcat, True)
        Ss = [NormScratch(P, es, "pr%d" % i) for i in range(2)]
        P.op("pe", lambda e: e.transpose(out=Ss[0].pt[:, 0, :], in_=kcat[:], identity=idb[:]), [kcat, idb], [Ss[0].pt])
        P.op("act", lambda e: e.copy(out=kT[:], in_=Ss[0].pt[:, 0, :]), [Ss[0].pt], [kT])
        kblk = P.sbuf(es, "pr_kblk", [128, 256], BF16)
        P.op("dve", lambda e: e.memset(kblk[:], 0.0), [], [kblk])
        P.op("dve", lambda e: e.tensor_copy(out=kblk[0:64, 0:128], in_=kT[0:64, :]), [kT], [kblk])
        P.op("dve", lambda e: e.tensor_copy(out=kblk[64:128, 128:256], in_=kT[64:128, :]), [kT], [kblk])
        xts = [P.sbuf(es, "pr_xt%d" % i, [128, D], F32) for i in range(2)]
        hnTs = [P.sbuf(es, "pr_hnT%d" % i, [128, 8, 128], BF16) for i in range(2)]
        qTs = [P.sbuf(es, "pr_qT%d" % i, [128, 8, 128], BF16) for i in range(2)]
        ps_q = P.psum(es, "pr_psq", [128, 4, 128], F32)
        ps_s = [P.psum(es, "pr_pss%d" % i, [128, 512], F32) for i in range(4)]
        ps_t = P.psum(es, "pr_pst", [128, 3, 128], F32)
        s_sb = P.sbuf(es, "pr_s", [128, 16 * 128], F32)
        segs = [view(P, s_sb, "seg%d" % j) for j in range(16)]
        v = P.sbuf(es, "pr_v", [128, 16, 16], F32)
        vseg = [view(P, v, "vseg%d" % j) for j in range(16)]
        ix = P.sbuf(es, "pr_ix", [128, 16, 16], U32)
        ixseg = [view(P, ix, "ixseg%d" % j) for j in range(16)]
        ixf = P.sbuf(es, "pr_ixf", [128, 16, 16], F32)
        cand = P.sbuf(es, "pr_cand", [128, 8, 256], F32)
        cseg = [view(P, cand, "cseg%d" % j) for j in range(8)]
        ts = P.sbuf(es, "pr_ts", [128, 8, 16], F32)
        tsseg = [view(P, ts, "tsseg%d" % j) for j in range(8)]
        pos = P.sbuf(es, "pr_pos", [128, 8, 16], U32)
        posseg = [view(P, pos, "posseg%d" % j) for j in range(8)]
        ku = [P.sbuf(es, "pr_ku%d" % i, [128, 8, 16], U32) for i in range(2)]
        kf = [P.sbuf(es, "pr_kf%d" % i, [128, 8, 16], F32) for i in range(2)]
        eq = [P.sbuf(es, "pr_eq%d" % i, [128, 128, 16], F32) for i in range(2)]
        res = P.sbuf(es, "pr_res", [128, 3, 128], F32)
        resv = [view(P, res, "resv%d" % j) for j in range(3)]
        dd = P.sbuf(es, "pr_dd", [128, 8, 16], F32)
        ee = P.sbuf(es, "pr_ee", [128, 8, 16], F32)
        zz = P.sbuf(es, "pr_zz", [128, 8], F32)
        rz = P.sbuf(es, "pr_rz", [128, 8], F32)
        resT = [P.sbuf(es, "pr_resT%d" % i, [128, 3, 128], F32) for i in range(2)]
        tab_v = tab_d.rearrange("a p t -> p a t")
        for tile in range(ntiles):
            k = tile % 2
            xt, S, hnT, qT = xts[k], Ss[k], hnTs[k], qTs[k]
            grp, sub = tile // 2, tile % 2
            P.dma("sp", xt[:], x_d[tile * 128:(tile + 1) * 128, :], xt, True)
            emit_norm_T(P, xt, 128, gt, idb, hnT, 0, S)
            P.dma("sp", hnT_d[grp].rearrange("p (dk t) -> p dk t", t=GRP)[:, :, sub * 128:(sub + 1) * 128], hnT[:], hnT, False)
            if stop <= 1:
                continue
            for half in range(2):
                for f4 in range(4):
                    fc = half * 4 + f4
                    for dk in range(8):
                        P.op("pe", lambda e, fc=fc, f4=f4, dk=dk: e.matmul(
                            ps_q[:, f4, :], lhsT=wq[:, dk, fc * 128:(fc + 1) * 128], rhs=hnT[:, dk, :],
                            start=(dk == 0), stop=(dk == 7)), [wq, hnT], [ps_q])
                P.op("act", lambda e, half=half: e.copy(out=qT[:, half * 4:(half + 1) * 4, :], in_=ps_q[:]), [ps_q], [qT])
            if stop <= 2:
                continue
            for b4 in range(4):
                for hh in range(2):
                    h = b4 * 2 + hh
                    P.op("pe", lambda e, b4=b4, hh=hh, h=h: e.matmul(
                        ps_s[b4][:, hh * 256:(hh + 1) * 256], lhsT=qT[:, h, :], rhs=kblk[:, :],
                        start=True, stop=True), [qT, kblk], [ps_s[b4]])
                P.op("act", lambda e, b4=b4: e.copy(out=s_sb[:, b4 * 512:(b4 + 1) * 512], in_=ps_s[b4][:]),
                     [ps_s[b4]], segs[b4 * 4:(b4 + 1) * 4])
            if stop <= 3:
                continue
            sg = lambda j: s_sb[:, j * 128:(j + 1) * 128]
            for j in range(16):
                P.op("dve", lambda e, j=j: e.max(out=v[:, j, 0:8], in_=sg(j)), [segs[j]], [vseg[j]])
            for j in range(16):
                P.op("dve", lambda e, j=j: e.max_index(out=ix[:, j, 0:8], in_max=v[:, j, 0:8], in_values=sg(j)),
                     [segs[j], vseg[j]], [ixseg[j]])
            for j in range(16):
                P.op("dve", lambda e, j=j: e.match_replace(out=sg(j), in_to_replace=v[:, j, 0:8], in_values=sg(j),
                                                           imm_value=NEG), [segs[j], vseg[j]], [segs[j]])
            for j in range(16):
                P.op("dve", lambda e, j=j: e.max(out=v[:, j, 8:16], in_=sg(j)), [segs[j]], [vseg[j]])
            for j in range(16):
                P.op("dve", lambda e, j=j: e.max_index(out=ix[:, j, 8:16], in_max=v[:, j, 8:16], in_values=sg(j)),
                     [segs[j], vseg[j]], [ixseg[j]])
            if stop <= 4:
                continue
            P.op("dve", lambda e: e.tensor_copy(out=ixf[:], in_=ix[:]), ixseg, [ixf])
            P.op("dve", lambda e: e.tensor_tensor(
                out=fap(cand[:], [(256, 8), (16, 16), (1, 16)]),
                in0=fap(v[:], [(32, 8), (1, 16), (0, 16)]),
                in1=fap(v[:, 1, :], [(32, 8), (0, 16), (1, 16)]), op=ALU.add), vseg, cseg)
            cs = lambda h: cand[:, h, :]
            for h in range(8):
                P.op("dve", lambda e, h=h: e.max(out=ts[:, h, 0:8], in_=cs(h)), [cseg[h]], [tsseg[h]])
            for h in range(8):
                P.op("dve", lambda e, h=h: e.max_index(out=pos[:, h, 0:8], in_max=ts[:, h, 0:8], in_values=cs(h)),
                     [cseg[h], tsseg[h]], [posseg[h]])
            for h in range(8):
                P.op("dve", lambda e, h=h: e.match_replace(out=cs(h), in_to_replace=ts[:, h, 0:8], in_values=cs(h),
                                                           imm_value=NEG), [cseg[h], tsseg[h]], [cseg[h]])
            for h in range(8):
                P.op("dve", lambda e, h=h: e.max(out=ts[:, h, 8:16], in_=cs(h)), [cseg[h]], [tsseg[h]])
            for h in range(8):
                P.op("dve", lambda e, h=h: e.max_index(out=pos[:, h, 8:16], in_max=ts[:, h, 8:16], in_values=cs(h)),
                     [cseg[h], tsseg[h]], [posseg[h]])
            if stop <= 5:
                continue
            P.op("dve", lambda e: e.tensor_single_scalar(out=ku[0][:], in_=pos[:], scalar=4, op=ALU.logical_shift_right),
                 posseg, [ku[0]])
            P.op("dve", lambda e: e.tensor_single_scalar(out=ku[1][:], in_=pos[:], scalar=15, op=ALU.bitwise_and),
                 posseg, [ku[1]])
            for sd in range(2):
                P.op("dve", lambda e, sd=sd: e.tensor_copy(out=kf[sd][:], in_=ku[sd][:]), [ku[sd]], [kf[sd]])
            if stop <= 6:
                continue
            for sd in range(2):
                P.op("dve", lambda e, sd=sd: e.tensor_tensor(
                    out=eq[sd][:], in0=fap(iot[:], [(0, 128), (1, 16)]), in1=fap(kf[sd][:], [(1, 128), (0, 16)]),
                    op=ALU.is_equal), [iot, kf[sd]], [eq[sd]])
                P.op("dve", lambda e, sd=sd: e.tensor_tensor(
                    out=fap(eq[sd][:], [(256, 8), (16, 16), (1, 16)]), in0=fap(eq[sd][:], [(256, 8), (16, 16), (1, 16)]),
                    in1=fap(ixf[:, sd, :], [(32, 8), (0, 16), (1, 16)]), op=ALU.mult), [eq[sd], ixf], [eq[sd]])
                P.op("dve", lambda e, sd=sd: e.tensor_reduce(out=res[:, sd, :], in_=eq[sd][:], axis=AX.X, op=ALU.add),
                     [eq[sd]], [resv[sd]])
            if stop <= 7:
                continue
            P.op("dve", lambda e: e.tensor_tensor(out=dd[:], in0=ts[:], in1=fap(ts[:], [(16, 8), (0, 16)]),
                                                  op=ALU.subtract), tsseg, [dd])
            P.op("act", lambda e: e.activation(out=ee[:], in_=dd[:], func=AF.Exp), [dd], [ee])
            P.op("dve", lambda e: e.tensor_reduce(out=zz[:], in_=ee[:], axis=AX.X, op=ALU.add), [ee], [zz])
            P.op("dve", lambda e: e.reciprocal(out=rz[:], in_=zz[:]), [zz], [rz])
            P.op("dve", lambda e: e.tensor_tensor(out=fap(res[:, 2, :], [(16, 8), (1, 16)]), in0=ee[:],
                                                  in1=fap(rz[:], [(1, 8), (0, 16)]), op=ALU.mult), [ee, rz], [resv[2]])
            if stop <= 8:
                continue
            for a in range(3):
                P.op("pe", lambda e, a=a: e.transpose(out=ps_t[:, a, :], in_=res[:, a, :], identity=idf[:]),
                     [resv[a], idf], [ps_t])
            P.op("act", lambda e, k=k: e.copy(out=resT[k][:], in_=ps_t[:]), [ps_t], [resT[k]])
            P.dma("sp", tab_v[:, :, tile * 128:(tile + 1) * 128], resT[k][:], resT[k], False)
        P.barrier()


def phase_peer_experts(P, nc, x_d, hnT_d, tab_d, UT_d, V_d, iota_ap, out_d, gfin_ap=None, ngrp=NGRP):
    SUBT = 8
    with ExitStack() as es:
        iot = P.sbuf(es, "pe_iot", [128, 128], BF16)
        P.dma("pool", iot[:], iota_ap, iot, True)
        GS = [P.sbuf(es, "pe_GS%d" % i, [128, 128, GRP], BF16) for i in range(2)]
        NR = 4
        utr = [P.sbuf(es, "pe_ut%d" % i, [128, 8, 128], BF16) for i in range(NR)]
        vr = [P.sbuf(es, "pe_v%d" % i, [128, D], BF16) for i in range(NR)]
        hnT = [P.sbuf(es, "pe_hnT%d" % i, [128, 8, GRP], BF16) for i in range(2)]
        tab = [P.sbuf(es, "pe_tab%d" % i, [128, 3, GRP], F32) for i in range(2)]
        xt = [P.sbuf(es, "pe_xt%d" % i, [128, D], F32) for i in range(2)]
        NSET = 3
        O1 = [P.sbuf(es, "pe_O1_%d" % i, [128, SUBT, 128], BF16) for i in range(NSET)]
        O2g = [P.sbuf(es, "pe_O2g_%d" % i, [128, SUBT, 128], BF16) for i in range(NSET)]
        tabb = [P.sbuf(es, "pe_tabb%d" % i, [128, 3, GRP], BF16) for i in range(2)]
        ga = [P.sbuf(es, "pe_ga%d" % i, [128, GRP], BF16) for i in range(2)]
        pT = [P.sbuf(es, "pe_pT%d" % i, [128, GRP], BF16) for i in range(3)]
        po = [[P.psum(es, "pe_po%d%d" % (i, j), [128, 512], F32) for j in range(2)] for i in range(2)]
        pa = [P.psum(es, "pe_pa%d" % i, [128, 512], F32) for i in range(2)]
        pg = [P.psum(es, "pe_pg%d" % i, [128, 4, 128], F32) for i in range(2)]
        if gfin_ap is not None:
            gfin = P.sbuf(es, "pe_gfin", [128, D], F32)
            P.dma("sp", gfin[:], bcast_rows(gfin_ap, 128), gfin, True)
            S = NormScratch.__new__(NormScratch)
            S.junk = P.sbuf(es, "pe_nj", [128, D], BF16)
            S.ss = P.sbuf(es, "pe_nss", [128, 1], F32)
            S.ms = P.sbuf(es, "pe_nms", [128, 1], F32)
            S.rstd = P.sbuf(es, "pe_nrs", [128, 1], F32)
        tab_v = tab_d.rearrange("a p t -> p a t")

        def load_group(g):
            P.dma("sp", tab[g % 2][:], tab_v[:, :, g * GRP:(g + 1) * GRP], tab[g % 2], True)
            P.op("pool", lambda e: e.tensor_copy(out=tabb[g % 2][:], in_=tab[g % 2][:]), [tab[g % 2]], [tabb[g % 2]])

        def stageA(g, sb):
            tb = tabb[g % 2]
            t0 = sb * SUBT
            k = sb % NSET
            P.op("dve", lambda e: e.tensor_tensor(out=O1[k][:], in0=fap(iot[:], [(0, SUBT), (1, 128)]),
                                                  in1=fap(tb[:, 0, t0:t0 + SUBT], [(1, SUBT), (0, 128)]), op=ALU.is_equal),
                 [iot, tb], [O1[k]])
            P.op("dve", lambda e: e.tensor_tensor(out=O2g[k][:], in0=fap(iot[:], [(0, SUBT), (1, 128)]),
                                                  in1=fap(tb[:, 1, t0:t0 + SUBT], [(1, SUBT), (0, 128)]), op=ALU.is_equal),
                 [iot, tb], [O2g[k]])
            P.op("pool", lambda e: e.tensor_tensor(out=O2g[k][:], in0=O2g[k][:],
                                                   in1=fap(tb[:, 2, t0:t0 + SUBT], [(1, SUBT), (0, 128)]), op=ALU.mult),
                 [O2g[k], tb], [O2g[k]])

        def stageB(g, sb):
            k = sb % NSET
            for q4 in range(SUBT // 4):
                pgk = pg[q4]
                for tl in range(4):
                    t = q4 * 4 + tl
                    P.op("pe", lambda e, t=t, tl=tl, pgk=pgk: e.matmul(pgk[:, tl, :], lhsT=O2g[k][:, t, :], rhs=O1[k][:, t, :],
                                                                     start=True, stop=True), [O2g[k], O1[k]], [pgk])

        def stageC(g, sb):
            gs = GS[g % 2]
            t0 = sb * SUBT
            for q4 in range(SUBT // 4):
                pgk = pg[q4]
                tt = t0 + q4 * 4
                P.op("act", lambda e, pgk=pgk, tt=tt: e.copy(out=fap(gs[:, 0, tt:tt + 1], [(1, 4), (GRP, 128)]), in_=pgk[:]),
                     [pgk], [gs])

        def gbuild_sub(g, sb):
            stageA(g, sb)
            stageB(g, sb)
            stageC(g, sb)

        nsubs = GRP // SUBT
        load_group(0)
        for sb in range(nsubs):
            gbuild_sub(0, sb)
        jobs = [(g, c) for g in range(ngrp) for c in range(128)]

        def emit_pa(i):
            g, c = jobs[i]
            hg = hnT[g % 2]
            if c == 0:
                P.dma("sp", hg[:], hnT_d[g].rearrange("p (dk t) -> p dk t", t=GRP), hg, True)
                if g + 1 < ngrp:
                    load_group(g + 1)
            r = i % NR
            P.dma("sp", utr[r][:], UT_d[c].rearrange("p (dk q) -> p dk q", q=128), utr[r], True)
            P.dma("sp", vr[r][:], V_d[c * 128:(c + 1) * 128, :], vr[r], True)
            pak = pa[i % 2]
            for dk in range(8):
                P.op("pe", lambda e, dk=dk, r=r, pak=pak: e.matmul(pak[:, 0:GRP], lhsT=utr[r][:, dk, :], rhs=hg[:, dk, :],
                                                                 start=(dk == 0), stop=(dk == 7)), [utr[r], hg], [pak])

        emit_pa(0)
        for i, (g, c) in enumerate(jobs):
            if i + 1 < len(jobs):
                emit_pa(i + 1)
            gs = GS[g % 2]
            r = i % NR
            pak = pa[i % 2]
            gk = ga[i % 2]
            pk = pT[i % 3]
            P.op("act", lambda e, gk=gk, pak=pak: e.activation(out=gk[:], in_=pak[:, 0:GRP], func=AF.Gelu), [pak], [gk])
            P.op("dve", lambda e, gk=gk, pk=pk, c=c, gs=gs: e.tensor_tensor(out=pk[:], in0=gk[:], in1=gs[:, c, :],
                                                                           op=ALU.mult), [gk, gs], [pk])
            for tsb in range(2):
                for dh in range(2):
                    P.op("pe", lambda e, tsb=tsb, dh=dh, pk=pk, r=r, c=c: e.matmul(
                        po[tsb][dh][:], lhsT=pk[:, tsb * 128:(tsb + 1) * 128], rhs=vr[r][:, dh * 512:(dh + 1) * 512],
                        start=(c == 0), stop=(c == 127)), [pk, vr[r]], [po[tsb][dh]])
            if g + 1 < ngrp:
                if c % 4 == 0:
                    stageA(g + 1, c // 4)
                elif c % 4 == 2:
                    stageB(g + 1, c // 4)
                elif c % 4 == 3:
                    stageC(g + 1, c // 4)
            if c != 127:
                continue
            for tsb in range(2):
                row0 = g * GRP + tsb * 128
                x_t = xt[tsb]
                P.dma("sp", x_t[:], x_d[row0:row0 + 128, :], x_t, True)
                for dh in range(2):
                    P.op("dve", lambda e, x_t=x_t, dh=dh, tsb=tsb: e.tensor_tensor(
                        out=x_t[:, dh * 512:(dh + 1) * 512], in0=x_t[:, dh * 512:(dh + 1) * 512], in1=po[tsb][dh][:],
                        op=ALU.add), [x_t, po[tsb][dh]], [x_t])
                if gfin_ap is not None:
                    emit_rstd(P, x_t, 128, S)
                    P.op("dve", lambda e, x_t=x_t: e.scalar_tensor_tensor(out=x_t[:], in0=x_t[:], scalar=S.rstd[:, 0:1],
                                                                         in1=gfin[:], op0=ALU.mult, op1=ALU.mult),
                         [x_t, S.rstd, gfin], [x_t])
                P.dma("sp", out_d[row0:row0 + 128, :], x_t[:], x_t, False)
        P.barrier()


def build_peer(final, phases=(1, 1, 1), debug=False, ntiles=NTOK // 128, stop=99):
    nc = bass.Bass("TRN2", target_bir_lowering=False)
    x = nc.dram_tensor("x", [NTOK, D], F32, kind="ExternalInput").ap()
    g = nc.dram_tensor("g", [D], F32, kind="ExternalInput").ap()
    wq = nc.dram_tensor("wq", [D, D], F32, kind="ExternalInput").ap()
    k1 = nc.dram_tensor("k1", [NKEY, 64], F32, kind="ExternalInput").ap()
    k2 = nc.dram_tensor("k2", [NKEY, 64], F32, kind="ExternalInput").ap()
    u = nc.dram_tensor("u", [NKEY * NKEY, D], F32, kind="ExternalInput").ap()
    v = nc.dram_tensor("v", [NKEY * NKEY, D], F32, kind="ExternalInput").ap()
    ident = nc.dram_tensor("ident", [128, 128], F32, kind="ExternalInput").ap()
    iota = nc.dram_tensor("iota", [128, 128], F32, kind="ExternalInput").ap()
    gfin = nc.dram_tensor("gfin", [D], F32, kind="ExternalInput").ap() if final else None
    out = nc.dram_tensor("out", [NTOK, D], F32, kind="ExternalOutput").ap()
    kd = "ExternalOutput" if debug else "Internal"
    UT_d = nc.dram_tensor("UT_d", [128, 128, D], BF16, kind=kd).ap()
    V_d = nc.dram_tensor("V_d", [NKEY * NKEY, D], BF16, kind=kd).ap()
    hnT_d = nc.dram_tensor("hnT_d", [NGRP, 128, 8 * GRP], BF16, kind=kd).ap()
    tab_d = nc.dram_tensor("tab_d", [3, 128, NTOK], F32, kind=kd).ap()
    with ExitStack() as es:
        P = Prog(nc, es)
        if phases[0]:
            phase_peer_prep(P, nc, u, v, ident, UT_d, V_d)
        if phases[1]:
            phase_peer_route(P, nc, x, g, wq, k1, k2, ident, iota, hnT_d, tab_d, ntiles, stop)
        if phases[2]:
            phase_peer_experts(P, nc, x, hnT_d, tab_d, UT_d, V_d, iota, out, gfin)
    return nc


def consts():
    return {"ident": np.eye(128, dtype=np.float32),
            "iota": np.tile(np.arange(128, dtype=np.float32)[None, :], (128, 1))}


NQT_ = NTOK // 128
NHEAD = 16
DH = 64
SEQ = 8192
NKT = SEQ // 128
NQT = NTOK // 128
NB = SEQ // BLK
BIG = 30000.0


def phase_norm_T(P, nc, x_d, g_ap, ident, hnT_own, ntiles=NQT_):
    with ExitStack() as es:
        idb = P.sbuf(es, "n_idb", [128, 128], BF16)
        gt = P.sbuf(es, "n_gt", [128, D], F32)
        P.dma("pool", idb[:], ident, idb, True)
        P.dma("sp", gt[:], bcast_rows(g_ap, 128), gt, True)
        Ss = [NormScratch(P, es, "n%d" % i) for i in range(2)]
        xts = [P.sbuf(es, "n_xt%d" % i, [128, D], F32) for i in range(2)]
        hnTs = [P.sbuf(es, "n_hnT%d" % i, [128, 8, 128], BF16) for i in range(2)]
        for tile in range(ntiles):
            k = tile % 2
            P.dma("sp", xts[k][:], x_d[tile * 128:(tile + 1) * 128, :], xts[k], True)
            emit_norm_T(P, xts[k], 128, gt, idb, hnTs[k], 0, Ss[k], evac=("act" if k else "dve"))
            P.dma("sp", hnT_own[tile].rearrange("p (dk t) -> p dk t", t=128), hnTs[k][:], hnTs[k], False)
        P.barrier()


def emit_rope(P, src, cs, dst, tmp):
    x1 = fap(src[:, 0:1], [(64, 16), (1, 32)])
    x2 = fap(src[:, 32:33], [(64, 16), (1, 32)])
    cosb = fap(cs[:, 0:1], [(0, 16), (1, 32)])
    sinb = fap(cs[:, 32:33], [(0, 16), (1, 32)])
    o1 = fap(dst[:, 0:1], [(64, 16), (1, 32)])
    o2 = fap(dst[:, 32:33], [(64, 16), (1, 32)])
    t1, t2, t3, t4 = tmp
    P.op("dve", lambda e: e.tensor_tensor(out=t1[:], in0=x1, in1=cosb, op=ALU.mult), [src, cs], [t1])
    P.op("dve", lambda e: e.tensor_tensor(out=t2[:], in0=x2, in1=sinb, op=ALU.mult), [src, cs], [t2])
    P.op("dve", lambda e: e.tensor_tensor(out=o1, in0=t1[:], in1=t2[:], op=ALU.subtract), [t1, t2], [dst])
    P.op("pool", lambda e: e.tensor_tensor(out=t3[:], in0=x2, in1=cosb, op=ALU.mult), [src, cs], [t3])
    P.op("pool", lambda e: e.tensor_tensor(out=t4[:], in0=x1, in1=sinb, op=ALU.mult), [src, cs], [t4])
    P.op("pool", lambda e: e.tensor_tensor(out=o2, in0=t3[:], in1=t4[:], op=ALU.add), [t3, t4], [dst])


def phase_attn_qkv(P, nc, hnT_all, hnT_own, w_qkv, cs_k, cs_q, ident, KT_d, V_d, QT_d):
    with ExitStack() as es:
        idb = P.sbuf(es, "aq_idb", [128, 128], BF16)
        P.dma("pool", idb[:], ident, idb, True)
        w = P.sbuf(es, "aq_w", [128, 8, 3 * D], BF16)
        w_v = w_qkv.rearrange("(dk p) f -> p dk f", p=128)
        for dk in range(8):
            P.dma("pool", w[:, dk, :], w_v[:, dk, :], w, True)
        hn = [P.sbuf(es, "aq_hn%d" % i, [128, 8, 128], BF16) for i in range(2)]
        cs = [P.sbuf(es, "aq_cs%d" % i, [128, 64], F32) for i in range(2)]
        ksb = [P.sbuf(es, "aq_ksb%d" % i, [128, D], F32) for i in range(2)]
        krot = [P.sbuf(es, "aq_krot%d" % i, [128, D], BF16) for i in range(2)]
        vsb = [P.sbuf(es, "aq_vsb%d" % i, [128, D], BF16) for i in range(2)]
        tmp = [[P.sbuf(es, "aq_t%d_%d" % (i, j), [128, 16, 32], F32) for j in range(4)] for i in range(2)]
        kTs = [P.sbuf(es, "aq_kT%d" % i, [128, 8, 128], BF16) for i in range(2)]
        pk = [P.psum(es, "aq_pk%d" % i, [128, 512], F32) for i in range(2)]
        pv = [P.psum(es, "aq_pv%d" % i, [128, 512], F32) for i in range(2)]
        pt = [P.psum(es, "aq_pt%d" % i, [128, 8, 128], BF16) for i in range(2)]
        KT_v = KT_d.rearrange("(hp two) d k -> (two d) hp k", two=2)
        QT_v = QT_d.rearrange("(hp two) d k -> (two d) hp k", two=2)

        def proj(hk, col0, ps):
            for half in range(2):
                for dk in range(8):
                    P.op("pe", lambda e, half=half, dk=dk: e.matmul(
                        ps[half][:], lhsT=hk[:, dk, :], rhs=w[:, dk, col0 + half * 512: col0 + (half + 1) * 512],
                        start=(dk == 0), stop=(dk == 7)), [hk, w], [ps[half]])

        def rope_part1(i, ps, cs_ap, scale):
            k = i % 2
            P.dma("sp", cs[k][:], cs_ap, cs[k], True)
            for half in range(2):
                P.op("act", lambda e, half=half, k=k: e.activation(out=ksb[k][:, half * 512:(half + 1) * 512], in_=ps[half][:],
                                                                func=AF.Copy, scale=scale), [ps[half]], [ksb[k]])
            emit_rope(P, ksb[k], cs[k], krot[k], tmp[k])

        def rope_part2(i, dst_v, col):
            k = i % 2
            for hp in range(8):
                P.op("pe", lambda e, hp=hp, k=k: e.transpose(out=pt[k][:, hp, :], in_=krot[k][:, hp * 128:(hp + 1) * 128],
                                                           identity=idb[:]), [krot[k], idb], [pt[k]])
            P.op("act", lambda e, k=k: e.copy(out=kTs[k][:], in_=pt[k][:]), [pt[k]], [kTs[k]])
            P.dma("sp", dst_v[:, :, col:col + 128], kTs[k][:], kTs[k], False)

        pend = None
        for T in range(NKT):
            k = T % 2
            P.dma("sp", hn[k][:], hnT_all[T].rearrange("p (dk t) -> p dk t", t=128), hn[k], True)
            proj(hn[k], D, pk)
            rope_part1(T, pk, cs_k[T], 1.0)
            proj(hn[k], 2 * D, pv)
            for half in range(2):
                P.op("dve", lambda e, half=half, k=k: e.tensor_copy(out=vsb[k][:, half * 512:(half + 1) * 512], in_=pv[half][:]),
                     [pv[half]], [vsb[k]])
            P.dma("sp", V_d[T * 128:(T + 1) * 128, :], vsb[k][:], vsb[k], False)
            if pend is not None:
                rope_part2(*pend)
            pend = (T, KT_v, T * 128)
        for t in range(NQT):
            i = NKT + t
            k = i % 2
            P.dma("sp", hn[k][:], hnT_own[t].rearrange("p (dk t) -> p dk t", t=128), hn[k], True)
            proj(hn[k], 0, pk)
            rope_part1(i, pk, cs_q[t], 0.125)
            if pend is not None:
                rope_part2(*pend)
            pend = (i, QT_v, t * 128)
        rope_part2(*pend)
        P.barrier()


def phase_attn_core(P, nc, x_d, KT_d, V_d, QT_d, E_ap, elig_ap, ownm1_ap, cm_ap, w_o, ident, out_d):
    with ExitStack() as es0:
        idb = P.sbuf(es0, "ac_idb", [128, 128], BF16)
        P.dma("pool", idb[:], ident, idb, True)
        attn = P.sbuf(es0, "ac_attn", [128, NQT, D], BF16)
        with ExitStack() as es:
            KT = [P.sbuf(es, "ac_KT%d" % i, [128, SEQ], BF16) for i in range(2)]
            QT = [P.sbuf(es, "ac_QT%d" % i, [128, NTOK], BF16) for i in range(2)]
            Vh = [P.sbuf(es, "ac_V%d" % i, [128, NKT, 130], BF16) for i in range(2)]
            elig = P.sbuf(es, "ac_elig", [128, NBLK, NB], F32)
            ownm1 = P.sbuf(es, "ac_ownm1", [128, NBLK, NB], F32)
            cm = P.sbuf(es, "ac_cm", [128, 2, BLK], BF16)
            km = [P.sbuf(es, "ac_km%d" % i, [128, NB], F32) for i in range(2)]
            kmb = [P.sbuf(es, "ac_kmb%d" % i, [128, NB], BF16) for i in range(2)]
            gm = [P.sbuf(es, "ac_gm%d" % i, [128, NB], F32) for i in range(4)]
            top8 = [P.sbuf(es, "ac_top8%d" % i, [128, 8], F32) for i in range(4)]
            thr = [P.sbuf(es, "ac_thr%d" % i, [128, 1], F32) for i in range(4)]
            sel = [P.sbuf(es, "ac_sel%d" % i, [128, NB], F32) for i in range(4)]
            NM = [P.sbuf(es, "ac_NM%d" % i, [128, 128], BF16) for i in range(4)]
            PT = [P.sbuf(es, "ac_PT%d" % i, [128, 2, BLK], BF16) for i in range(3)]
            rs = [P.sbuf(es, "ac_rs%d" % i, [128, 1], F32) for i in range(2)]
            pS = [P.psum(es, "ac_pS%d" % i, [128, 2, BLK], F32) for i in range(2)]
            pO = [[P.psum(es, "ac_pO%d%d" % (i, j), [128, 512], F32) for j in range(2)] for i in range(2)]
            pG = P.psum(es, "ac_pG", [128, 512], F32)
            pN = P.psum(es, "ac_pN", [128, 128], BF16)
            P.dma("sp", elig[:], elig_ap.rearrange("(o a) b -> o a b", o=1).to_broadcast([128, NBLK, NB]), elig, True)
            P.dma("sp", ownm1[:], ownm1_ap.rearrange("(o a) b -> o a b", o=1).to_broadcast([128, NBLK, NB]), ownm1, True)
            P.dma("pool", cm[:], cm_ap, cm, True)
            for i in range(4):
                P.op("dve", lambda e, i=i: e.memset(NM[i][:], 0.0), [], [NM[i]])
            for i in range(2):
                P.op("pool", lambda e, i=i: e.memset(KT[i][64:128, :], 0.0), [], [KT[i]])
                P.dma("pool", KT[i][64:96, :], E_ap, KT[i], True)
                P.op("pool", lambda e, i=i: e.memset(QT[i][64:128, :], 0.0), [], [QT[i]])
                P.op("dve", lambda e, i=i: e.memset(Vh[i][:, :, 0:1], 1.0), [], [Vh[i]])
                P.op("dve", lambda e, i=i: e.memset(Vh[i][:, :, 129:130], 1.0), [], [Vh[i]])
            for i in range(2):
                P.op("dve", lambda e, i=i: e.memset(km[i][:], 0.0), [], [km[i]])
            V_v = V_d.rearrange("(T p) f -> p T f", p=128)
            st = dict(npS=0, nPT=0, ngate=0)

            def load_head(h):
                hp = h // 2
                if h % 2 == 0:
                    vb = Vh[hp % 2]
                    for q4 in range(4):
                        P.dma("sp", vb[:, q4 * 16:(q4 + 1) * 16, 1:129], V_v[:, q4 * 16:(q4 + 1) * 16, hp * 128:(hp + 1) * 128],
                              vb, True)
                kt_, qt_ = KT[h % 2], QT[h % 2]
                P.dma("sp", kt_[0:64, :], KT_d[h], kt_, True)
                P.dma("sp", qt_[0:64, :], QT_d[h], qt_, True)
                P.op("dve", lambda e: e.tensor_reduce(out=km[h % 2][0:64, :], in_=fap(kt_[0:64, 0:1], [(BLK, NB), (1, BLK)]),
                                                      axis=AX.X, op=ALU.add), [kt_], [km[h % 2]])
                P.op("dve", lambda e: e.tensor_copy(out=kmb[h % 2][:], in_=km[h % 2][:]), [km[h % 2]], [kmb[h % 2]])

            def gating_a(h, qt):
                qt_ = QT[h % 2]
                jl = qt // 2
                g = qt % 4
                P.op("pe", lambda e: e.matmul(pG[:, 0:NB], lhsT=qt_[:, qt * 128:(qt + 1) * 128], rhs=kmb[h % 2][:, :],
                                              start=True, stop=True), [qt_, kmb[h % 2]], [pG])
                P.op("dve", lambda e: e.tensor_tensor(out=gm[g][:], in0=pG[:, 0:NB], in1=elig[:, jl, :], op=ALU.add),
                     [pG, elig], [gm[g]])
                P.op("dve", lambda e: e.max(out=top8[g][:], in_=gm[g][:]), [gm[g]], [top8[g]])
                P.op("dve", lambda e: e.tensor_scalar(out=thr[g][:], in0=top8[g][:, 2:3], scalar1=-1.0e29, scalar2=None,
                                                      op0=ALU.max), [top8[g]], [thr[g]])
                P.op("dve", lambda e: e.tensor_scalar(out=sel[g][:], in0=gm[g][:], scalar1=thr[g][:, 0:1], scalar2=None,
                                                      op0=ALU.is_ge), [gm[g], thr[g]], [sel[g]])
                P.op("dve", lambda e: e.scalar_tensor_tensor(out=NM[g][:, 64:96], in0=sel[g][:], scalar=BIG,
                                                             in1=ownm1[:, jl, :], op0=ALU.mult, op1=ALU.add),
                     [sel[g], ownm1], [NM[g]])

            def gating_b(h, qt):
                qt_ = QT[h % 2]
                g = qt % 4
                P.op("pe", lambda e: e.transpose(out=pN[:], in_=NM[g][:], identity=idb[:]), [NM[g], idb], [pN])
                P.op("act", lambda e: e.copy(out=qt_[64:96, qt * 128:(qt + 1) * 128], in_=pN[64:96, :]), [pN], [qt_])

            def attention(h, jl):
                hb = h % 2
                hp = h // 2
                kt_, qt_ = KT[h % 2], QT[h % 2]
                vb = Vh[hp % 2]
                po = pO[jl % 2]
                blocks = [rr * 16 + m for m in range(jl + 1) for rr in range(2)]
                nb_ = len(blocks)
                def qk(bi):
                    b = blocks[bi]
                    ps = pS[(st["npS"] + bi) % 2]
                    for kt2 in range(2):
                        T = b * 2 + kt2
                        P.op("pe", lambda e, kt2=kt2, T=T: e.matmul(
                            ps[:, kt2, :], lhsT=kt_[:, T * 128:(T + 1) * 128], rhs=qt_[:, jl * BLK:(jl + 1) * BLK],
                            start=True, stop=True), [kt_, qt_], [ps])

                qk(0)
                for bi, b in enumerate(blocks):
                    if bi + 1 < nb_:
                        qk(bi + 1)
                    ps = pS[(st["npS"] + bi) % 2]
                    pt_ = PT[st["nPT"] % 3]
                    st["nPT"] += 1
                    P.op("act", lambda e: e.activation(out=pt_[:], in_=ps[:], func=AF.Exp), [ps], [pt_])
                    if b == jl:
                        P.op("pool", lambda e: e.tensor_tensor(out=pt_[:], in0=pt_[:], in1=cm[:, :, :], op=ALU.mult),
                             [pt_, cm], [pt_])
                    for kt2 in range(2):
                        T = b * 2 + kt2
                        for qs in range(2):
                            P.op("pe", lambda e, kt2=kt2, qs=qs, T=T: e.matmul(
                                po[qs][:, 0:65], lhsT=pt_[:, kt2, qs * 128:(qs + 1) * 128],
                                rhs=vb[:, T, hb * 65:hb * 65 + 65],
                                start=(bi == 0 and kt2 == 0), stop=(bi == nb_ - 1 and kt2 == 1)), [pt_, vb], [po[qs]])
                st["npS"] += nb_
                for qs in range(2):
                    sc = 0 if hb == 0 else 64
                    oc = 1 if hb == 0 else 0
                    P.op("dve", lambda e, qs=qs: e.reciprocal(out=rs[qs][:], in_=po[qs][:, sc:sc + 1]), [po[qs]], [rs[qs]])
                    P.op("dve", lambda e, qs=qs: e.tensor_scalar(
                        out=attn[:, jl * 2 + qs, h * 64:(h + 1) * 64], in0=po[qs][:, oc:oc + 64], scalar1=rs[qs][:, 0:1],
                        scalar2=None, op0=ALU.mult), [po[qs], rs[qs]], [attn])

            load_head(0)
            for qt in range(NQT):
                gating_a(0, qt)
                gating_b(0, qt)
            for h in range(NHEAD):
                if h + 1 < NHEAD:
                    load_head(h + 1)
                for jl in range(NBLK):
                    attention(h, jl)
                    if h + 1 < NHEAD:
                        gating_a(h + 1, 2 * jl)
                        gating_a(h + 1, 2 * jl + 1)
                        if jl > 0:
                            gating_b(h + 1, 2 * jl - 2)
                            gating_b(h + 1, 2 * jl - 1)
                if h + 1 < NHEAD:
                    gating_b(h + 1, NQT - 2)
                    gating_b(h + 1, NQT - 1)
        P.barrier()
        with ExitStack() as es:
            wo = P.sbuf(es, "ao_wo", [128, 8, D], BF16)
            w_o_v = w_o.rearrange("(dk p) f -> p dk f", p=128)
            for dk in range(0, 8, 4):
                P.dma("pool", wo[:, dk:dk + 4, :], w_o_v[:, dk:dk + 4, :], wo, True)
            aT = [P.sbuf(es, "ao_aT%d" % i, [128, 8, 128], BF16) for i in range(2)]
            xt = [P.sbuf(es, "ao_xt%d" % i, [128, D], F32) for i in range(2)]
            pt = [P.psum(es, "ao_pt%d" % i, [128, 8, 128], BF16) for i in range(2)]
            py = [P.psum(es, "ao_py%d" % i, [128, 512], F32) for i in range(2)]
            npy = 0
            for qt in range(NQT):
                k = qt % 2
                P.dma("sp", xt[k][:], x_d[qt * 128:(qt + 1) * 128, :], xt[k], True)
                for fk in range(8):
                    P.op("pe", lambda e, fk=fk, k=k, qt=qt: e.transpose(out=pt[k][:, fk, :], in_=attn[:, qt, fk * 128:(fk + 1) * 128],
                                                                     identity=idb[:]), [attn, idb], [pt[k]])
                P.op("act", lambda e, k=k: e.copy(out=aT[k][:], in_=pt[k][:]), [pt[k]], [aT[k]])
                for fh in range(2):
                    pyk = py[npy % 2]
                    npy += 1
                    for fk in range(8):
                        P.op("pe", lambda e, fk=fk, k=k, fh=fh, pyk=pyk: e.matmul(pyk[:], lhsT=aT[k][:, fk, :], rhs=wo[:, fk, fh * 512:(fh + 1) * 512],
                                                                                start=(fk == 0), stop=(fk == 7)), [aT[k], wo], [pyk])
                    P.op("dve", lambda e, k=k, fh=fh, pyk=pyk: e.tensor_tensor(out=xt[k][:, fh * 512:(fh + 1) * 512], in0=xt[k][:, fh * 512:(fh + 1) * 512],
                                                                              in1=pyk[:], op=ALU.add), [xt[k], pyk], [xt[k]])
                P.dma("sp", out_d[qt * 128:(qt + 1) * 128, :], xt[k][:], xt[k], False)
        P.barrier()


def attn_tables(core):
    r = core % 2
    half = DH // 2
    inv = (np.float32(10000.0) ** (-np.arange(half, dtype=np.float32) / np.float32(half))).astype(np.float32)

    def cs_for(rr, lt):
        j = 2 * (lt // 2) + rr
        pos = (j * BLK + (lt % 2) * 128 + np.arange(128)).astype(np.float32)
        ang = (pos[:, None] * inv[None, :]).astype(np.float32)
        return np.concatenate([np.cos(ang), np.sin(ang)], axis=1).astype(np.float32)

    rank_of = lambda T: r if T < NQT else 1 - r
    cs_k = np.stack([cs_for(rank_of(T), T % NQT) for T in range(NKT)])
    cs_q = np.stack([cs_for(r, t) for t in range(NQT)])
    nglob = np.array([2 * (b % 16) + (r if b < 16 else 1 - r) for b in range(NB)])
    E = np.zeros((NB, SEQ), np.float32)
    for b in range(NB):
        E[b, b * BLK:(b + 1) * BLK] = 1.0
    elig = np.zeros((NBLK, NB), np.float32)
    ownm1 = np.zeros((NBLK, NB), np.float32)
    for jl in range(NBLK):
        j = 2 * jl + r
        elig[jl] = np.where(nglob < j, 0.0, -1.0e30)
        ownm1[jl] = np.where(nglob == j, 0.0, -BIG)
    tri = np.zeros((128, 2, BLK), np.float32)
    for kt in range(2):
        kp = kt * 128 + np.arange(128)
        tri[:, kt, :] = (kp[:, None] <= np.arange(BLK)[None, :]).astype(np.float32)
    return {"cs_k": cs_k, "cs_q": cs_q, "E": E, "elig": elig, "ownm1": ownm1, "cm": np.ascontiguousarray(tri)}


def build_norm():
    nc = bass.Bass("TRN2", target_bir_lowering=False)
    x = nc.dram_tensor("x", [NTOK, D], F32, kind="ExternalInput").ap()
    g = nc.dram_tensor("g", [D], F32, kind="ExternalInput").ap()
    ident = nc.dram_tensor("ident", [128, 128], F32, kind="ExternalInput").ap()
    hnT_own = nc.dram_tensor("hnT_own", [NQT, 128, D], BF16, kind="ExternalOutput").ap()
    with ExitStack() as es:
        P = Prog(nc, es)
        phase_norm_T(P, nc, x, g, ident, hnT_own)
    return nc


def attn_decls(nc):
    d = {}
    d["cs_k"] = nc.dram_tensor("cs_k", [NKT, 128, 64], F32, kind="ExternalInput").ap()
    d["cs_q"] = nc.dram_tensor("cs_q", [NQT, 128, 64], F32, kind="ExternalInput").ap()
    d["E"] = nc.dram_tensor("E", [NB, SEQ], F32, kind="ExternalInput").ap()
    d["elig"] = nc.dram_tensor("elig", [NBLK, NB], F32, kind="ExternalInput").ap()
    d["ownm1"] = nc.dram_tensor("ownm1", [NBLK, NB], F32, kind="ExternalInput").ap()
    d["cm"] = nc.dram_tensor("cm", [128, 2, BLK], F32, kind="ExternalInput").ap()
    return d


def build_attn(debug=False):
    nc = bass.Bass("TRN2", target_bir_lowering=False)
    x = nc.dram_tensor("x", [NTOK, D], F32, kind="ExternalInput").ap()
    hnT_all = nc.dram_tensor("hnT_all", [NKT, 128, D], BF16, kind="ExternalInput").ap()
    hnT_own = nc.dram_tensor("hnT_own", [NQT, 128, D], BF16, kind="ExternalInput").ap()
    w_qkv = nc.dram_tensor("w_qkv", [D, 3 * D], F32, kind="ExternalInput").ap()
    w_o = nc.dram_tensor("w_o", [D, D], F32, kind="ExternalInput").ap()
    ident = nc.dram_tensor("ident", [128, 128], F32, kind="ExternalInput").ap()
    t = attn_decls(nc)
    out = nc.dram_tensor("out", [NTOK, D], F32, kind="ExternalOutput").ap()
    kd = "ExternalOutput" if debug else "Internal"
    KT_d = nc.dram_tensor("KT_d", [NHEAD, DH, SEQ], BF16, kind=kd).ap()
    V_d = nc.dram_tensor("V_d", [SEQ, D], BF16, kind=kd).ap()
    QT_d = nc.dram_tensor("QT_d", [NHEAD, DH, NTOK], BF16, kind=kd).ap()
    with ExitStack() as es:
        P = Prog(nc, es)
        phase_attn_qkv(P, nc, hnT_all, hnT_own, w_qkv, t["cs_k"], t["cs_q"], ident, KT_d, V_d, QT_d)
        phase_attn_core(P, nc, x, KT_d, V_d, QT_d, t["E"], t["elig"], t["ownm1"], t["cm"], w_o, ident, out)
    return nc


def build_conv():
    nc = bass.Bass("TRN2", target_bir_lowering=False)
    x0 = nc.dram_tensor("x0", [NTOK, D], F32, kind="ExternalInput").ap()
    xh = nc.dram_tensor("xh", [NBLK * 2, D], F32, kind="ExternalInput").ap()
    g = nc.dram_tensor("g", [D], F32, kind="ExternalInput").ap()
    w_in = nc.dram_tensor("w_in", [D, 3 * D], F32, kind="ExternalInput").ap()
    conv_w = nc.dram_tensor("conv_w", [3, D], F32, kind="ExternalInput").ap()
    w_out = nc.dram_tensor("w_out", [D, D], F32, kind="ExternalInput").ap()
    ident = nc.dram_tensor("ident", [128, 128], F32, kind="ExternalInput").ap()
    x1 = nc.dram_tensor("x1", [NTOK, D], F32, kind="ExternalOutput").ap()
    with ExitStack() as es:
        P = Prog(nc, es)
        phase_conv(P, nc, x0, xh, g, w_in, conv_w, w_out, ident, x1)
    return nc


_NC_CACHE = {}
NALL = 2 * NTOK


def build_fused():
    nc = bass.Bass("TRN2", target_bir_lowering=False)
    I = lambda name, shape, dt=F32: nc.dram_tensor(name, list(shape), dt, kind="ExternalInput").ap()
    S = lambda name, shape, dt=F32: nc.dram_tensor(name, list(shape), dt, kind="Internal").ap()
    x_all = I("x_all", [NALL, D])
    xh_all = I("xh_all", [2 * NBLK * 2, D])
    norm_mix = I("norm_mix", [2, D])
    norm_ffn = I("norm_ffn", [2, D])
    norm_final = I("norm_final", [D])
    conv_w_in = I("conv_w_in", [D, 3 * D])
    conv_w = I("conv_w", [3, D])
    conv_w_out = I("conv_w_out", [D, D])
    w_qkv = I("w_qkv", [D, 3 * D])
    w_o = I("w_o", [D, D])
    wq = [I("wq%d" % l, [D, D]) for l in range(2)]
    k1 = [I("k1_%d" % l, [NKEY, 64]) for l in range(2)]
    k2 = [I("k2_%d" % l, [NKEY, 64]) for l in range(2)]
    u = [I("u%d" % l, [NKEY * NKEY, D]) for l in range(2)]
    v = [I("v%d" % l, [NKEY * NKEY, D]) for l in range(2)]
    ident = I("ident", [128, 128])
    iota = I("iota", [128, 128])
    t = attn_decls(nc)
    out = nc.dram_tensor("out", [NTOK, D], F32, kind="ExternalOutput").ap()
    x1 = S("x1", [NALL, D])
    x2 = S("x2", [NALL, D])
    x3 = S("x3", [NTOK, D])
    hnT_all = S("hnT_all", [NKT, 128, D], BF16)
    KT_d = S("KT_d", [NHEAD, DH, SEQ], BF16)
    V_d = S("V_d", [SEQ, D], BF16)
    QT_d = S("QT_d", [NHEAD, DH, NTOK], BF16)
    UT = [S("UT%d" % l, [128, 128, D], BF16) for l in range(2)]
    VV = [S("VV%d" % l, [NKEY * NKEY, D], BF16) for l in range(2)]
    hnT_d = S("hnT_d", [NALL // GRP, 128, 8 * GRP], BF16)
    tab_d = S("tab_d", [3, 128, NALL])
    with ExitStack() as es:
        P = Prog(nc, es)
        phase_conv(P, nc, x_all, xh_all, norm_mix[0], conv_w_in, conv_w, conv_w_out, ident, x1, nblk=2 * NBLK,
                   side_jobs=[(u[l], v[l], UT[l], VV[l]) for l in range(2)])
        phase_peer_route(P, nc, x1, norm_ffn[0], wq[0], k1[0], k2[0], ident, iota, hnT_d, tab_d, NALL // 128)
        phase_peer_experts(P, nc, x1, hnT_d, tab_d, UT[0], VV[0], iota, x2, None, NALL // GRP)
        phase_norm_T(P, nc, x2, norm_mix[1], ident, hnT_all, NKT)
        phase_attn_qkv(P, nc, hnT_all, hnT_all, w_qkv, t["cs_k"], t["cs_q"], ident, KT_d, V_d, QT_d)
        phase_attn_core(P, nc, x2, KT_d, V_d, QT_d, t["E"], t["elig"], t["ownm1"], t["cm"], w_o, ident, x3)
        phase_peer_route(P, nc, x3, norm_ffn[1], wq[1], k1[1], k2[1], ident, iota, hnT_d, tab_d, NTOK // 128)
        phase_peer_experts(P, nc, x3, hnT_d, tab_d, UT[1], VV[1], iota, out, norm_final, NTOK // GRP)
        P.wait_all_dma("sp")
    return nc


def kernel(x, norm_mix, norm_ffn, conv_w_in, conv_w, conv_w_out, attn_w_qkv, attn_w_o,
           peer_w_q, peer_k1, peer_k2, peer_u, peer_v, norm_final):
    from concourse.bass_utils import run_bass_kernel_spmd
    f = lambda a: np.ascontiguousarray(np.asarray(a, dtype=np.float32))
    x = f(x)
    B, S, _ = x.shape
    cores = list(range(8))
    if "fused" not in _NC_CACHE:
        _NC_CACHE["fused"] = build_fused()
    shared = {"norm_mix": f(norm_mix), "norm_ffn": f(norm_ffn), "norm_final": f(norm_final),
              "conv_w_in": f(conv_w_in[0]), "conv_w": f(conv_w[0]), "conv_w_out": f(conv_w_out[0]),
              "w_qkv": f(attn_w_qkv[0]), "w_o": f(attn_w_o[0])}
    for l in range(2):
        shared["wq%d" % l] = f(peer_w_q[l])
        shared["k1_%d" % l] = f(peer_k1[l])
        shared["k2_%d" % l] = f(peer_k2[l])
        shared["u%d" % l] = f(peer_u[l])
        shared["v%d" % l] = f(peer_v[l])
    shared.update(consts())
    in_maps = []
    for c in cores:
        xs, xh = shard_tokens(x, c)
        xo, xoh = shard_tokens(x, c ^ 1)
        m = dict(shared)
        m["x_all"] = np.concatenate([xs, xo], axis=0)
        m["xh_all"] = np.concatenate([xh, xoh], axis=0)
        m.update(attn_tables(c))
        in_maps.append(m)
    res = run_bass_kernel_spmd(_NC_CACHE["fused"], in_maps, core_ids=cores)
    outs = [r["out"] for r in res.results]
    return unshard_tokens(outs, B, S)
```

```python
import numpy as np
import concourse.bass as bass
import concourse.mybir as mybir
from contextlib import ExitStack

F32 = mybir.dt.float32
BF16 = mybir.dt.bfloat16
I32 = mybir.dt.int32
U32 = mybir.dt.uint32
ALU = mybir.AluOpType
AF = mybir.ActivationFunctionType
AX = mybir.AxisListType

ENGS = ("pe", "act", "dve", "pool", "sp")


class Buf:
    def __init__(self, prog, t, name):
        self.p = prog
        self.t = t
        self.name = name
        self.last_w = None
        self.readers = {}
        self.dsem = None
        self.dcount = 0

    def __getitem__(self, idx):
        return self.t[idx]


class Prog:
    def __init__(self, nc, es, direct=True):
        self.nc = nc
        self.es = es
        self.direct = direct
        self.eng = {"pe": nc.tensor, "act": nc.scalar, "dve": nc.vector, "pool": nc.gpsimd, "sp": nc.sync}
        self.streams = {e: [] for e in ENGS}
        self.count = {e: 0 for e in ENGS}
        self.known = {e: {} for e in ENGS}
        self.sems = {}
        for e in ENGS:
            self.sems["E_" + e] = es.enter_context(nc.semaphore("E_" + e))
        self.free_dsems = []
        self.ndsem = 0
        self.dsem_count = {}
        self.all_bufs = []

    def sbuf(self, es, name, shape, dt):
        self.uid = getattr(self, "uid", 0) + 1
        t = es.enter_context(self.nc.sbuf_tensor("%s_%d" % (name, self.uid), list(shape), dt))
        b = Buf(self, t, name)
        return b

    def psum(self, es, name, shape, dt):
        self.uid = getattr(self, "uid", 0) + 1
        t = es.enter_context(self.nc.psum_tensor("%s_%d" % (name, self.uid), list(shape), dt))
        b = Buf(self, t, name)
        return b

    def _get_dsem(self, b):
        if b.dsem is None:
            if self.free_dsems:
                k = self.free_dsems.pop()
            else:
                k = "D_%d" % self.ndsem
                self.ndsem += 1
                self.sems[k] = self.es.enter_context(self.nc.semaphore(k))
                self.dsem_count[k] = 0
            b.dsem = k
            b.dcount = self.dsem_count[k]
            self.all_bufs.append(b)
        return b.dsem

    def release(self, bufs):
        for b in bufs:
            if b.dsem is not None:
                self.dsem_count[b.dsem] = b.dcount
                self.free_dsems.append(b.dsem)
                b.dsem = None

    def _collect(self, eng, reads, writes):
        deps = {}

        def add(ev):
            if ev is None:
                return
            k, v = ev
            if deps.get(k, 0) < v:
                deps[k] = v

        for b in reads:
            add(b.last_w)
        for b in writes:
            add(b.last_w)
            for k, v in b.readers.items():
                add((k, v))
        waits = []
        for k, v in deps.items():
            if eng == "pe" and k == "E_pe":
                continue
            if self.known[eng].get(k, 0) < v:
                self.known[eng][k] = v
                waits.append((k, v))
        return waits

    def op(self, eng, fn, reads=(), writes=()):
        waits = self._collect(eng, reads, writes)
        self.count[eng] += 1
        ev = ("E_" + eng, self.count[eng])
        self._push(eng, (waits, fn, ("E_" + eng, 1)))
        for b in reads:
            b.readers[ev[0]] = ev[1]
        for b in writes:
            b.last_w = ev
            b.readers = {}
        return ev

    def dma(self, q, out, in_, buf, load, **kw):
        if load:
            waits = self._collect(q, (), (buf,))
        else:
            waits = self._collect(q, (buf,), ())
        k = self._get_dsem(buf)
        buf.dcount += 16
        ev = (k, buf.dcount)
        self._push(q, (waits, lambda e: e.dma_start(out=out, in_=in_, **kw), (k, 16)))
        if load:
            buf.last_w = ev
            buf.readers = {}
        else:
            buf.readers[k] = buf.dcount
        return ev

    def dma2(self, q, out, in_, dst_buf, src_buf, **kw):
        raise NotImplementedError

    def barrier(self):
        waits = []
        for b in self.all_bufs:
            if b.dsem is not None and self.known["sp"].get(b.dsem, 0) < b.dcount:
                self.known["sp"][b.dsem] = b.dcount
                waits.append((b.dsem, b.dcount))
        for e in ENGS:
            if e == "sp":
                continue
            k = "E_" + e
            if self.known["sp"].get(k, 0) < self.count[e]:
                self.known["sp"][k] = self.count[e]
                waits.append((k, self.count[e]))
        self.count["sp"] += 1
        v = self.count["sp"]
        self._push("sp", (waits, lambda e, s=self.sems["E_sp"]: e.sem_inc(s, 1), None))
        for e in ENGS:
            if e == "sp":
                continue
            self.known[e]["E_sp"] = v
            self._push(e, ([("E_sp", v)], None, None))
            for k2, v2 in self.known["sp"].items():
                if self.known[e].get(k2, 0) < v2:
                    self.known[e][k2] = v2
        self.release(self.all_bufs)
        self.all_bufs = [b for b in self.all_bufs if False]

    def wait_all_dma(self, eng="sp"):
        waits = []
        for b in self.all_bufs:
            if b.dsem is not None and self.known[eng].get(b.dsem, 0) < b.dcount:
                self.known[eng][b.dsem] = b.dcount
                waits.append((b.dsem, b.dcount))
        self._push(eng, (waits, None, None))

    def _push(self, e, item):
        if not self.direct:
            self.streams[e].append(item)
            return
        waits, fn, inc = item
        eng = self.eng[e]
        for k, v in waits:
            eng.wait_ge(self.sems[k], v)
        if fn is not None:
            ins = fn(eng)
            if inc is not None:
                ins.then_inc(self.sems[inc[0]], inc[1])

    def emit(self):
        if self.direct:
            return
        nc = self.nc
        engmap = {"pe": "tensor", "act": "scalar", "dve": "vector", "pool": "gpsimd", "sp": "sync"}
        with nc.Block() as block:
            for e in ENGS:
                stream = self.streams[e]

                def body(eng, stream=stream):
                    for waits, fn, inc in stream:
                        for k, v in waits:
                            eng.wait_ge(self.sems[k], v)
                        if fn is not None:
                            ins = fn(eng)
                            if inc is not None:
                                ins.then_inc(self.sems[inc[0]], inc[1])

                getattr(block, engmap[e])(body)


D = 1024
NTOK = 4096
BLK = 256
NBLK = NTOK // BLK
EPS = 1e-6


def bcast_rows(ap1d, nparts):
    n = ap1d.shape[0]
    return ap1d.rearrange("(o n) -> o n", o=1).to_broadcast([nparts, n])


class NormScratch:
    def __init__(self, P, es, tag):
        self.junk = P.sbuf(es, "nj" + tag, [128, D], BF16)
        self.ss = P.sbuf(es, "nss" + tag, [128, 1], F32)
        self.ms = P.sbuf(es, "nms" + tag, [128, 1], F32)
        self.rstd = P.sbuf(es, "nrs" + tag, [128, 1], F32)
        self.hn = P.sbuf(es, "nhn" + tag, [128, D], BF16)
        self.pt = P.psum(es, "npt" + tag, [128, 8, 128], BF16)


def emit_rstd(P, xt, rows, S):
    P.op("act", lambda e: e.activation(out=S.junk[0:rows, :], in_=xt[0:rows, :], func=AF.Square,
                                       accum_out=S.ss[0:rows, :]), [xt], [S.junk, S.ss])
    P.op("dve", lambda e: e.tensor_scalar(out=S.ms[0:rows, :], in0=S.ss[0:rows, :], scalar1=1.0 / D, scalar2=EPS,
                                          op0=ALU.mult, op1=ALU.add), [S.ss], [S.ms])
    P.op("act", lambda e: e.activation(out=S.ms[0:rows, :], in_=S.ms[0:rows, :], func=AF.Sqrt), [S.ms], [S.ms])
    P.op("dve", lambda e: e.reciprocal(out=S.rstd[0:rows, :], in_=S.ms[0:rows, :]), [S.ms], [S.rstd])


def emit_norm_T(P, xt, rows, gt, idb, hnT, col0, S, evac="act"):
    emit_rstd(P, xt, rows, S)
    P.op("dve", lambda e: e.scalar_tensor_tensor(out=S.hn[0:rows, :], in0=xt[0:rows, :], scalar=S.rstd[0:rows, 0:1],
                                                 in1=gt[0:rows, :], op0=ALU.mult, op1=ALU.mult),
         [xt, S.rstd, gt], [S.hn])
    for dk in range(8):
        P.op("pe", lambda e, dk=dk: e.transpose(out=S.pt[:, dk, 0:rows], in_=S.hn[0:rows, dk * 128:(dk + 1) * 128],
                                                identity=idb[0:rows, 0:rows]), [S.hn, idb], [S.pt])
    if evac == "act":
        P.op("act", lambda e: e.copy(out=hnT[:, :, col0:col0 + rows], in_=S.pt[:, :, 0:rows]), [S.pt], [hnT])
    else:
        P.op("dve", lambda e: e.tensor_copy(out=hnT[:, :, col0:col0 + rows], in_=S.pt[:, :, 0:rows]), [S.pt], [hnT])


def phase_conv(P, nc, x0, xh, g_ap, w_in, conv_w, w_out, ident, x1, nblk=NBLK, side_jobs=None):
    with ExitStack() as es:
        side = prep_gen(P, es, side_jobs, ident) if side_jobs else None
        side_steps = (128 * len(side_jobs) + nblk - 1) // nblk if side_jobs else 0
        idb = P.sbuf(es, "c_idb", [128, 128], BF16)
        gt = P.sbuf(es, "c_gt", [128, D], F32)
        win = P.sbuf(es, "c_win", [128, 8, 3 * D], BF16)
        wout = P.sbuf(es, "c_wout", [128, 8, D], BF16)
        cw = P.sbuf(es, "c_cw", [128, 8, 3], F32)
        P.dma("pool", idb[:], ident, idb, True)
        P.dma("sp", gt[:], bcast_rows(g_ap, 128), gt, True)
        w_in_v = w_in.rearrange("(dk p) f -> p dk f", p=128)
        for dk in range(8):
            P.dma("pool", win[:, dk, :], w_in_v[:, dk, :], win, True)
        w_out_v = w_out.rearrange("(dk p) f -> p dk f", p=128)
        for dk in range(0, 8, 4):
            P.dma("pool", wout[:, dk:dk + 4, :], w_out_v[:, dk:dk + 4, :], wout, True)
        for kk in range(3):
            P.dma("sp", cw[:, :, kk], conv_w[kk, :].rearrange("(fc p) -> p fc", p=128), cw, True,
                  allow_slow_non_contiguous=True)
        sets = []
        for s in range(2):
            t = str(s)
            st = dict(
                xt=[P.sbuf(es, "c_xt%d_%s" % (i, t), [128, D], F32) for i in range(2)],
                xh=P.sbuf(es, "c_xh" + t, [2, D], F32),
                hnT=P.sbuf(es, "c_hnT" + t, [128, 8, BLK + 2], BF16),
                gT=P.sbuf(es, "c_gT" + t, [128, 8, BLK], BF16),
                S=NormScratch(P, es, "c" + t),
            )
            sets.append(st)
        csb = [P.sbuf(es, "c_csb%d" % i, [128, BLK + 2], F32) for i in range(2)]
        u = [P.sbuf(es, "c_u%d" % i, [128, BLK + 2], F32) for i in range(2)]
        acc = [P.sbuf(es, "c_acc%d" % i, [128, BLK], F32) for i in range(2)]
        pb = [P.psum(es, "c_pb%d" % i, [128, 512], F32) for i in range(3)]
        py = [P.psum(es, "c_py%d" % i, [128, 512], F32) for i in range(2)]
        N = BLK + 2
        nbuf = 0
        npy = 0
        for blk in range(nblk):
            st = sets[blk % 2]
            S = st["S"]
            hnT = st["hnT"]
            gT = st["gT"]
            P.dma("sp", st["xh"][:], xh[blk * 2:blk * 2 + 2, :], st["xh"], True)
            for i in range(2):
                P.dma("sp", st["xt"][i][:], x0[blk * BLK + i * 128: blk * BLK + (i + 1) * 128, :], st["xt"][i], True)
            emit_norm_T(P, st["xh"], 2, gt, idb, hnT, 0, S)
            for i in range(2):
                emit_norm_T(P, st["xt"][i], 128, gt, idb, hnT, 2 + i * 128, S)
            for fc in range(8):
                for j in range(3):
                    col = j * D + fc * 128
                    for dk in range(8):
                        P.op("pe", lambda e, j=j, dk=dk, col=col: e.matmul(
                            pb[j][:, 0:N], lhsT=win[:, dk, col:col + 128], rhs=hnT[:, dk, 0:N],
                            start=(dk == 0), stop=(dk == 7)), [win, hnT], [pb[j]])
                k = nbuf % 2
                nbuf += 1
                P.op("act", lambda e, k=k: e.copy(out=csb[k][:, 0:N], in_=pb[1][:, 0:N]), [pb[1]], [csb[k]])
                P.op("dve", lambda e, k=k: e.tensor_tensor(out=u[k][:, 0:N], in0=csb[k][:, 0:N], in1=pb[2][:, 0:N],
                                                           op=ALU.mult), [csb[k], pb[2]], [u[k]])
                P.op("dve", lambda e, k=k, fc=fc: e.tensor_scalar(out=acc[k][:], in0=u[k][:, 0:BLK],
                                                                   scalar1=cw[:, fc, 0:1], scalar2=None, op0=ALU.mult),
                     [u[k], cw], [acc[k]])
                for kk in (1, 2):
                    P.op("dve", lambda e, k=k, fc=fc, kk=kk: e.scalar_tensor_tensor(
                        out=acc[k][:], in0=u[k][:, kk:kk + BLK], scalar=cw[:, fc, kk:kk + 1], in1=acc[k][:],
                        op0=ALU.mult, op1=ALU.add), [u[k], cw, acc[k]], [acc[k]])
                P.op("dve", lambda e, k=k, fc=fc: e.tensor_tensor(out=gT[:, fc, :], in0=acc[k][:], in1=pb[0][:, 2:N],
                                                                   op=ALU.mult), [acc[k], pb[0]], [gT])
            for _ in range(side_steps):
                next(side, None)
            for i in range(2):
                xt = st["xt"][i]
                for fh in range(2):
                    pyk = py[npy % 2]
                    npy += 1
                    for fc in range(8):
                        P.op("pe", lambda e, fc=fc, i=i, fh=fh, pyk=pyk: e.matmul(
                            pyk[:, :], lhsT=gT[:, fc, i * 128:(i + 1) * 128], rhs=wout[:, fc, fh * 512:(fh + 1) * 512],
                            start=(fc == 0), stop=(fc == 7)), [gT, wout], [pyk])
                    P.op("dve", lambda e, xt=xt, fh=fh, pyk=pyk: e.tensor_tensor(
                        out=xt[:, fh * 512:(fh + 1) * 512], in0=xt[:, fh * 512:(fh + 1) * 512], in1=pyk[:, :],
                        op=ALU.add), [xt, pyk], [xt])
                P.dma("sp", x1[blk * BLK + i * 128: blk * BLK + (i + 1) * 128, :], xt[:], xt, False)
        P.barrier()


def block_of(core, jl):
    return core // 2, 2 * jl + (core % 2)


def shard_tokens(x, core):
    b = core // 2
    xs = np.empty((NTOK, x.shape[2]), x.dtype)
    xh = np.zeros((NBLK * 2, x.shape[2]), x.dtype)
    for jl in range(NBLK):
        _, j = block_of(core, jl)
        xs[jl * BLK:(jl + 1) * BLK] = x[b, j * BLK:(j + 1) * BLK]
        if j > 0:
            xh[jl * 2:jl * 2 + 2] = x[b, j * BLK - 2:j * BLK]
    return xs, xh


def unshard_tokens(outs, B, S):
    full = np.empty((B, S, outs[0].shape[1]), outs[0].dtype)
    for core, o in enumerate(outs):
        for jl in range(NBLK):
            b, j = block_of(core, jl)
            full[b, j * BLK:(j + 1) * BLK] = o[jl * BLK:(jl + 1) * BLK]
    return full


NH = 8
NKEY = 128
TOPK = 16
GRP = 256
NGRP = NTOK // GRP
NEG = -1.0e30


def fap(a, dims):
    return bass.AP(a.tensor, a.offset, [list(a.ap[0])] + [list(d) for d in dims])


def view(P, buf, name):
    b = Buf(P, buf.t, name)
    return b


def prep_gen(P, es, jobs, ident):
    idb = P.sbuf(es, "pg_idb", [128, 128], BF16)
    P.dma("pool", idb[:], ident, idb, True)
    ub = [P.sbuf(es, "pg_ub%d" % i, [128, D], BF16) for i in range(3)]
    ut = [P.sbuf(es, "pg_ut%d" % i, [128, 8, 128], BF16) for i in range(3)]
    vb = [P.sbuf(es, "pg_vb%d" % i, [128, D], BF16) for i in range(3)]
    pt = P.psum(es, "pg_pt", [128, 8, 128], BF16)
    n = 0
    for (u_ap, v_ap, UT_d, V_d) in jobs:
        for c in range(128):
            k = n % 3
            n += 1
            P.dma("pool", ub[k][:], u_ap[c * 128:(c + 1) * 128, :], ub[k], True)
            for dk in range(8):
                P.op("pe", lambda e, dk=dk, k=k: e.transpose(out=pt[:, dk, :], in_=ub[k][:, dk * 128:(dk + 1) * 128],
                                                           identity=idb[:]), [ub[k], idb], [pt])
            P.op("act", lambda e, k=k: e.copy(out=ut[k][:], in_=pt[:]), [pt], [ut[k]])
            P.dma("sp", UT_d[c].rearrange("p (dk q) -> p dk q", q=128), ut[k][:], ut[k], False)
            P.dma("pool", vb[k][:], v_ap[c * 128:(c + 1) * 128, :], vb[k], True)
            P.dma("sp", V_d[c * 128:(c + 1) * 128, :], vb[k][:], vb[k], False)
            yield


def phase_peer_prep(P, nc, u_ap, v_ap, ident, UT_d, V_d):
    with ExitStack() as es:
        idb = P.sbuf(es, "pp_idb", [128, 128], BF16)
        P.dma("pool", idb[:], ident, idb, True)
        ub = [P.sbuf(es, "pp_ub%d" % i, [128, D], BF16) for i in range(3)]
        ut = [P.sbuf(es, "pp_ut%d" % i, [128, 8, 128], BF16) for i in range(3)]
        vb = [P.sbuf(es, "pp_vb%d" % i, [128, D], BF16) for i in range(3)]
        pt = [P.psum(es, "pp_pt%d" % i, [128, 8, 128], BF16) for i in range(2)]
        for c in range(128):
            k = c % 3
            P.dma("pool", ub[k][:], u_ap[c * 128:(c + 1) * 128, :], ub[k], True)
            for dk in range(8):
                P.op("pe", lambda e, dk=dk, k=k, c=c: e.transpose(out=pt[c % 2][:, dk, :], in_=ub[k][:, dk * 128:(dk + 1) * 128],
                                                                 identity=idb[:]), [ub[k], idb], [pt[c % 2]])
            if c % 2:
                P.op("act", lambda e, k=k, c=c: e.copy(out=ut[k][:], in_=pt[c % 2][:]), [pt[c % 2]], [ut[k]])
            else:
                P.op("dve", lambda e, k=k, c=c: e.tensor_copy(out=ut[k][:], in_=pt[c % 2][:]), [pt[c % 2]], [ut[k]])
            P.dma("sp", UT_d[c].rearrange("p (dk q) -> p dk q", q=128), ut[k][:], ut[k], False)
            P.dma("pool", vb[k][:], v_ap[c * 128:(c + 1) * 128, :], vb[k], True)
            P.dma("sp", V_d[c * 128:(c + 1) * 128, :], vb[k][:], vb[k], False)
        P.barrier()


def phase_peer_route(P, nc, x_d, g_ap, wq_ap, k1_ap, k2_ap, ident, iota_ap, hnT_d, tab_d, ntiles=NTOK // 128, stop=99):
    with ExitStack() as es:
        idb = P.sbuf(es, "pr_idb", [128, 128], BF16)
        idf = P.sbuf(es, "pr_idf", [128, 128], F32)
        iot = P.sbuf(es, "pr_iot", [128, 128], F32)
        gt = P.sbuf(es, "pr_gt", [128, D], F32)
        wq = P.sbuf(es, "pr_wq", [128, 8, D], BF16)
        kcat = P.sbuf(es, "pr_kcat", [128, 128], BF16)
        kT = P.sbuf(es, "pr_kT", [128, 128], BF16)
        P.dma("pool", idb[:], ident, idb, True)
        P.dma("sp", idf[:], ident, idf, True)
        P.dma("sp", iot[:], iota_ap, iot, True)
        P.dma("sp", gt[:], bcast_rows(g_ap, 128), gt, True)
        wq_v = wq_ap.rearrange("(dk p) f -> p dk f", p=128)
        for dk in range(0, 8, 4):
            P.dma("pool", wq[:, dk:dk + 4, :], wq_v[:, dk:dk + 4, :], wq, True)
        P.dma("pool", kcat[:, 0:64], k1_ap, kcat, True)
        P.dma("pool", kcat[:, 64:128], k2_ap, kcat, True)
        Ss = [NormScratch(P, es, "pr%d" % i) for i in range(2)]
        P.op("pe", lambda e: e.transpose(out=Ss[0].pt[:, 0, :], in_=kcat[:], identity=idb[:]), [kcat, idb], [Ss[0].pt])
        P.op("act", lambda e: e.copy(out=kT[:], in_=Ss[0].pt[:, 0, :]), [Ss[0].pt], [kT])
        kblk = P.sbuf(es, "pr_kblk", [128, 256], BF16)
        P.op("dve", lambda e: e.memset(kblk[:], 0.0), [], [kblk])
        P.op("dve", lambda e: e.tensor_copy(out=kblk[0:64, 0:128], in_=kT[0:64, :]), [kT], [kblk])
        P.op("dve", lambda e: e.tensor_copy(out=kblk[64:128, 128:256], in_=kT[64:128, :]), [kT], [kblk])
        xts = [P.sbuf(es, "pr_xt%d" % i, [128, D], F32) for i in range(2)]
        hnTs = [P.sbuf(es, "pr_hnT%d" % i, [128, 8, 128], BF16) for i in range(2)]
        qTs = [P.sbuf(es, "pr_qT%d" % i, [128, 8, 128], BF16) for i in range(2)]
        ps_q = P.psum(es, "pr_psq", [128, 4, 128], F32)
        ps_s = [P.psum(es, "pr_pss%d" % i, [128, 512], F32) for i in range(4)]
        ps_t = P.psum(es, "pr_pst", [128, 3, 128], F32)
        s_sb = P.sbuf(es, "pr_s", [128, 16 * 128], F32)
        segs = [view(P, s_sb, "seg%d" % j) for j in range(16)]
        v = P.sbuf(es, "pr_v", [128, 16, 16], F32)
        vseg = [view(P, v, "vseg%d" % j) for j in range(16)]
        ix = P.sbuf(es, "pr_ix", [128, 16, 16], U32)
        ixseg = [view(P, ix, "ixseg%d" % j) for j in range(16)]
        ixf = P.sbuf(es, "pr_ixf", [128, 16, 16], F32)
        cand = P.sbuf(es, "pr_cand", [128, 8, 256], F32)
        cseg = [view(P, cand, "cseg%d" % j) for j in range(8)]
        ts = P.sbuf(es, "pr_ts", [128, 8, 16], F32)
        tsseg = [view(P, ts, "tsseg%d" % j) for j in range(8)]
        pos = P.sbuf(es, "pr_pos", [128, 8, 16], U32)
        posseg = [view(P, pos, "posseg%d" % j) for j in range(8)]
        ku = [P.sbuf(es, "pr_ku%d" % i, [128, 8, 16], U32) for i in range(2)]
        kf = [P.sbuf(es, "pr_kf%d" % i, [128, 8, 16], F32) for i in range(2)]
        eq = [P.sbuf(es, "pr_eq%d" % i, [128, 128, 16], F32) for i in range(2)]
        res = P.sbuf(es, "pr_res", [128, 3, 128], F32)
        resv = [view(P, res, "resv%d" % j) for j in range(3)]
        dd = P.sbuf(es, "pr_dd", [128, 8, 16], F32)
        ee = P.sbuf(es, "pr_ee", [128, 8, 16], F32)
        zz = P.sbuf(es, "pr_zz", [128, 8], F32)
        rz = P.sbuf(es, "pr_rz", [128, 8], F32)
        resT = [P.sbuf(es, "pr_resT%d" % i, [128, 3, 128], F32) for i in range(2)]
        tab_v = tab_d.rearrange("a p t -> p a t")
        for tile in range(ntiles):
            k = tile % 2
            xt, S, hnT, qT = xts[k], Ss[k], hnTs[k], qTs[k]
            grp, sub = tile // 2, tile % 2
            P.dma("sp", xt[:], x_d[tile * 128:(tile + 1) * 128, :], xt, True)
            emit_norm_T(P, xt, 128, gt, idb, hnT, 0, S)
            P.dma("sp", hnT_d[grp].rearrange("p (dk t) -> p dk t", t=GRP)[:, :, sub * 128:(sub + 1) * 128], hnT[:], hnT, False)
            if stop <= 1:
                continue
            for half in range(2):
                for f4 in range(4):
                    fc = half * 4 + f4
                    for dk in range(8):
                        P.op("pe", lambda e, fc=fc, f4=f4, dk=dk: e.matmul(
                            ps_q[:, f4, :], lhsT=wq[:, dk, fc * 128:(fc + 1) * 128], rhs=hnT[:, dk, :],
                            start=(dk == 0), stop=(dk == 7)), [wq, hnT], [ps_q])
                P.op("act", lambda e, half=half: e.copy(out=qT[:, half * 4:(half + 1) * 4, :], in_=ps_q[:]), [ps_q], [qT])
            if stop <= 2:
                continue
            for b4 in range(4):
                for hh in range(2):
                    h = b4 * 2 + hh
                    P.op("pe", lambda e, b4=b4, hh=hh, h=h: e.matmul(
                        ps_s[b4][:, hh * 256:(hh + 1) * 256], lhsT=qT[:, h, :], rhs=kblk[:, :],
                        start=True, stop=True), [qT, kblk], [ps_s[b4]])
                P.op("act", lambda e, b4=b4: e.copy(out=s_sb[:, b4 * 512:(b4 + 1) * 512], in_=ps_s[b4][:]),
                     [ps_s[b4]], segs[b4 * 4:(b4 + 1) * 4])
            if stop <= 3:
                continue
            sg = lambda j: s_sb[:, j * 128:(j + 1) * 128]
            for j in range(16):
                P.op("dve", lambda e, j=j: e.max(out=v[:, j, 0:8], in_=sg(j)), [segs[j]], [vseg[j]])
            for j in range(16):
                P.op("dve", lambda e, j=j: e.max_index(out=ix[:, j, 0:8], in_max=v[:, j, 0:8], in_values=sg(j)),
                     [segs[j], vseg[j]], [ixseg[j]])
            for j in range(16):
                P.op("dve", lambda e, j=j: e.match_replace(out=sg(j), in_to_replace=v[:, j, 0:8], in_values=sg(j),
                                                           imm_value=NEG), [segs[j], vseg[j]], [segs[j]])
            for j in range(16):
                P.op("dve", lambda e, j=j: e.max(out=v[:, j, 8:16], in_=sg(j)), [segs[j]], [vseg[j]])
            for j in range(16):
                P.op("dve", lambda e, j=j: e.max_index(out=ix[:, j, 8:16], in_max=v[:, j, 8:16], in_values=sg(j)),
                     [segs[j], vseg[j]], [ixseg[j]])
            if stop <= 4:
                continue
            P.op("dve", lambda e: e.tensor_copy(out=ixf[:], in_=ix[:]), ixseg, [ixf])
            P.op("dve", lambda e: e.tensor_tensor(
                out=fap(cand[:], [(256, 8), (16, 16), (1, 16)]),
                in0=fap(v[:], [(32, 8), (1, 16), (0, 16)]),
                in1=fap(v[:, 1, :], [(32, 8), (0, 16), (1, 16)]), op=ALU.add), vseg, cseg)
            cs = lambda h: cand[:, h, :]
            for h in range(8):
                P.op("dve", lambda e, h=h: e.max(out=ts[:, h, 0:8], in_=cs(h)), [cseg[h]], [tsseg[h]])
            for h in range(8):
                P.op("dve", lambda e, h=h: e.max_index(out=pos[:, h, 0:8], in_max=ts[:, h, 0:8], in_values=cs(h)),
                     [cseg[h], tsseg[h]], [posseg[h]])
            for h in range(8):
                P.op("dve", lambda e, h=h: e.match_replace(out=cs(h), in_to_replace=ts[:, h, 0:8], in_values=cs(h),
                                                           imm_value=NEG), [cseg[h], tsseg[h]], [cseg[h]])
            for h in range(8):
                P.op("dve", lambda e, h=h: e.max(out=ts[:, h, 8:16], in_=cs(h)), [cseg[h]], [tsseg[h]])
            for h in range(8):
                P.op("dve", lambda e, h=h: e.max_index(out=pos[:, h, 8:16], in_max=ts[:, h, 8:16], in_values=cs(h)),
                     [cseg[h], tsseg[h]], [posseg[h]])
            if stop <= 5:
                continue
            P.op("dve", lambda e: e.tensor_single_scalar(out=ku[0][:], in_=pos[:], scalar=4, op=ALU.logical_shift_right),
                 posseg, [ku[0]])
            P.op("dve", lambda e: e.tensor_single_scalar(out=ku[1][:], in_=pos[:], scalar=15, op=ALU.bitwise_and),
                 posseg, [ku[1]])
            for sd in range(2):
                P.op("dve", lambda e, sd=sd: e.tensor_copy(out=kf[sd][:], in_=ku[sd][:]), [ku[sd]], [kf[sd]])
            if stop <= 6:
                continue
            for sd in range(2):
                P.op("dve", lambda e, sd=sd: e.tensor_tensor(
                    out=eq[sd][:], in0=fap(iot[:], [(0, 128), (1, 16)]), in1=fap(kf[sd][:], [(1, 128), (0, 16)]),
                    op=ALU.is_equal), [iot, kf[sd]], [eq[sd]])
                P.op("dve", lambda e, sd=sd: e.tensor_tensor(
                    out=fap(eq[sd][:], [(256, 8), (16, 16), (1, 16)]), in0=fap(eq[sd][:], [(256, 8), (16, 16), (1, 16)]),
                    in1=fap(ixf[:, sd, :], [(32, 8), (0, 16), (1, 16)]), op=ALU.mult), [eq[sd], ixf], [eq[sd]])
                P.op("dve", lambda e, sd=sd: e.tensor_reduce(out=res[:, sd, :], in_=eq[sd][:], axis=AX.X, op=ALU.add),
                     [eq[sd]], [resv[sd]])
            if stop <= 7:
                continue
            P.op("dve", lambda e: e.tensor_tensor(out=dd[:], in0=ts[:], in1=fap(ts[:], [(16, 8), (0, 16)]),
                                                  op=ALU.subtract), tsseg, [dd])
            P.op("act", lambda e: e.activation(out=ee[:], in_=dd[:], func=AF.Exp), [dd], [ee])
            P.op("dve", lambda e: e.tensor_reduce(out=zz[:], in_=ee[:], axis=AX.X, op=ALU.add), [ee], [zz])
            P.op("dve", lambda e: e.reciprocal(out=rz[:], in_=zz[:]), [zz], [rz])
            P.op("dve", lambda e: e.tensor_tensor(out=fap(res[:, 2, :], [(16, 8), (1, 16)]), in0=ee[:],
                                                  in1=fap(rz[:], [(1, 8), (0, 16)]), op=ALU.mult), [ee, rz], [resv[2]])
            if stop <= 8:
                continue
            for a in range(3):
                P.op("pe", lambda e, a=a: e.transpose(out=ps_t[:, a, :], in_=res[:, a, :], identity=idf[:]),
                     [resv[a], idf], [ps_t])
            P.op("act", lambda e, k=k: e.copy(out=resT[k][:], in_=ps_t[:]), [ps_t], [resT[k]])
            P.dma("sp", tab_v[:, :, tile * 128:(tile + 1) * 128], resT[k][:], resT[k], False)
        P.barrier()


def phase_peer_experts(P, nc, x_d, hnT_d, tab_d, UT_d, V_d, iota_ap, out_d, gfin_ap=None, ngrp=NGRP):
    SUBT = 8
    with ExitStack() as es:
        iot = P.sbuf(es, "pe_iot", [128, 128], BF16)
        P.dma("pool", iot[:], iota_ap, iot, True)
        GS = [P.sbuf(es, "pe_GS%d" % i, [128, GRP, 128], BF16) for i in range(2)]
        NR = 4
        utr = [P.sbuf(es, "pe_ut%d" % i, [128, 8, 128], BF16) for i in range(NR)]
        vr = [P.sbuf(es, "pe_v%d" % i, [128, D], BF16) for i in range(NR)]
        hnT = [P.sbuf(es, "pe_hnT%d" % i, [128, 8, GRP], BF16) for i in range(2)]
        tab = [P.sbuf(es, "pe_tab%d" % i, [128, 3, GRP], F32) for i in range(2)]
        xt = [P.sbuf(es, "pe_xt%d" % i, [128, D], F32) for i in range(2)]
        NSET = 3
        O1 = [P.sbuf(es, "pe_O1_%d" % i, [128, SUBT, 128], BF16) for i in range(NSET)]
        O2g = [P.sbuf(es, "pe_O2g_%d" % i, [128, SUBT, 128], BF16) for i in range(NSET)]
        tabb = [P.sbuf(es, "pe_tabb%d" % i, [128, 3, GRP], BF16) for i in range(2)]
        ga = [P.sbuf(es, "pe_ga%d" % i, [128, GRP], BF16) for i in range(2)]
        pT = [P.sbuf(es, "pe_pT%d" % i, [128, GRP], BF16) for i in range(3)]
        po = [[P.psum(es, "pe_po%d%d" % (i, j), [128, 512], F32) for j in range(2)] for i in range(2)]
        pa = [P.psum(es, "pe_pa%d" % i, [128, 512], F32) for i in range(2)]
        pg = [P.psum(es, "pe_pg%d" % i, [128, 4, 128], F32) for i in range(2)]
        if gfin_ap is not None:
            gfin = P.sbuf(es, "pe_gfin", [128, D], F32)
            P.dma("sp", gfin[:], bcast_rows(gfin_ap, 128), gfin, True)
            S = NormScratch.__new__(NormScratch)
            S.junk = P.sbuf(es, "pe_nj", [128, D], BF16)
            S.ss = P.sbuf(es, "pe_nss", [128, 1], F32)
            S.ms = P.sbuf(es, "pe_nms", [128, 1], F32)
            S.rstd = P.sbuf(es, "pe_nrs", [128, 1], F32)
        tab_v = tab_d.rearrange("a p t -> p a t")

        def load_group(g):
            P.dma("sp", tab[g % 2][:], tab_v[:, :, g * GRP:(g + 1) * GRP], tab[g % 2], True)
            P.op("pool", lambda e: e.tensor_copy(out=tabb[g % 2][:], in_=tab[g % 2][:]), [tab[g % 2]], [tabb[g % 2]])

        def stageA(g, sb):
            tb = tabb[g % 2]
            t0 = sb * SUBT
            k = sb % NSET
            P.op("dve", lambda e: e.tensor_tensor(out=O1[k][:], in0=fap(iot[:], [(0, SUBT), (1, 128)]),
                                                  in1=fap(tb[:, 0, t0:t0 + SUBT], [(1, SUBT), (0, 128)]), op=ALU.is_equal),
                 [iot, tb], [O1[k]])
            P.op("dve", lambda e: e.tensor_tensor(out=O2g[k][:], in0=fap(iot[:], [(0, SUBT), (1, 128)]),
                                                  in1=fap(tb[:, 1, t0:t0 + SUBT], [(1, SUBT), (0, 128)]), op=ALU.is_equal),
                 [iot, tb], [O2g[k]])
            P.op("pool", lambda e: e.tensor_tensor(out=O2g[k][:], in0=O2g[k][:],
                                                   in1=fap(tb[:, 2, t0:t0 + SUBT], [(1, SUBT), (0, 128)]), op=ALU.mult),
                 [O2g[k], tb], [O2g[k]])

        def stageB(g, sb):
            k = sb % NSET
            for q4 in range(SUBT // 4):
                pgk = pg[q4]
                for tl in range(4):
                    t = q4 * 4 + tl
                    P.op("pe", lambda e, t=t, tl=tl, pgk=pgk: e.matmul(pgk[:, tl, :], lhsT=O2g[k][:, t, :], rhs=O1[k][:, t, :],
                                                                     start=True, stop=True), [O2g[k], O1[k]], [pgk])

        def stageC(g, sb):
            gs = GS[g % 2]
            t0 = sb * SUBT
            for q4 in range(SUBT // 4):
                pgk = pg[q4]
                tt = t0 + q4 * 4
                P.op("act", lambda e, pgk=pgk, tt=tt: e.copy(out=gs[:, tt:tt + 4, :], in_=pgk[:]),
                     [pgk], [gs])

        def gbuild_sub(g, sb):
            stageA(g, sb)
            stageB(g, sb)
            stageC(g, sb)

        nsubs = GRP // SUBT
        load_group(0)
        for sb in range(nsubs):
            gbuild_sub(0, sb)
        jobs = [(g, c) for g in range(ngrp) for c in range(128)]

        def emit_pa(i):
            g, c = jobs[i]
            hg = hnT[g % 2]
            if c == 0:
                P.dma("sp", hg[:], hnT_d[g].rearrange("p (dk t) -> p dk t", t=GRP), hg, True)
                if g + 1 < ngrp:
                    load_group(g + 1)
            r = i % NR
            P.dma("sp", utr[r][:], UT_d[c].rearrange("p (dk q) -> p dk q", q=128), utr[r], True)
            P.dma("sp", vr[r][:], V_d[c * 128:(c + 1) * 128, :], vr[r], True)
            pak = pa[i % 2]
            for dk in range(8):
                P.op("pe", lambda e, dk=dk, r=r, pak=pak: e.matmul(pak[:, 0:GRP], lhsT=utr[r][:, dk, :], rhs=hg[:, dk, :],
                                                                 start=(dk == 0), stop=(dk == 7)), [utr[r], hg], [pak])

        emit_pa(0)
        for i, (g, c) in enumerate(jobs):
            if i + 1 < len(jobs):
                emit_pa(i + 1)
            gs = GS[g % 2]
            r = i % NR
            pak = pa[i % 2]
            gk = ga[i % 2]
            pk = pT[i % 3]
            P.op("act", lambda e, gk=gk, pak=pak: e.activation(out=gk[:], in_=pak[:, 0:GRP], func=AF.Gelu), [pak], [gk])
            P.op("dve", lambda e, gk=gk, pk=pk, c=c, gs=gs: e.tensor_tensor(out=pk[:], in0=gk[:], in1=fap(gs[:, 0, c:c + 1], [(128, GRP)]),
                                                                           op=ALU.mult), [gk, gs], [pk])
            for tsb in range(2):
                for dh in range(2):
                    P.op("pe", lambda e, tsb=tsb, dh=dh, pk=pk, r=r, c=c: e.matmul(
                        po[tsb][dh][:], lhsT=pk[:, tsb * 128:(tsb + 1) * 128], rhs=vr[r][:, dh * 512:(dh + 1) * 512],
                        start=(c == 0), stop=(c == 127)), [pk, vr[r]], [po[tsb][dh]])
            if g + 1 < ngrp:
                if c % 4 == 0:
                    stageA(g + 1, c // 4)
                elif c % 4 == 2:
                    stageB(g + 1, c // 4)
                elif c % 4 == 3:
                    stageC(g + 1, c // 4)
            if c != 127:
                continue
            for tsb in range(2):
                row0 = g * GRP + tsb * 128
                x_t = xt[tsb]
                P.dma("sp", x_t[:], x_d[row0:row0 + 128, :], x_t, True)
                for dh in range(2):
                    P.op("dve", lambda e, x_t=x_t, dh=dh, tsb=tsb: e.tensor_tensor(
                        out=x_t[:, dh * 512:(dh + 1) * 512], in0=x_t[:, dh * 512:(dh + 1) * 512], in1=po[tsb][dh][:],
                        op=ALU.add), [x_t, po[tsb][dh]], [x_t])
                if gfin_ap is not None:
                    emit_rstd(P, x_t, 128, S)
                    P.op("dve", lambda e, x_t=x_t: e.scalar_tensor_tensor(out=x_t[:], in0=x_t[:], scalar=S.rstd[:, 0:1],
                                                                         in1=gfin[:], op0=ALU.mult, op1=ALU.mult),
                         [x_t, S.rstd, gfin], [x_t])
                P.dma("sp", out_d[row0:row0 + 128, :], x_t[:], x_t, False)
        P.barrier()


def build_peer(final, phases=(1, 1, 1), debug=False, ntiles=NTOK // 128, stop=99):
    nc = bass.Bass("TRN2", target_bir_lowering=False)
    x = nc.dram_tensor("x", [NTOK, D], F32, kind="ExternalInput").ap()
    g = nc.dram_tensor("g", [D], F32, kind="ExternalInput").ap()
    wq = nc.dram_tensor("wq", [D, D], F32, kind="ExternalInput").ap()
    k1 = nc.dram_tensor("k1", [NKEY, 64], F32, kind="ExternalInput").ap()
    k2 = nc.dram_tensor("k2", [NKEY, 64], F32, kind="ExternalInput").ap()
    u = nc.dram_tensor("u", [NKEY * NKEY, D], F32, kind="ExternalInput").ap()
    v = nc.dram_tensor("v", [NKEY * NKEY, D], F32, kind="ExternalInput").ap()
    ident = nc.dram_tensor("ident", [128, 128], F32, kind="ExternalInput").ap()
    iota = nc.dram_tensor("iota", [128, 128], F32, kind="ExternalInput").ap()
    gfin = nc.dram_tensor("gfin", [D], F32, kind="ExternalInput").ap() if final else None
    out = nc.dram_tensor("out", [NTOK, D], F32, kind="ExternalOutput").ap()
    kd = "ExternalOutput" if debug else "Internal"
    UT_d = nc.dram_tensor("UT_d", [128, 128, D], BF16, kind=kd).ap()
    V_d = nc.dram_tensor("V_d", [NKEY * NKEY, D], BF16, kind=kd).ap()
    hnT_d = nc.dram_tensor("hnT_d", [NGRP, 128, 8 * GRP], BF16, kind=kd).ap()
    tab_d = nc.dram_tensor("tab_d", [3, 128, NTOK], F32, kind=kd).ap()
    with ExitStack() as es:
        P = Prog(nc, es)
        if phases[0]:
            phase_peer_prep(P, nc, u, v, ident, UT_d, V_d)
        if phases[1]:
            phase_peer_route(P, nc, x, g, wq, k1, k2, ident, iota, hnT_d, tab_d, ntiles, stop)
        if phases[2]:
            phase_peer_experts(P, nc, x, hnT_d, tab_d, UT_d, V_d, iota, out, gfin)
    return nc


def consts():
    return {"ident": np.eye(128, dtype=np.float32),
            "iota": np.tile(np.arange(128, dtype=np.float32)[None, :], (128, 1))}


NQT_ = NTOK // 128
NHEAD = 16
DH = 64
SEQ = 8192
NKT = SEQ // 128
NQT = NTOK // 128
NB = SEQ // BLK
BIG = 30000.0


def phase_norm_T(P, nc, x_d, g_ap, ident, hnT_own, ntiles=NQT_):
    with ExitStack() as es:
        idb = P.sbuf(es, "n_idb", [128, 128], BF16)
        gt = P.sbuf(es, "n_gt", [128, D], F32)
        P.dma("pool", idb[:], ident, idb, True)
        P.dma("sp", gt[:], bcast_rows(g_ap, 128), gt, True)
        Ss = [NormScratch(P, es, "n%d" % i) for i in range(2)]
        xts = [P.sbuf(es, "n_xt%d" % i, [128, D], F32) for i in range(2)]
        hnTs = [P.sbuf(es, "n_hnT%d" % i, [128, 8, 128], BF16) for i in range(2)]
        for tile in range(ntiles):
            k = tile % 2
            P.dma("sp", xts[k][:], x_d[tile * 128:(tile + 1) * 128, :], xts[k], True)
            emit_norm_T(P, xts[k], 128, gt, idb, hnTs[k], 0, Ss[k], evac=("act" if k else "dve"))
            P.dma("sp", hnT_own[tile].rearrange("p (dk t) -> p dk t", t=128), hnTs[k][:], hnTs[k], False)
        P.barrier()


def emit_rope(P, src, cs, dst, tmp):
    x1 = fap(src[:, 0:1], [(64, 16), (1, 32)])
    x2 = fap(src[:, 32:33], [(64, 16), (1, 32)])
    cosb = fap(cs[:, 0:1], [(0, 16), (1, 32)])
    sinb = fap(cs[:, 32:33], [(0, 16), (1, 32)])
    o1 = fap(dst[:, 0:1], [(64, 16), (1, 32)])
    o2 = fap(dst[:, 32:33], [(64, 16), (1, 32)])
    t1, t2, t3, t4 = tmp
    P.op("dve", lambda e: e.tensor_tensor(out=t1[:], in0=x1, in1=cosb, op=ALU.mult), [src, cs], [t1])
    P.op("dve", lambda e: e.tensor_tensor(out=t2[:], in0=x2, in1=sinb, op=ALU.mult), [src, cs], [t2])
    P.op("dve", lambda e: e.tensor_tensor(out=o1, in0=t1[:], in1=t2[:], op=ALU.subtract), [t1, t2], [dst])
    P.op("pool", lambda e: e.tensor_tensor(out=t3[:], in0=x2, in1=cosb, op=ALU.mult), [src, cs], [t3])
    P.op("pool", lambda e: e.tensor_tensor(out=t4[:], in0=x1, in1=sinb, op=ALU.mult), [src, cs], [t4])
    P.op("pool", lambda e: e.tensor_tensor(out=o2, in0=t3[:], in1=t4[:], op=ALU.add), [t3, t4], [dst])


def phase_attn_qkv(P, nc, hnT_all, hnT_own, w_qkv, cs_k, cs_q, ident, KT_d, V_d, QT_d):
    with ExitStack() as es:
        idb = P.sbuf(es, "aq_idb", [128, 128], BF16)
        P.dma("pool", idb[:], ident, idb, True)
        w = P.sbuf(es, "aq_w", [128, 8, 3 * D], BF16)
        w_v = w_qkv.rearrange("(dk p) f -> p dk f", p=128)
        for dk in range(8):
            P.dma("pool", w[:, dk, :], w_v[:, dk, :], w, True)
        hn = [P.sbuf(es, "aq_hn%d" % i, [128, 8, 128], BF16) for i in range(2)]
        cs = [P.sbuf(es, "aq_cs%d" % i, [128, 64], F32) for i in range(2)]
        ksb = [P.sbuf(es, "aq_ksb%d" % i, [128, D], F32) for i in range(2)]
        krot = [P.sbuf(es, "aq_krot%d" % i, [128, D], BF16) for i in range(2)]
        vsb = [P.sbuf(es, "aq_vsb%d" % i, [128, D], BF16) for i in range(2)]
        tmp = [[P.sbuf(es, "aq_t%d_%d" % (i, j), [128, 16, 32], F32) for j in range(4)] for i in range(2)]
        kTs = [P.sbuf(es, "aq_kT%d" % i, [128, 8, 128], BF16) for i in range(2)]
        pk = [P.psum(es, "aq_pk%d" % i, [128, 512], F32) for i in range(2)]
        pv = [P.psum(es, "aq_pv%d" % i, [128, 512], F32) for i in range(2)]
        pt = [P.psum(es, "aq_pt%d" % i, [128, 8, 128], BF16) for i in range(2)]
        KT_v = KT_d.rearrange("(hp two) d k -> (two d) hp k", two=2)
        QT_v = QT_d.rearrange("(hp two) d k -> (two d) hp k", two=2)

        def proj(hk, col0, ps):
            for half in range(2):
                for dk in range(8):
                    P.op("pe", lambda e, half=half, dk=dk: e.matmul(
                        ps[half][:], lhsT=hk[:, dk, :], rhs=w[:, dk, col0 + half * 512: col0 + (half + 1) * 512],
                        start=(dk == 0), stop=(dk == 7)), [hk, w], [ps[half]])

        def rope_part1(i, ps, cs_ap, scale):
            k = i % 2
            P.dma("sp", cs[k][:], cs_ap, cs[k], True)
            for half in range(2):
                P.op("act", lambda e, half=half, k=k: e.activation(out=ksb[k][:, half * 512:(half + 1) * 512], in_=ps[half][:],
                                                                func=AF.Copy, scale=scale), [ps[half]], [ksb[k]])
            emit_rope(P, ksb[k], cs[k], krot[k], tmp[k])

        def rope_part2(i, dst_v, col):
            k = i % 2
            for hp in range(8):
                P.op("pe", lambda e, hp=hp, k=k: e.transpose(out=pt[k][:, hp, :], in_=krot[k][:, hp * 128:(hp + 1) * 128],
                                                           identity=idb[:]), [krot[k], idb], [pt[k]])
            P.op("act", lambda e, k=k: e.copy(out=kTs[k][:], in_=pt[k][:]), [pt[k]], [kTs[k]])
            P.dma("sp", dst_v[:, :, col:col + 128], kTs[k][:], kTs[k], False)

        pend = None
        for T in range(NKT):
            k = T % 2
            P.dma("sp", hn[k][:], hnT_all[T].rearrange("p (dk t) -> p dk t", t=128), hn[k], True)
            proj(hn[k], D, pk)
            rope_part1(T, pk, cs_k[T], 1.0)
            proj(hn[k], 2 * D, pv)
            for half in range(2):
                P.op("dve", lambda e, half=half, k=k: e.tensor_copy(out=vsb[k][:, half * 512:(half + 1) * 512], in_=pv[half][:]),
                     [pv[half]], [vsb[k]])
            P.dma("sp", V_d[T * 128:(T + 1) * 128, :], vsb[k][:], vsb[k], False)
            if pend is not None:
                rope_part2(*pend)
            pend = (T, KT_v, T * 128)
        for t in range(NQT):
            i = NKT + t
            k = i % 2
            P.dma("sp", hn[k][:], hnT_own[t].rearrange("p (dk t) -> p dk t", t=128), hn[k], True)
            proj(hn[k], 0, pk)
            rope_part1(i, pk, cs_q[t], 0.125)
            if pend is not None:
                rope_part2(*pend)
            pend = (i, QT_v, t * 128)
        rope_part2(*pend)
        P.barrier()


def phase_attn_core(P, nc, x_d, KT_d, V_d, QT_d, E_ap, elig_ap, ownm1_ap, cm_ap, w_o, ident, out_d):
    with ExitStack() as es0:
        idb = P.sbuf(es0, "ac_idb", [128, 128], BF16)
        P.dma("pool", idb[:], ident, idb, True)
        attn = P.sbuf(es0, "ac_attn", [128, NQT, D], BF16)
        with ExitStack() as es:
            KT = [P.sbuf(es, "ac_KT%d" % i, [128, SEQ], BF16) for i in range(2)]
            QT = [P.sbuf(es, "ac_QT%d" % i, [128, NTOK], BF16) for i in range(2)]
            Vh = [P.sbuf(es, "ac_V%d" % i, [128, NKT, 130], BF16) for i in range(2)]
            elig = P.sbuf(es, "ac_elig", [128, NBLK, NB], F32)
            ownm1 = P.sbuf(es, "ac_ownm1", [128, NBLK, NB], F32)
            cm = P.sbuf(es, "ac_cm", [128, 2, BLK], BF16)
            km = [P.sbuf(es, "ac_km%d" % i, [128, NB], F32) for i in range(2)]
            kmb = [P.sbuf(es, "ac_kmb%d" % i, [128, NB], BF16) for i in range(2)]
            gm = [P.sbuf(es, "ac_gm%d" % i, [128, NB], F32) for i in range(4)]
            top8 = [P.sbuf(es, "ac_top8%d" % i, [128, 8], F32) for i in range(4)]
            thr = [P.sbuf(es, "ac_thr%d" % i, [128, 1], F32) for i in range(4)]
            sel = [P.sbuf(es, "ac_sel%d" % i, [128, NB], F32) for i in range(4)]
            NM = [P.sbuf(es, "ac_NM%d" % i, [128, 128], BF16) for i in range(4)]
            PT = [P.sbuf(es, "ac_PT%d" % i, [128, 2, BLK], BF16) for i in range(3)]
            rs = [P.sbuf(es, "ac_rs%d" % i, [128, 1], F32) for i in range(2)]
            pS = [P.psum(es, "ac_pS%d" % i, [128, 2, BLK], F32) for i in range(2)]
            pO = [[P.psum(es, "ac_pO%d%d" % (i, j), [128, 512], F32) for j in range(2)] for i in range(2)]
            pG = P.psum(es, "ac_pG", [128, 512], F32)
            pN = P.psum(es, "ac_pN", [128, 128], BF16)
            P.dma("sp", elig[:], elig_ap.rearrange("(o a) b -> o a b", o=1).to_broadcast([128, NBLK, NB]), elig, True)
            P.dma("sp", ownm1[:], ownm1_ap.rearrange("(o a) b -> o a b", o=1).to_broadcast([128, NBLK, NB]), ownm1, True)
            P.dma("pool", cm[:], cm_ap, cm, True)
            for i in range(4):
                P.op("dve", lambda e, i=i: e.memset(NM[i][:], 0.0), [], [NM[i]])
            for i in range(2):
                P.op("pool", lambda e, i=i: e.memset(KT[i][64:128, :], 0.0), [], [KT[i]])
                P.dma("pool", KT[i][64:96, :], E_ap, KT[i], True)
                P.op("pool", lambda e, i=i: e.memset(QT[i][64:128, :], 0.0), [], [QT[i]])
                P.op("dve", lambda e, i=i: e.memset(Vh[i][:, :, 0:1], 1.0), [], [Vh[i]])
                P.op("dve", lambda e, i=i: e.memset(Vh[i][:, :, 129:130], 1.0), [], [Vh[i]])
            for i in range(2):
                P.op("dve", lambda e, i=i: e.memset(km[i][:], 0.0), [], [km[i]])
            V_v = V_d.rearrange("(T p) f -> p T f", p=128)
            st = dict(npS=0, nPT=0, ngate=0)

            def load_head(h):
                hp = h // 2
                if h % 2 == 0:
                    vb = Vh[hp % 2]
                    for q4 in range(4):
                        P.dma("sp", vb[:, q4 * 16:(q4 + 1) * 16, 1:129], V_v[:, q4 * 16:(q4 + 1) * 16, hp * 128:(hp + 1) * 128],
                              vb, True)
                kt_, qt_ = KT[h % 2], QT[h % 2]
                P.dma("sp", kt_[0:64, :], KT_d[h], kt_, True)
                P.dma("sp", qt_[0:64, :], QT_d[h], qt_, True)
                P.op("dve", lambda e: e.tensor_reduce(out=km[h % 2][0:64, :], in_=fap(kt_[0:64, 0:1], [(BLK, NB), (1, BLK)]),
                                                      axis=AX.X, op=ALU.add), [kt_], [km[h % 2]])
                P.op("dve", lambda e: e.tensor_copy(out=kmb[h % 2][:], in_=km[h % 2][:]), [km[h % 2]], [kmb[h % 2]])

            def gating_a(h, qt):
                qt_ = QT[h % 2]
                jl = qt // 2
                g = qt % 4
                P.op("pe", lambda e: e.matmul(pG[:, 0:NB], lhsT=qt_[:, qt * 128:(qt + 1) * 128], rhs=kmb[h % 2][:, :],
                                              start=True, stop=True), [qt_, kmb[h % 2]], [pG])
                P.op("dve", lambda e: e.tensor_tensor(out=gm[g][:], in0=pG[:, 0:NB], in1=elig[:, jl, :], op=ALU.add),
                     [pG, elig], [gm[g]])
                P.op("dve", lambda e: e.max(out=top8[g][:], in_=gm[g][:]), [gm[g]], [top8[g]])
                P.op("dve", lambda e: e.tensor_scalar(out=thr[g][:], in0=top8[g][:, 2:3], scalar1=-1.0e29, scalar2=None,
                                                      op0=ALU.max), [top8[g]], [thr[g]])
                P.op("dve", lambda e: e.tensor_scalar(out=sel[g][:], in0=gm[g][:], scalar1=thr[g][:, 0:1], scalar2=None,
                                                      op0=ALU.is_ge), [gm[g], thr[g]], [sel[g]])
                P.op("dve", lambda e: e.scalar_tensor_tensor(out=NM[g][:, 64:96], in0=sel[g][:], scalar=BIG,
                                                             in1=ownm1[:, jl, :], op0=ALU.mult, op1=ALU.add),
                     [sel[g], ownm1], [NM[g]])

            def gating_b(h, qt):
                qt_ = QT[h % 2]
                g = qt % 4
                P.op("pe", lambda e: e.transpose(out=pN[:], in_=NM[g][:], identity=idb[:]), [NM[g], idb], [pN])
                P.op("act", lambda e: e.copy(out=qt_[64:96, qt * 128:(qt + 1) * 128], in_=pN[64:96, :]), [pN], [qt_])

            def attention(h, jl):
                hb = h % 2
                hp = h // 2
                kt_, qt_ = KT[h % 2], QT[h % 2]
                vb = Vh[hp % 2]
                po = pO[jl % 2]
                blocks = [rr * 16 + m for m in range(jl + 1) for rr in range(2)]
                nb_ = len(blocks)
                def qk(bi):
                    b = blocks[bi]
                    ps = pS[(st["npS"] + bi) % 2]
                    for kt2 in range(2):
                        T = b * 2 + kt2
                        P.op("pe", lambda e, kt2=kt2, T=T: e.matmul(
                            ps[:, kt2, :], lhsT=kt_[:, T * 128:(T + 1) * 128], rhs=qt_[:, jl * BLK:(jl + 1) * BLK],
                            start=True, stop=True), [kt_, qt_], [ps])

                qk(0)
                for bi, b in enumerate(blocks):
                    if bi + 1 < nb_:
                        qk(bi + 1)
                    ps = pS[(st["npS"] + bi) % 2]
                    pt_ = PT[st["nPT"] % 3]
                    st["nPT"] += 1
                    P.op("act", lambda e: e.activation(out=pt_[:], in_=ps[:], func=AF.Exp), [ps], [pt_])
                    if b == jl:
                        P.op("pool", lambda e: e.tensor_tensor(out=pt_[:], in0=pt_[:], in1=cm[:, :, :], op=ALU.mult),
                             [pt_, cm], [pt_])
                    for kt2 in range(2):
                        T = b * 2 + kt2
                        for qs in range(2):
                            P.op("pe", lambda e, kt2=kt2, qs=qs, T=T: e.matmul(
                                po[qs][:, 0:65], lhsT=pt_[:, kt2, qs * 128:(qs + 1) * 128],
                                rhs=vb[:, T, hb * 65:hb * 65 + 65],
                                start=(bi == 0 and kt2 == 0), stop=(bi == nb_ - 1 and kt2 == 1)), [pt_, vb], [po[qs]])
                st["npS"] += nb_
                for qs in range(2):
                    sc = 0 if hb == 0 else 64
                    oc = 1 if hb == 0 else 0
                    P.op("dve", lambda e, qs=qs: e.reciprocal(out=rs[qs][:], in_=po[qs][:, sc:sc + 1]), [po[qs]], [rs[qs]])
                    P.op("dve", lambda e, qs=qs: e.tensor_scalar(
                        out=attn[:, jl * 2 + qs, h * 64:(h + 1) * 64], in0=po[qs][:, oc:oc + 64], scalar1=rs[qs][:, 0:1],
                        scalar2=None, op0=ALU.mult), [po[qs], rs[qs]], [attn])

            load_head(0)
            for qt in range(NQT):
                gating_a(0, qt)
                gating_b(0, qt)
            for h in range(NHEAD):
                if h + 1 < NHEAD:
                    load_head(h + 1)
                for jl in range(NBLK):
                    attention(h, jl)
                    if h + 1 < NHEAD:
                        gating_a(h + 1, 2 * jl)
                        gating_a(h + 1, 2 * jl + 1)
                        if jl > 0:
                            gating_b(h + 1, 2 * jl - 2)
                            gating_b(h + 1, 2 * jl - 1)
                if h + 1 < NHEAD:
                    gating_b(h + 1, NQT - 2)
                    gating_b(h + 1, NQT - 1)
        P.barrier()
        with ExitStack() as es:
            wo = P.sbuf(es, "ao_wo", [128, 8, D], BF16)
            w_o_v = w_o.rearrange("(dk p) f -> p dk f", p=128)
            for dk in range(0, 8, 4):
                P.dma("pool", wo[:, dk:dk + 4, :], w_o_v[:, dk:dk + 4, :], wo, True)
            aT = [P.sbuf(es, "ao_aT%d" % i, [128, 8, 128], BF16) for i in range(2)]
            xt = [P.sbuf(es, "ao_xt%d" % i, [128, D], F32) for i in range(2)]
            pt = [P.psum(es, "ao_pt%d" % i, [128, 8, 128], BF16) for i in range(2)]
            py = [P.psum(es, "ao_py%d" % i, [128, 512], F32) for i in range(2)]
            npy = 0
            for qt in range(NQT):
                k = qt % 2
                P.dma("sp", xt[k][:], x_d[qt * 128:(qt + 1) * 128, :], xt[k], True)
                for fk in range(8):
                    P.op("pe", lambda e, fk=fk, k=k, qt=qt: e.transpose(out=pt[k][:, fk, :], in_=attn[:, qt, fk * 128:(fk + 1) * 128],
                                                                     identity=idb[:]), [attn, idb], [pt[k]])
                P.op("act", lambda e, k=k: e.copy(out=aT[k][:], in_=pt[k][:]), [pt[k]], [aT[k]])
                for fh in range(2):
                    pyk = py[npy % 2]
                    npy += 1
                    for fk in range(8):
                        P.op("pe", lambda e, fk=fk, k=k, fh=fh, pyk=pyk: e.matmul(pyk[:], lhsT=aT[k][:, fk, :], rhs=wo[:, fk, fh * 512:(fh + 1) * 512],
                                                                                start=(fk == 0), stop=(fk == 7)), [aT[k], wo], [pyk])
                    P.op("dve", lambda e, k=k, fh=fh, pyk=pyk: e.tensor_tensor(out=xt[k][:, fh * 512:(fh + 1) * 512], in0=xt[k][:, fh * 512:(fh + 1) * 512],
                                                                              in1=pyk[:], op=ALU.add), [xt[k], pyk], [xt[k]])
                P.dma("sp", out_d[qt * 128:(qt + 1) * 128, :], xt[k][:], xt[k], False)
        P.barrier()


def attn_tables(core):
    r = core % 2
    half = DH // 2
    inv = (np.float32(10000.0) ** (-np.arange(half, dtype=np.float32) / np.float32(half))).astype(np.float32)

    def cs_for(rr, lt):
        j = 2 * (lt // 2) + rr
        pos = (j * BLK + (lt % 2) * 128 + np.arange(128)).astype(np.float32)
        ang = (pos[:, None] * inv[None, :]).astype(np.float32)
        return np.concatenate([np.cos(ang), np.sin(ang)], axis=1).astype(np.float32)

    rank_of = lambda T: r if T < NQT else 1 - r
    cs_k = np.stack([cs_for(rank_of(T), T % NQT) for T in range(NKT)])
    cs_q = np.stack([cs_for(r, t) for t in range(NQT)])
    nglob = np.array([2 * (b % 16) + (r if b < 16 else 1 - r) for b in range(NB)])
    E = np.zeros((NB, SEQ), np.float32)
    for b in range(NB):
        E[b, b * BLK:(b + 1) * BLK] = 1.0
    elig = np.zeros((NBLK, NB), np.float32)
    ownm1 = np.zeros((NBLK, NB), np.float32)
    for jl in range(NBLK):
        j = 2 * jl + r
        elig[jl] = np.where(nglob < j, 0.0, -1.0e30)
        ownm1[jl] = np.where(nglob == j, 0.0, -BIG)
    tri = np.zeros((128, 2, BLK), np.float32)
    for kt in range(2):
        kp = kt * 128 + np.arange(128)
        tri[:, kt, :] = (kp[:, None] <= np.arange(BLK)[None, :]).astype(np.float32)
    return {"cs_k": cs_k, "cs_q": cs_q, "E": E, "elig": elig, "ownm1": ownm1, "cm": np.ascontiguousarray(tri)}


def build_norm():
    nc = bass.Bass("TRN2", target_bir_lowering=False)
    x = nc.dram_tensor("x", [NTOK, D], F32, kind="ExternalInput").ap()
    g = nc.dram_tensor("g", [D], F32, kind="ExternalInput").ap()
    ident = nc.dram_tensor("ident", [128, 128], F32, kind="ExternalInput").ap()
    hnT_own = nc.dram_tensor("hnT_own", [NQT, 128, D], BF16, kind="ExternalOutput").ap()
    with ExitStack() as es:
        P = Prog(nc, es)
        phase_norm_T(P, nc, x, g, ident, hnT_own)
    return nc


def attn_decls(nc):
    d = {}
    d["cs_k"] = nc.dram_tensor("cs_k", [NKT, 128, 64], F32, kind="ExternalInput").ap()
    d["cs_q"] = nc.dram_tensor("cs_q", [NQT, 128, 64], F32, kind="ExternalInput").ap()
    d["E"] = nc.dram_tensor("E", [NB, SEQ], F32, kind="ExternalInput").ap()
    d["elig"] = nc.dram_tensor("elig", [NBLK, NB], F32, kind="ExternalInput").ap()
    d["ownm1"] = nc.dram_tensor("ownm1", [NBLK, NB], F32, kind="ExternalInput").ap()
    d["cm"] = nc.dram_tensor("cm", [128, 2, BLK], F32, kind="ExternalInput").ap()
    return d


def build_attn(debug=False):
    nc = bass.Bass("TRN2", target_bir_lowering=False)
    x = nc.dram_tensor("x", [NTOK, D], F32, kind="ExternalInput").ap()
    hnT_all = nc.dram_tensor("hnT_all", [NKT, 128, D], BF16, kind="ExternalInput").ap()
    hnT_own = nc.dram_tensor("hnT_own", [NQT, 128, D], BF16, kind="ExternalInput").ap()
    w_qkv = nc.dram_tensor("w_qkv", [D, 3 * D], F32, kind="ExternalInput").ap()
    w_o = nc.dram_tensor("w_o", [D, D], F32, kind="ExternalInput").ap()
    ident = nc.dram_tensor("ident", [128, 128], F32, kind="ExternalInput").ap()
    t = attn_decls(nc)
    out = nc.dram_tensor("out", [NTOK, D], F32, kind="ExternalOutput").ap()
    kd = "ExternalOutput" if debug else "Internal"
    KT_d = nc.dram_tensor("KT_d", [NHEAD, DH, SEQ], BF16, kind=kd).ap()
    V_d = nc.dram_tensor("V_d", [SEQ, D], BF16, kind=kd).ap()
    QT_d = nc.dram_tensor("QT_d", [NHEAD, DH, NTOK], BF16, kind=kd).ap()
    with ExitStack() as es:
        P = Prog(nc, es)
        phase_attn_qkv(P, nc, hnT_all, hnT_own, w_qkv, t["cs_k"], t["cs_q"], ident, KT_d, V_d, QT_d)
        phase_attn_core(P, nc, x, KT_d, V_d, QT_d, t["E"], t["elig"], t["ownm1"], t["cm"], w_o, ident, out)
    return nc


def build_conv():
    nc = bass.Bass("TRN2", target_bir_lowering=False)
    x0 = nc.dram_tensor("x0", [NTOK, D], F32, kind="ExternalInput").ap()
    xh = nc.dram_tensor("xh", [NBLK * 2, D], F32, kind="ExternalInput").ap()
    g = nc.dram_tensor("g", [D], F32, kind="ExternalInput").ap()
    w_in = nc.dram_tensor("w_in", [D, 3 * D], F32, kind="ExternalInput").ap()
    conv_w = nc.dram_tensor("conv_w", [3, D], F32, kind="ExternalInput").ap()
    w_out = nc.dram_tensor("w_out", [D, D], F32, kind="ExternalInput").ap()
    ident = nc.dram_tensor("ident", [128, 128], F32, kind="ExternalInput").ap()
    x1 = nc.dram_tensor("x1", [NTOK, D], F32, kind="ExternalOutput").ap()
    with ExitStack() as es:
        P = Prog(nc, es)
        phase_conv(P, nc, x0, xh, g, w_in, conv_w, w_out, ident, x1)
    return nc


_NC_CACHE = {}
NALL = 2 * NTOK


def build_fused():
    nc = bass.Bass("TRN2", target_bir_lowering=False)
    I = lambda name, shape, dt=F32: nc.dram_tensor(name, list(shape), dt, kind="ExternalInput").ap()
    S = lambda name, shape, dt=F32: nc.dram_tensor(name, list(shape), dt, kind="Internal").ap()
    x_all = I("x_all", [NALL, D])
    xh_all = I("xh_all", [2 * NBLK * 2, D])
    norm_mix = I("norm_mix", [2, D])
    norm_ffn = I("norm_ffn", [2, D])
    norm_final = I("norm_final", [D])
    conv_w_in = I("conv_w_in", [D, 3 * D])
    conv_w = I("conv_w", [3, D])
    conv_w_out = I("conv_w_out", [D, D])
    w_qkv = I("w_qkv", [D, 3 * D])
    w_o = I("w_o", [D, D])
    wq = [I("wq%d" % l, [D, D]) for l in range(2)]
    k1 = [I("k1_%d" % l, [NKEY, 64]) for l in range(2)]
    k2 = [I("k2_%d" % l, [NKEY, 64]) for l in range(2)]
    u = [I("u%d" % l, [NKEY * NKEY, D]) for l in range(2)]
    v = [I("v%d" % l, [NKEY * NKEY, D]) for l in range(2)]
    ident = I("ident", [128, 128])
    iota = I("iota", [128, 128])
    t = attn_decls(nc)
    out = nc.dram_tensor("out", [NTOK, D], F32, kind="ExternalOutput").ap()
    x1 = S("x1", [NALL, D])
    x2 = S("x2", [NALL, D])
    x3 = S("x3", [NTOK, D])
    hnT_all = S("hnT_all", [NKT, 128, D], BF16)
    KT_d = S("KT_d", [NHEAD, DH, SEQ], BF16)
    V_d = S("V_d", [SEQ, D], BF16)
    QT_d = S("QT_d", [NHEAD, DH, NTOK], BF16)
    UT = [S("UT%d" % l, [128, 128, D], BF16) for l in range(2)]
    VV = [S("VV%d" % l, [NKEY * NKEY, D], BF16) for l in range(2)]
    hnT_d = S("hnT_d", [NALL // GRP, 128, 8 * GRP], BF16)
    tab_d = S("tab_d", [3, 128, NALL])
    with ExitStack() as es:
        P = Prog(nc, es)
        phase_conv(P, nc, x_all, xh_all, norm_mix[0], conv_w_in, conv_w, conv_w_out, ident, x1, nblk=2 * NBLK,
                   side_jobs=[(u[l], v[l], UT[l], VV[l]) for l in range(2)])
        phase_peer_route(P, nc, x1, norm_ffn[0], wq[0], k1[0], k2[0], ident, iota, hnT_d, tab_d, NALL // 128)
        phase_peer_experts(P, nc, x1, hnT_d, tab_d, UT[0], VV[0], iota, x2, None, NALL // GRP)
        phase_norm_T(P, nc, x2, norm_mix[1], ident, hnT_all, NKT)
        phase_attn_qkv(P, nc, hnT_all, hnT_all, w_qkv, t["cs_k"], t["cs_q"], ident, KT_d, V_d, QT_d)
        phase_attn_core(P, nc, x2, KT_d, V_d, QT_d, t["E"], t["elig"], t["ownm1"], t["cm"], w_o, ident, x3)
        phase_peer_route(P, nc, x3, norm_ffn[1], wq[1], k1[1], k2[1], ident, iota, hnT_d, tab_d, NTOK // 128)
        phase_peer_experts(P, nc, x3, hnT_d, tab_d, UT[1], VV[1], iota, out, norm_final, NTOK // GRP)
        P.wait_all_dma("sp")
    return nc


def kernel(x, norm_mix, norm_ffn, conv_w_in, conv_w, conv_w_out, attn_w_qkv, attn_w_o,
           peer_w_q, peer_k1, peer_k2, peer_u, peer_v, norm_final):
    from concourse.bass_utils import run_bass_kernel_spmd
    f = lambda a: np.ascontiguousarray(np.asarray(a, dtype=np.float32))
    x = f(x)
    B, S, _ = x.shape
    cores = list(range(8))
    if "fused" not in _NC_CACHE:
        _NC_CACHE["fused"] = build_fused()
    shared = {"norm_mix": f(norm_mix), "norm_ffn": f(norm_ffn), "norm_final": f(norm_final),
              "conv_w_in": f(conv_w_in[0]), "conv_w": f(conv_w[0]), "conv_w_out": f(conv_w_out[0]),
              "w_qkv": f(attn_w_qkv[0]), "w_o": f(attn_w_o[0])}
    for l in range(2):
        shared["wq%d" % l] = f(peer_w_q[l])
        shared["k1_%d" % l] = f(peer_k1[l])
        shared["k2_%d" % l] = f(peer_k2[l])
        shared["u%d" % l] = f(peer_u[l])
        shared["v%d" % l] = f(peer_v[l])
    shared.update(consts())
    in_maps = []
    for c in cores:
        xs, xh = shard_tokens(x, c)
        xo, xoh = shard_tokens(x, c ^ 1)
        m = dict(shared)
        m["x_all"] = np.concatenate([xs, xo], axis=0)
        m["xh_all"] = np.concatenate([xh, xoh], axis=0)
        m.update(attn_tables(c))
        in_maps.append(m)
    res = run_bass_kernel_spmd(_NC_CACHE["fused"], in_maps, core_ids=cores)
    outs = [r["out"] for r in res.results]
    return unshard_tokens(outs, B, S)
```

```python
import numpy as np
import concourse.bass as bass
import concourse.mybir as mybir
from contextlib import ExitStack

F32 = mybir.dt.float32
BF16 = mybir.dt.bfloat16
I32 = mybir.dt.int32
U32 = mybir.dt.uint32
ALU = mybir.AluOpType
AF = mybir.ActivationFunctionType
AX = mybir.AxisListType

ENGS = ("pe", "act", "dve", "pool", "sp")


class Buf:
    def __init__(self, prog, t, name):
        self.p = prog
        self.t = t
        self.name = name
        self.last_w = None
        self.readers = {}
        self.dsem = None
        self.dcount = 0

    def __getitem__(self, idx):
        return self.t[idx]


class Prog:
    def __init__(self, nc, es, direct=True):
        self.nc = nc
        self.es = es
        self.direct = direct
        self.eng = {"pe": nc.tensor, "act": nc.scalar, "dve": nc.vector, "pool": nc.gpsimd, "sp": nc.sync}
        self.streams = {e: [] for e in ENGS}
        self.count = {e: 0 for e in ENGS}
        self.known = {e: {} for e in ENGS}
        self.sems = {}
        for e in ENGS:
            self.sems["E_" + e] = es.enter_context(nc.semaphore("E_" + e))
        self.free_dsems = []
        self.ndsem = 0
        self.dsem_count = {}
        self.all_bufs = []

    def sbuf(self, es, name, shape, dt):
        self.uid = getattr(self, "uid", 0) + 1
        t = es.enter_context(self.nc.sbuf_tensor("%s_%d" % (name, self.uid), list(shape), dt))
        b = Buf(self, t, name)
        return b

    def psum(self, es, name, shape, dt):
        self.uid = getattr(self, "uid", 0) + 1
        t = es.enter_context(self.nc.psum_tensor("%s_%d" % (name, self.uid), list(shape), dt))
        b = Buf(self, t, name)
        return b

    def _get_dsem(self, b):
        if b.dsem is None:
            if self.free_dsems:
                k = self.free_dsems.pop()
            else:
                k = "D_%d" % self.ndsem
                self.ndsem += 1
                self.sems[k] = self.es.enter_context(self.nc.semaphore(k))
                self.dsem_count[k] = 0
            b.dsem = k
            b.dcount = self.dsem_count[k]
            self.all_bufs.append(b)
        return b.dsem

    def release(self, bufs):
        for b in bufs:
            if b.dsem is not None:
                self.dsem_count[b.dsem] = b.dcount
                self.free_dsems.append(b.dsem)
                b.dsem = None

    def _collect(self, eng, reads, writes):
        deps = {}

        def add(ev):
            if ev is None:
                return
            k, v = ev
            if deps.get(k, 0) < v:
                deps[k] = v

        for b in reads:
            add(b.last_w)
        for b in writes:
            add(b.last_w)
            for k, v in b.readers.items():
                add((k, v))
        waits = []
        for k, v in deps.items():
            if eng == "pe" and k == "E_pe":
                continue
            if self.known[eng].get(k, 0) < v:
                self.known[eng][k] = v
                waits.append((k, v))
        return waits

    def op(self, eng, fn, reads=(), writes=()):
        waits = self._collect(eng, reads, writes)
        self.count[eng] += 1
        ev = ("E_" + eng, self.count[eng])
        self._push(eng, (waits, fn, ("E_" + eng, 1)))
        for b in reads:
            b.readers[ev[0]] = ev[1]
        for b in writes:
            b.last_w = ev
            b.readers = {}
        return ev

    def dma(self, q, out, in_, buf, load, **kw):
        if load:
            waits = self._collect(q, (), (buf,))
        else:
            waits = self._collect(q, (buf,), ())
        k = self._get_dsem(buf)
        buf.dcount += 16
        ev = (k, buf.dcount)
        self._push(q, (waits, lambda e: e.dma_start(out=out, in_=in_, **kw), (k, 16)))
        if load:
            buf.last_w = ev
            buf.readers = {}
        else:
            buf.readers[k] = buf.dcount
        return ev

    def dma2(self, q, out, in_, dst_buf, src_buf, **kw):
        raise NotImplementedError

    def barrier(self):
        waits = []
        for b in self.all_bufs:
            if b.dsem is not None and self.known["sp"].get(b.dsem, 0) < b.dcount:
                self.known["sp"][b.dsem] = b.dcount
                waits.append((b.dsem, b.dcount))
        for e in ENGS:
            if e == "sp":
                continue
            k = "E_" + e
            if self.known["sp"].get(k, 0) < self.count[e]:
                self.known["sp"][k] = self.count[e]
                waits.append((k, self.count[e]))
        self.count["sp"] += 1
        v = self.count["sp"]
        self._push("sp", (waits, lambda e, s=self.sems["E_sp"]: e.sem_inc(s, 1), None))
        for e in ENGS:
            if e == "sp":
                continue
            self.known[e]["E_sp"] = v
            self._push(e, ([("E_sp", v)], None, None))
            for k2, v2 in self.known["sp"].items():
                if self.known[e].get(k2, 0) < v2:
                    self.known[e][k2] = v2
        self.release(self.all_bufs)
        self.all_bufs = [b for b in self.all_bufs if False]

    def wait_all_dma(self, eng="sp"):
        waits = []
        for b in self.all_bufs:
            if b.dsem is not None and self.known[eng].get(b.dsem, 0) < b.dcount:
                self.known[eng][b.dsem] = b.dcount
                waits.append((b.dsem, b.dcount))
        self._push(eng, (waits, None, None))

    def _push(self, e, item):
        if not self.direct:
            self.streams[e].append(item)
            return
        waits, fn, inc = item
        eng = self.eng[e]
        for k, v in waits:
            eng.wait_ge(self.sems[k], v)
        if fn is not None:
            ins = fn(eng)
            if inc is not None:
                ins.then_inc(self.sems[inc[0]], inc[1])

    def emit(self):
        if self.direct:
            return
        nc = self.nc
        engmap = {"pe": "tensor", "act": "scalar", "dve": "vector", "pool": "gpsimd", "sp": "sync"}
        with nc.Block() as block:
            for e in ENGS:
                stream = self.streams[e]

                def body(eng, stream=stream):
                    for waits, fn, inc in stream:
                        for k, v in waits:
                            eng.wait_ge(self.sems[k], v)
                        if fn is not None:
                            ins = fn(eng)
                            if inc is not None:
                                ins.then_inc(self.sems[inc[0]], inc[1])

                getattr(block, engmap[e])(body)


D = 1024
NTOK = 4096
BLK = 256
NBLK = NTOK // BLK
EPS = 1e-6


def bcast_rows(ap1d, nparts):
    n = ap1d.shape[0]
    return ap1d.rearrange("(o n) -> o n", o=1).to_broadcast([nparts, n])


class NormScratch:
    def __init__(self, P, es, tag):
        self.junk = P.sbuf(es, "nj" + tag, [128, D], BF16)
        self.ss = P.sbuf(es, "nss" + tag, [128, 1], F32)
        self.ms = P.sbuf(es, "nms" + tag, [128, 1], F32)
        self.rstd = P.sbuf(es, "nrs" + tag, [128, 1], F32)
        self.hn = P.sbuf(es, "nhn" + tag, [128, D], BF16)
        self.pt = P.psum(es, "npt" + tag, [128, 8, 128], BF16)


def emit_rstd(P, xt, rows, S):
    P.op("act", lambda e: e.activation(out=S.junk[0:rows, :], in_=xt[0:rows, :], func=AF.Square,
                                       accum_out=S.ss[0:rows, :]), [xt], [S.junk, S.ss])
    P.op("dve", lambda e: e.tensor_scalar(out=S.ms[0:rows, :], in0=S.ss[0:rows, :], scalar1=1.0 / D, scalar2=EPS,
                                          op0=ALU.mult, op1=ALU.add), [S.ss], [S.ms])
    P.op("act", lambda e: e.activation(out=S.ms[0:rows, :], in_=S.ms[0:rows, :], func=AF.Sqrt), [S.ms], [S.ms])
    P.op("dve", lambda e: e.reciprocal(out=S.rstd[0:rows, :], in_=S.ms[0:rows, :]), [S.ms], [S.rstd])


def emit_norm_T(P, xt, rows, gt, idb, hnT, col0, S, evac="act"):
    emit_rstd(P, xt, rows, S)
    P.op("dve", lambda e: e.scalar_tensor_tensor(out=S.hn[0:rows, :], in0=xt[0:rows, :], scalar=S.rstd[0:rows, 0:1],
                                                 in1=gt[0:rows, :], op0=ALU.mult, op1=ALU.mult),
         [xt, S.rstd, gt], [S.hn])
    for dk in range(8):
        P.op("pe", lambda e, dk=dk: e.transpose(out=S.pt[:, dk, 0:rows], in_=S.hn[0:rows, dk * 128:(dk + 1) * 128],
                                                identity=idb[0:rows, 0:rows]), [S.hn, idb], [S.pt])
    if evac == "act":
        P.op("act", lambda e: e.copy(out=hnT[:, :, col0:col0 + rows], in_=S.pt[:, :, 0:rows]), [S.pt], [hnT])
    else:
        P.op("dve", lambda e: e.tensor_copy(out=hnT[:, :, col0:col0 + rows], in_=S.pt[:, :, 0:rows]), [S.pt], [hnT])


def phase_conv(P, nc, x0, xh, g_ap, w_in, conv_w, w_out, ident, x1, nblk=NBLK, side_jobs=None):
    with ExitStack() as es:
        side = prep_gen(P, es, side_jobs, ident) if side_jobs else None
        side_steps = (128 * len(side_jobs) + nblk - 1) // nblk if side_jobs else 0
        idb = P.sbuf(es, "c_idb", [128, 128], BF16)
        gt = P.sbuf(es, "c_gt", [128, D], F32)
        win = P.sbuf(es, "c_win", [128, 8, 3 * D], BF16)
        wout = P.sbuf(es, "c_wout", [128, 8, D], BF16)
        cw = P.sbuf(es, "c_cw", [128, 8, 3], F32)
        P.dma("pool", idb[:], ident, idb, True)
        P.dma("sp", gt[:], bcast_rows(g_ap, 128), gt, True)
        w_in_v = w_in.rearrange("(dk p) f -> p dk f", p=128)
        for dk in range(8):
            P.dma("pool", win[:, dk, :], w_in_v[:, dk, :], win, True)
        w_out_v = w_out.rearrange("(dk p) f -> p dk f", p=128)
        for dk in range(0, 8, 4):
            P.dma("pool", wout[:, dk:dk + 4, :], w_out_v[:, dk:dk + 4, :], wout, True)
        for kk in range(3):
            P.dma("sp", cw[:, :, kk], conv_w[kk, :].rearrange("(fc p) -> p fc", p=128), cw, True,
                  allow_slow_non_contiguous=True)
        sets = []
        for s in range(2):
            t = str(s)
            st = dict(
                xt=[P.sbuf(es, "c_xt%d_%s" % (i, t), [128, D], F32) for i in range(2)],
                xh=P.sbuf(es, "c_xh" + t, [2, D], F32),
                hnT=P.sbuf(es, "c_hnT" + t, [128, 8, BLK + 2], BF16),
                gT=P.sbuf(es, "c_gT" + t, [128, 8, BLK], BF16),
                S=NormScratch(P, es, "c" + t),
            )
            sets.append(st)
        csb = [P.sbuf(es, "c_csb%d" % i, [128, BLK + 2], F32) for i in range(2)]
        u = [P.sbuf(es, "c_u%d" % i, [128, BLK + 2], F32) for i in range(2)]
        acc = [P.sbuf(es, "c_acc%d" % i, [128, BLK], F32) for i in range(2)]
        pb = [P.psum(es, "c_pb%d" % i, [128, 512], F32) for i in range(3)]
        py = [P.psum(es, "c_py%d" % i, [128, 512], F32) for i in range(2)]
        N = BLK + 2
        nbuf = 0
        npy = 0
        for blk in range(nblk):
            st = sets[blk % 2]
            S = st["S"]
            hnT = st["hnT"]
            gT = st["gT"]
            P.dma("sp", st["xh"][:], xh[blk * 2:blk * 2 + 2, :], st["xh"], True)
            for i in range(2):
                P.dma("sp", st["xt"][i][:], x0[blk * BLK + i * 128: blk * BLK + (i + 1) * 128, :], st["xt"][i], True)
            emit_norm_T(P, st["xh"], 2, gt, idb, hnT, 0, S)
            for i in range(2):
                emit_norm_T(P, st["xt"][i], 128, gt, idb, hnT, 2 + i * 128, S)
            for fc in range(8):
                for j in range(3):
                    col = j * D + fc * 128
                    for dk in range(8):
                        P.op("pe", lambda e, j=j, dk=dk, col=col: e.matmul(
                            pb[j][:, 0:N], lhsT=win[:, dk, col:col + 128], rhs=hnT[:, dk, 0:N],
                            start=(dk == 0), stop=(dk == 7)), [win, hnT], [pb[j]])
                k = nbuf % 2
                nbuf += 1
                P.op("act", lambda e, k=k: e.copy(out=csb[k][:, 0:N], in_=pb[1][:, 0:N]), [pb[1]], [csb[k]])
                P.op("dve", lambda e, k=k: e.tensor_tensor(out=u[k][:, 0:N], in0=csb[k][:, 0:N], in1=pb[2][:, 0:N],
                                                           op=ALU.mult), [csb[k], pb[2]], [u[k]])
                P.op("dve", lambda e, k=k, fc=fc: e.tensor_scalar(out=acc[k][:], in0=u[k][:, 0:BLK],
                                                                   scalar1=cw[:, fc, 0:1], scalar2=None, op0=ALU.mult),
                     [u[k], cw], [acc[k]])
                for kk in (1, 2):
                    P.op("dve", lambda e, k=k, fc=fc, kk=kk: e.scalar_tensor_tensor(
                        out=acc[k][:], in0=u[k][:, kk:kk + BLK], scalar=cw[:, fc, kk:kk + 1], in1=acc[k][:],
                        op0=ALU.mult, op1=ALU.add), [u[k], cw, acc[k]], [acc[k]])
                P.op("dve", lambda e, k=k, fc=fc: e.tensor_tensor(out=gT[:, fc, :], in0=acc[k][:], in1=pb[0][:, 2:N],
                                                                   op=ALU.mult), [acc[k], pb[0]], [gT])
            for _ in range(side_steps):
                next(side, None)
            for i in range(2):
                xt = st["xt"][i]
                for fh in range(2):
                    pyk = py[npy % 2]
                    npy += 1
                    for fc in range(8):
                        P.op("pe", lambda e, fc=fc, i=i, fh=fh, pyk=pyk: e.matmul(
                            pyk[:, :], lhsT=gT[:, fc, i * 128:(i + 1) * 128], rhs=wout[:, fc, fh * 512:(fh + 1) * 512],
                            start=(fc == 0), stop=(fc == 7)), [gT, wout], [pyk])
                    P.op("dve", lambda e, xt=xt, fh=fh, pyk=pyk: e.tensor_tensor(
                        out=xt[:, fh * 512:(fh + 1) * 512], in0=xt[:, fh * 512:(fh + 1) * 512], in1=pyk[:, :],
                        op=ALU.add), [xt, pyk], [xt])
                P.dma("sp", x1[blk * BLK + i * 128: blk * BLK + (i + 1) * 128, :], xt[:], xt, False)
        P.barrier()


def block_of(core, jl):
    return core // 2, 2 * jl + (core % 2)


def shard_tokens(x, core):
    b = core // 2
    xs = np.empty((NTOK, x.shape[2]), x.dtype)
    xh = np.zeros((NBLK * 2, x.shape[2]), x.dtype)
    for jl in range(NBLK):
        _, j = block_of(core, jl)
        xs[jl * BLK:(jl + 1) * BLK] = x[b, j * BLK:(j + 1) * BLK]
        if j > 0:
            xh[jl * 2:jl * 2 + 2] = x[b, j * BLK - 2:j * BLK]
    return xs, xh


def unshard_tokens(outs, B, S):
    full = np.empty((B, S, outs[0].shape[1]), outs[0].dtype)
    for core, o in enumerate(outs):
        for jl in range(NBLK):
            b, j = block_of(core, jl)
            full[b, j * BLK:(j + 1) * BLK] = o[jl * BLK:(jl + 1) * BLK]
    return full


NH = 8
NKEY = 128
TOPK = 16
GRP = 256
NGRP = NTOK // GRP
NEG = -1.0e30


def fap(a, dims):
    return bass.AP(a.tensor, a.offset, [list(a.ap[0])] + [list(d) for d in dims])


def view(P, buf, name):
    b = Buf(P, buf.t, name)
    return b


def prep_gen(P, es, jobs, ident):
    idb = P.sbuf(es, "pg_idb", [128, 128], BF16)
    P.dma("pool", idb[:], ident, idb, True)
    ub = [P.sbuf(es, "pg_ub%d" % i, [128, D], BF16) for i in range(3)]
    ut = [P.sbuf(es, "pg_ut%d" % i, [128, 8, 128], BF16) for i in range(3)]
    vb = [P.sbuf(es, "pg_vb%d" % i, [128, D], BF16) for i in range(3)]
    pt = P.psum(es, "pg_pt", [128, 8, 128], BF16)
    n = 0
    for (u_ap, v_ap, UT_d, V_d) in jobs:
        for c in range(128):
            k = n % 3
            n += 1
            P.dma("pool", ub[k][:], u_ap[c * 128:(c + 1) * 128, :], ub[k], True)
            for dk in range(8):
                P.op("pe", lambda e, dk=dk, k=k: e.transpose(out=pt[:, dk, :], in_=ub[k][:, dk * 128:(dk + 1) * 128],
                                                           identity=idb[:]), [ub[k], idb], [pt])
            P.op("act", lambda e, k=k: e.copy(out=ut[k][:], in_=pt[:]), [pt], [ut[k]])
            P.dma("sp", UT_d[c].rearrange("p (dk q) -> p dk q", q=128), ut[k][:], ut[k], False)
            P.dma("pool", vb[k][:], v_ap[c * 128:(c + 1) * 128, :], vb[k], True)
            P.dma("sp", V_d[c * 128:(c + 1) * 128, :], vb[k][:], vb[k], False)
            yield


def phase_peer_prep(P, nc, u_ap, v_ap, ident, UT_d, V_d):
    with ExitStack() as es:
        idb = P.sbuf(es, "pp_idb", [128, 128], BF16)
        P.dma("pool", idb[:], ident, idb, True)
        ub = [P.sbuf(es, "pp_ub%d" % i, [128, D], BF16) for i in range(3)]
        ut = [P.sbuf(es, "pp_ut%d" % i, [128, 8, 128], BF16) for i in range(3)]
        vb = [P.sbuf(es, "pp_vb%d" % i, [128, D], BF16) for i in range(3)]
        pt = [P.psum(es, "pp_pt%d" % i, [128, 8, 128], BF16) for i in range(2)]
        for c in range(128):
            k = c % 3
            P.dma("pool", ub[k][:], u_ap[c * 128:(c + 1) * 128, :], ub[k], True)
            for dk in range(8):
                P.op("pe", lambda e, dk=dk, k=k, c=c: e.transpose(out=pt[c % 2][:, dk, :], in_=ub[k][:, dk * 128:(dk + 1) * 128],
                                                                 identity=idb[:]), [ub[k], idb], [pt[c % 2]])
            if c % 2:
                P.op("act", lambda e, k=k, c=c: e.copy(out=ut[k][:], in_=pt[c % 2][:]), [pt[c % 2]], [ut[k]])
            else:
                P.op("dve", lambda e, k=k, c=c: e.tensor_copy(out=ut[k][:], in_=pt[c % 2][:]), [pt[c % 2]], [ut[k]])
            P.dma("sp", UT_d[c].rearrange("p (dk q) -> p dk q", q=128), ut[k][:], ut[k], False)
            P.dma("pool", vb[k][:], v_ap[c * 128:(c + 1) * 128, :], vb[k], True)
            P.dma("sp", V_d[c * 128:(c + 1) * 128, :], vb[k][:], vb[k], False)
        P.barrier()


def phase_peer_route(P, nc, x_d, g_ap, wq_ap, k1_ap, k2_ap, ident, iota_ap, hnT_d, tab_d, ntiles=NTOK // 128, stop=99):
    with ExitStack() as es:
        idb = P.sbuf(es, "pr_idb", [128, 128], BF16)
        idf = P.sbuf(es, "pr_idf", [128, 128], F32)
        iot = P.sbuf(es, "pr_iot", [128, 128], F32)
        gt = P.sbuf(es, "pr_gt", [128, D], F32)
        wq = P.sbuf(es, "pr_wq", [128, 8, D], BF16)
        kcat = P.sbuf(es, "pr_kcat", [128, 128], BF16)
        kT = P.sbuf(es, "pr_kT", [128, 128], BF16)
        P.dma("pool", idb[:], ident, idb, True)
        P.dma("sp", idf[:], ident, idf, True)
        P.dma("sp", iot[:], iota_ap, iot, True)
        P.dma("sp", gt[:], bcast_rows(g_ap, 128), gt, True)
        wq_v = wq_ap.rearrange("(dk p) f -> p dk f", p=128)
        for dk in range(0, 8, 4):
            P.dma("pool", wq[:, dk:dk + 4, :], wq_v[:, dk:dk + 4, :], wq, True)
        P.dma("pool", kcat[:, 0:64], k1_ap, kcat, True)
        P.dma("pool", kcat[:, 64:128], k2_ap, kcat, True)
        Ss = [NormScratch(P, es, "pr%d" % i) for i in range(2)]
        P.op("pe", lambda e: e.transpose(out=Ss[0].pt[:, 0, :], in_=kcat[:], identity=idb[:]), [kcat, idb], [Ss[0].pt])
        P.op("act", lambda e: e.copy(out=kT[:], in_=Ss[0].pt[:, 0, :]), [Ss[0].pt], [kT])
        kblk = P.sbuf(es, "pr_kblk", [128, 256], BF16)
        P.op("dve", lambda e: e.memset(kblk[:], 0.0), [], [kblk])
        P.op("dve", lambda e: e.tensor_copy(out=kblk[0:64, 0:128], in_=kT[0:64, :]), [kT], [kblk])
        P.op("dve", lambda e: e.tensor_copy(out=kblk[64:128, 128:256], in_=kT[64:128, :]), [kT], [kblk])
        xts = [P.sbuf(es, "pr_xt%d" % i, [128, D], F32) for i in range(2)]
        hnTs = [P.sbuf(es, "pr_hnT%d" % i, [128, 8, 128], BF16) for i in range(2)]
        qTs = [P.sbuf(es, "pr_qT%d" % i, [128, 8, 128], BF16) for i in range(2)]
        ps_q = P.psum(es, "pr_psq", [128, 4, 128], F32)
        ps_s = [P.psum(es, "pr_pss%d" % i, [128, 512], F32) for i in range(4)]
        ps_t = P.psum(es, "pr_pst", [128, 3, 128], F32)
        s_sb = P.sbuf(es, "pr_s", [128, 16 * 128], F32)
        segs = [view(P, s_sb, "seg%d" % j) for j in range(16)]
        v = P.sbuf(es, "pr_v", [128, 16, 16], F32)
        vseg = [view(P, v, "vseg%d" % j) for j in range(16)]
        ix = P.sbuf(es, "pr_ix", [128, 16, 16], U32)
        ixseg = [view(P, ix, "ixseg%d" % j) for j in range(16)]
        ixf = P.sbuf(es, "pr_ixf", [128, 16, 16], F32)
        cand = P.sbuf(es, "pr_cand", [128, 8, 256], F32)
        cseg = [view(P, cand, "cseg%d" % j) for j in range(8)]
        ts = P.sbuf(es, "pr_ts", [128, 8, 16], F32)
        tsseg = [view(P, ts, "tsseg%d" % j) for j in range(8)]
        pos = P.sbuf(es, "pr_pos", [128, 8, 16], U32)
        posseg = [view(P, pos, "posseg%d" % j) for j in range(8)]
        ku = [P.sbuf(es, "pr_ku%d" % i, [128, 8, 16], U32) for i in range(2)]
        kf = [P.sbuf(es, "pr_kf%d" % i, [128, 8, 16], F32) for i in range(2)]
        eq = [P.sbuf(es, "pr_eq%d" % i, [128, 128, 16], F32) for i in range(2)]
        res = P.sbuf(es, "pr_res", [128, 3, 128], F32)
        resv = [view(P, res, "resv%d" % j) for j in range(3)]
        dd = P.sbuf(es, "pr_dd", [128, 8, 16], F32)
        ee = P.sbuf(es, "pr_ee", [128, 8, 16], F32)
        zz = P.sbuf(es, "pr_zz", [128, 8], F32)
        rz = P.sbuf(es, "pr_rz", [128, 8], F32)
        resT = [P.sbuf(es, "pr_resT%d" % i, [128, 3, 128], F32) for i in range(2)]
        tab_v = tab_d.rearrange("a p t -> p a t")
        for tile in range(ntiles):
            k = tile % 2
            xt, S, hnT, qT = xts[k], Ss[k], hnTs[k], qTs[k]
            grp, sub = tile // 2, tile % 2
            P.dma("sp", xt[:], x_d[tile * 128:(tile + 1) * 128, :], xt, True)
            emit_norm_T(P, xt, 128, gt, idb, hnT, 0, S)
            P.dma("sp", hnT_d[grp].rearrange("p (dk t) -> p dk t", t=GRP)[:, :, sub * 128:(sub + 1) * 128], hnT[:], hnT, False)
            if stop <= 1:
                continue
            for half in range(2):
                for f4 in range(4):
                    fc = half * 4 + f4
                    for dk in range(8):
                        P.op("pe", lambda e, fc=fc, f4=f4, dk=dk: e.matmul(
                            ps_q[:, f4, :], lhsT=wq[:, dk, fc * 128:(fc + 1) * 128], rhs=hnT[:, dk, :],
                            start=(dk == 0), stop=(dk == 7)), [wq, hnT], [ps_q])
                P.op("act", lambda e, half=half: e.copy(out=qT[:, half * 4:(half + 1) * 4, :], in_=ps_q[:]), [ps_q], [qT])
            if stop <= 2:
                continue
            for b4 in range(4):
                for hh in range(2):
                    h = b4 * 2 + hh
                    P.op("pe", lambda e, b4=b4, hh=hh, h=h: e.matmul(
                        ps_s[b4][:, hh * 256:(hh + 1) * 256], lhsT=qT[:, h, :], rhs=kblk[:, :],
                        start=True, stop=True), [qT, kblk], [ps_s[b4]])
                P.op("act", lambda e, b4=b4: e.copy(out=s_sb[:, b4 * 512:(b4 + 1) * 512], in_=ps_s[b4][:]),
                     [ps_s[b4]], segs[b4 * 4:(b4 + 1) * 4])
            if stop <= 3:
                continue
            sg = lambda j: s_sb[:, j * 128:(j + 1) * 128]
            for j in range(16):
                P.op("dve", lambda e, j=j: e.max(out=v[:, j, 0:8], in_=sg(j)), [segs[j]], [vseg[j]])
            for j in range(16):
                P.op("dve", lambda e, j=j: e.max_index(out=ix[:, j, 0:8], in_max=v[:, j, 0:8], in_values=sg(j)),
                     [segs[j], vseg[j]], [ixseg[j]])
            for j in range(16):
                P.op("dve", lambda e, j=j: e.match_replace(out=sg(j), in_to_replace=v[:, j, 0:8], in_values=sg(j),
                                                           imm_value=NEG), [segs[j], vseg[j]], [segs[j]])
            for j in range(16):
                P.op("dve", lambda e, j=j: e.max(out=v[:, j, 8:16], in_=sg(j)), [segs[j]], [vseg[j]])
            for j in range(16):
                P.op("dve", lambda e, j=j: e.max_index(out=ix[:, j, 8:16], in_max=v[:, j, 8:16], in_values=sg(j)),
                     [segs[j], vseg[j]], [ixseg[j]])
            if stop <= 4:
                continue
            P.op("dve", lambda e: e.tensor_copy(out=ixf[:], in_=ix[:]), ixseg, [ixf])
            P.op("dve", lambda e: e.tensor_tensor(
                out=fap(cand[:], [(256, 8), (16, 16), (1, 16)]),
                in0=fap(v[:], [(32, 8), (1, 16), (0, 16)]),
                in1=fap(v[:, 1, :], [(32, 8), (0, 16), (1, 16)]), op=ALU.add), vseg, cseg)
            cs = lambda h: cand[:, h, :]
            for h in range(8):
                P.op("dve", lambda e, h=h: e.max(out=ts[:, h, 0:8], in_=cs(h)), [cseg[h]], [tsseg[h]])
            for h in range(8):
                P.op("dve", lambda e, h=h: e.max_index(out=pos[:, h, 0:8], in_max=ts[:, h, 0:8], in_values=cs(h)),
                     [cseg[h], tsseg[h]], [posseg[h]])
            for h in range(8):
                P.op("dve", lambda e, h=h: e.match_replace(out=cs(h), in_to_replace=ts[:, h, 0:8], in_values=cs(h),
                                                           imm_value=NEG), [cseg[h], tsseg[h]], [cseg[h]])
            for h in range(8):
                P.op("dve", lambda e, h=h: e.max(out=ts[:, h, 8:16], in_=cs(h)), [cseg[h]], [tsseg[h]])
            for h in range(8):
                P.op("dve", lambda e, h=h: e.max_index(out=pos[:, h, 8:16], in_max=ts[:, h, 8:16], in_values=cs(h)),
                     [cseg[h], tsseg[h]], [posseg[h]])
            if stop <= 5:
                continue
            P.op("dve", lambda e: e.tensor_single_scalar(out=ku[0][:], in_=pos[:], scalar=4, op=ALU.logical_shift_right),
                 posseg, [ku[0]])
            P.op("dve", lambda e: e.tensor_single_scalar(out=ku[1][:], in_=pos[:], scalar=15, op=ALU.bitwise_and),
                 posseg, [ku[1]])
            for sd in range(2):
                P.op("dve", lambda e, sd=sd: e.tensor_copy(out=kf[sd][:], in_=ku[sd][:]), [ku[sd]], [kf[sd]])
            if stop <= 6:
                continue
            for sd in range(2):
                P.op("dve", lambda e, sd=sd: e.tensor_tensor(
                    out=eq[sd][:], in0=fap(iot[:], [(0, 128), (1, 16)]), in1=fap(kf[sd][:], [(1, 128), (0, 16)]),
                    op=ALU.is_equal), [iot, kf[sd]], [eq[sd]])
                P.op("dve", lambda e, sd=sd: e.tensor_tensor(
                    out=fap(eq[sd][:], [(256, 8), (16, 16), (1, 16)]), in0=fap(eq[sd][:], [(256, 8), (16, 16), (1, 16)]),
                    in1=fap(ixf[:, sd, :], [(32, 8), (0, 16), (1, 16)]), op=ALU.mult), [eq[sd], ixf], [eq[sd]])
                P.op("dve", lambda e, sd=sd: e.tensor_reduce(out=res[:, sd, :], in_=eq[sd][:], axis=AX.X, op=ALU.add),
                     [eq[sd]], [resv[sd]])
            if stop <= 7:
                continue
            P.op("dve", lambda e: e.tensor_tensor(out=dd[:], in0=ts[:], in1=fap(ts[:], [(16, 8), (0, 16)]),
                                                  op=ALU.subtract), tsseg, [dd])
            P.op("act", lambda e: e.activation(out=ee[:], in_=dd[:], func=AF.Exp), [dd], [ee])
            P.op("dve", lambda e: e.tensor_reduce(out=zz[:], in_=ee[:], axis=AX.X, op=ALU.add), [ee], [zz])
            P.op("dve", lambda e: e.reciprocal(out=rz[:], in_=zz[:]), [zz], [rz])
            P.op("dve", lambda e: e.tensor_tensor(out=fap(res[:, 2, :], [(16, 8), (1, 16)]), in0=ee[:],
                                                  in1=fap(rz[:], [(1, 8), (0, 16)]), op=ALU.mult), [ee, rz], [resv[2]])
            if stop <= 8:
                continue
            for a in range(3):
                P.op("pe", lambda e, a=a: e.transpose(out=ps_t[:, a, :], in_=res[:, a, :], identity=idf[:]),
                     [resv[a], idf], [ps_t])
            P.op("act", lambda e, k=k: e.copy(out=resT[k][:], in_=ps_t[:]), [ps_t], [resT[k]])
            P.dma("sp", tab_v[:, :, tile * 128:(tile + 1) * 128], resT[k][:], resT[k], False)
        P.barrier()


def phase_peer_experts(P, nc, x_d, hnT_d, tab_d, UT_d, V_d, iota_ap, out_d, gfin_ap=None, ngrp=NGRP):
    SUBT = 8
    with ExitStack() as es:
        iot = P.sbuf(es, "pe_iot", [128, 128], BF16)
        P.dma("pool", iot[:], iota_ap, iot, True)
        GS = [P.sbuf(es, "pe_GS%d" % i, [128, GRP, 128], BF16) for i in range(2)]
        NR = 6
        utr = [P.sbuf(es, "pe_ut%d" % i, [128, 8, 128], BF16) for i in range(NR)]
        vr = [P.sbuf(es, "pe_v%d" % i, [128, D], BF16) for i in range(NR)]
        hnT = [P.sbuf(es, "pe_hnT%d" % i, [128, 8, GRP], BF16) for i in range(2)]
        tab = [P.sbuf(es, "pe_tab%d" % i, [128, 3, GRP], F32) for i in range(2)]
        xt = [P.sbuf(es, "pe_xt%d" % i, [128, D], F32) for i in range(2)]
        NSET = 3
        O1 = [P.sbuf(es, "pe_O1_%d" % i, [128, SUBT, 128], BF16) for i in range(NSET)]
        O2g = [P.sbuf(es, "pe_O2g_%d" % i, [128, SUBT, 128], BF16) for i in range(NSET)]
        tabb = [P.sbuf(es, "pe_tabb%d" % i, [128, 3, GRP], BF16) for i in range(2)]
        ga = [P.sbuf(es, "pe_ga%d" % i, [128, GRP], BF16) for i in range(3)]
        pT = [P.sbuf(es, "pe_pT%d" % i, [128, GRP], BF16) for i in range(4)]
        po = [[P.psum(es, "pe_po%d%d" % (i, j), [128, 512], F32) for j in range(2)] for i in range(2)]
        pa = [P.psum(es, "pe_pa%d" % i, [128, 512], F32) for i in range(3)]
        pav = pa
        pa_ap = lambda s: pa[s][:, 0:GRP]
        pg = [P.psum(es, "pe_pg%d" % i, [128, 4, 128], F32) for i in range(1)]
        if gfin_ap is not None:
            gfin = P.sbuf(es, "pe_gfin", [128, D], F32)
            P.dma("sp", gfin[:], bcast_rows(gfin_ap, 128), gfin, True)
            S = NormScratch.__new__(NormScratch)
            S.junk = P.sbuf(es, "pe_nj", [128, D], BF16)
            S.ss = P.sbuf(es, "pe_nss", [128, 1], F32)
            S.ms = P.sbuf(es, "pe_nms", [128, 1], F32)
            S.rstd = P.sbuf(es, "pe_nrs", [128, 1], F32)
        tab_v = tab_d.rearrange("a p t -> p a t")

        def load_group(g):
            P.dma("sp", tab[g % 2][:], tab_v[:, :, g * GRP:(g + 1) * GRP], tab[g % 2], True)
            P.op("pool", lambda e: e.tensor_copy(out=tabb[g % 2][:], in_=tab[g % 2][:]), [tab[g % 2]], [tabb[g % 2]])

        def stageA(g, sb):
            tb = tabb[g % 2]
            t0 = sb * SUBT
            k = sb % NSET
            P.op("dve", lambda e: e.tensor_tensor(out=O1[k][:], in0=fap(iot[:], [(0, SUBT), (1, 128)]),
                                                  in1=fap(tb[:, 0, t0:t0 + SUBT], [(1, SUBT), (0, 128)]), op=ALU.is_equal),
                 [iot, tb], [O1[k]])
            P.op("dve", lambda e: e.tensor_tensor(out=O2g[k][:], in0=fap(iot[:], [(0, SUBT), (1, 128)]),
                                                  in1=fap(tb[:, 1, t0:t0 + SUBT], [(1, SUBT), (0, 128)]), op=ALU.is_equal),
                 [iot, tb], [O2g[k]])
            P.op("pool", lambda e: e.tensor_tensor(out=O2g[k][:], in0=O2g[k][:],
                                                   in1=fap(tb[:, 2, t0:t0 + SUBT], [(1, SUBT), (0, 128)]), op=ALU.mult),
                 [O2g[k], tb], [O2g[k]])

        def stageB(g, sb, quarters=(0, 1)):
            k = sb % NSET
            for q4 in quarters:
                pgk = pg[0]
                for tl in range(4):
                    t = q4 * 4 + tl
                    P.op("pe", lambda e, t=t, tl=tl, pgk=pgk: e.matmul(pgk[:, tl, :], lhsT=O2g[k][:, t, :], rhs=O1[k][:, t, :],
                                                                     start=True, stop=True), [O2g[k], O1[k]], [pgk])

        def stageC(g, sb, quarters=(0, 1)):
            gs = GS[g % 2]
            t0 = sb * SUBT
            for q4 in quarters:
                pgk = pg[0]
                tt = t0 + q4 * 4
                P.op("act", lambda e, pgk=pgk, tt=tt: e.copy(out=gs[:, tt:tt + 4, :], in_=pgk[:]),
                     [pgk], [gs])

        def gbuild_sub(g, sb):
            stageA(g, sb)
            for q4 in range(2):
                stageB(g, sb, (q4,))
                stageC(g, sb, (q4,))

        nsubs = GRP // SUBT
        load_group(0)
        for sb in range(nsubs):
            gbuild_sub(0, sb)
        jobs = [(g, c) for g in range(ngrp) for c in range(128)]

        def emit_pa(i):
            g, c = jobs[i]
            hg = hnT[g % 2]
            if c == 0:
                P.dma("sp", hg[:], hnT_d[g].rearrange("p (dk t) -> p dk t", t=GRP), hg, True)
                if g + 1 < ngrp:
                    load_group(g + 1)
            r = i % NR
            P.dma("sp", utr[r][:], UT_d[c].rearrange("p (dk q) -> p dk q", q=128), utr[r], True)
            P.dma("sp", vr[r][:], V_d[c * 128:(c + 1) * 128, :], vr[r], True)
            s = i % 3
            for dk in range(8):
                P.op("pe", lambda e, dk=dk, r=r, s=s: e.matmul(pa_ap(s), lhsT=utr[r][:, dk, :], rhs=hg[:, dk, :],
                                                             start=(dk == 0), stop=(dk == 7)), [utr[r], hg], [pav[s]])

        emit_pa(0)
        emit_pa(1)
        for i, (g, c) in enumerate(jobs):
            if i + 2 < len(jobs):
                emit_pa(i + 2)
            gs = GS[g % 2]
            r = i % NR
            s = i % 3
            gk = ga[i % 3]
            pk = pT[i % 4]
            P.op("act", lambda e, gk=gk, s=s: e.activation(out=gk[:], in_=pa_ap(s), func=AF.Gelu), [pav[s]], [gk])
            P.op("dve", lambda e, gk=gk, pk=pk, c=c, gs=gs: e.tensor_tensor(out=pk[:], in0=gk[:], in1=fap(gs[:, 0, c:c + 1], [(128, GRP)]),
                                                                           op=ALU.mult), [gk, gs], [pk])
            for tsb in range(2):
                for dh in range(2):
                    P.op("pe", lambda e, tsb=tsb, dh=dh, pk=pk, r=r, c=c: e.matmul(
                        po[tsb][dh][:], lhsT=pk[:, tsb * 128:(tsb + 1) * 128], rhs=vr[r][:, dh * 512:(dh + 1) * 512],
                        start=(c == 0), stop=(c == 127)), [pk, vr[r]], [po[tsb][dh]])
            if g + 1 < ngrp:
                sb = c // 4
                if c % 4 == 0:
                    if sb > 0:
                        stageC(g + 1, sb - 1, (1,))
                    stageA(g + 1, sb)
                elif c % 4 == 1:
                    stageB(g + 1, sb, (0,))
                elif c % 4 == 2:
                    stageC(g + 1, sb, (0,))
                elif c % 4 == 3:
                    stageB(g + 1, sb, (1,))
                    if c == 127:
                        stageC(g + 1, sb, (1,))
            if c != 127:
                continue
            for tsb in range(2):
                row0 = g * GRP + tsb * 128
                x_t = xt[tsb]
                P.dma("sp", x_t[:], x_d[row0:row0 + 128, :], x_t, True)
                for dh in range(2):
                    P.op("dve", lambda e, x_t=x_t, dh=dh, tsb=tsb: e.tensor_tensor(
                        out=x_t[:, dh * 512:(dh + 1) * 512], in0=x_t[:, dh * 512:(dh + 1) * 512], in1=po[tsb][dh][:],
                        op=ALU.add), [x_t, po[tsb][dh]], [x_t])
                if gfin_ap is not None:
                    emit_rstd(P, x_t, 128, S)
                    P.op("dve", lambda e, x_t=x_t: e.scalar_tensor_tensor(out=x_t[:], in0=x_t[:], scalar=S.rstd[:, 0:1],
                                                                         in1=gfin[:], op0=ALU.mult, op1=ALU.mult),
                         [x_t, S.rstd, gfin], [x_t])
                P.dma("sp", out_d[row0:row0 + 128, :], x_t[:], x_t, False)
        P.barrier()


def build_peer(final, phases=(1, 1, 1), debug=False, ntiles=NTOK // 128, stop=99):
    nc = bass.Bass("TRN2", target_bir_lowering=False)
    x = nc.dram_tensor("x", [NTOK, D], F32, kind="ExternalInput").ap()
    g = nc.dram_tensor("g", [D], F32, kind="ExternalInput").ap()
    wq = nc.dram_tensor("wq", [D, D], F32, kind="ExternalInput").ap()
    k1 = nc.dram_tensor("k1", [NKEY, 64], F32, kind="ExternalInput").ap()
    k2 = nc.dram_tensor("k2", [NKEY, 64], F32, kind="ExternalInput").ap()
    u = nc.dram_tensor("u", [NKEY * NKEY, D], F32, kind="ExternalInput").ap()
    v = nc.dram_tensor("v", [NKEY * NKEY, D], F32, kind="ExternalInput").ap()
    ident = nc.dram_tensor("ident", [128, 128], F32, kind="ExternalInput").ap()
    iota = nc.dram_tensor("iota", [128, 128], F32, kind="ExternalInput").ap()
    gfin = nc.dram_tensor("gfin", [D], F32, kind="ExternalInput").ap() if final else None
    out = nc.dram_tensor("out", [NTOK, D], F32, kind="ExternalOutput").ap()
    kd = "ExternalOutput" if debug else "Internal"
    UT_d = nc.dram_tensor("UT_d", [128, 128, D], BF16, kind=kd).ap()
    V_d = nc.dram_tensor("V_d", [NKEY * NKEY, D], BF16, kind=kd).ap()
    hnT_d = nc.dram_tensor("hnT_d", [NGRP, 128, 8 * GRP], BF16, kind=kd).ap()
    tab_d = nc.dram_tensor("tab_d", [3, 128, NTOK], F32, kind=kd).ap()
    with ExitStack() as es:
        P = Prog(nc, es)
        if phases[0]:
            phase_peer_prep(P, nc, u, v, ident, UT_d, V_d)
        if phases[1]:
            phase_peer_route(P, nc, x, g, wq, k1, k2, ident, iota, hnT_d, tab_d, ntiles, stop)
        if phases[2]:
            phase_peer_experts(P, nc, x, hnT_d, tab_d, UT_d, V_d, iota, out, gfin)
    return nc


def consts():
    return {"ident": np.eye(128, dtype=np.float32),
            "iota": np.tile(np.arange(128, dtype=np.float32)[None, :], (128, 1))}


NQT_ = NTOK // 128
NHEAD = 16
DH = 64
SEQ = 8192
NKT = SEQ // 128
NQT = NTOK // 128
NB = SEQ // BLK
BIG = 30000.0


def phase_norm_T(P, nc, x_d, g_ap, ident, hnT_own, ntiles=NQT_):
    with ExitStack() as es:
        idb = P.sbuf(es, "n_idb", [128, 128], BF16)
        gt = P.sbuf(es, "n_gt", [128, D], F32)
        P.dma("pool", idb[:], ident, idb, True)
        P.dma("sp", gt[:], bcast_rows(g_ap, 128), gt, True)
        Ss = [NormScratch(P, es, "n%d" % i) for i in range(2)]
        xts = [P.sbuf(es, "n_xt%d" % i, [128, D], F32) for i in range(2)]
        hnTs = [P.sbuf(es, "n_hnT%d" % i, [128, 8, 128], BF16) for i in range(2)]
        for tile in range(ntiles):
            k = tile % 2
            P.dma("sp", xts[k][:], x_d[tile * 128:(tile + 1) * 128, :], xts[k], True)
            emit_norm_T(P, xts[k], 128, gt, idb, hnTs[k], 0, Ss[k], evac=("act" if k else "dve"))
            P.dma("sp", hnT_own[tile].rearrange("p (dk t) -> p dk t", t=128), hnTs[k][:], hnTs[k], False)
        P.barrier()


def emit_rope(P, src, cs, dst, tmp):
    x1 = fap(src[:, 0:1], [(64, 16), (1, 32)])
    x2 = fap(src[:, 32:33], [(64, 16), (1, 32)])
    cosb = fap(cs[:, 0:1], [(0, 16), (1, 32)])
    sinb = fap(cs[:, 32:33], [(0, 16), (1, 32)])
    o1 = fap(dst[:, 0:1], [(64, 16), (1, 32)])
    o2 = fap(dst[:, 32:33], [(64, 16), (1, 32)])
    t1, t2, t3, t4 = tmp
    P.op("dve", lambda e: e.tensor_tensor(out=t1[:], in0=x1, in1=cosb, op=ALU.mult), [src, cs], [t1])
    P.op("dve", lambda e: e.tensor_tensor(out=t2[:], in0=x2, in1=sinb, op=ALU.mult), [src, cs], [t2])
    P.op("dve", lambda e: e.tensor_tensor(out=o1, in0=t1[:], in1=t2[:], op=ALU.subtract), [t1, t2], [dst])
    P.op("pool", lambda e: e.tensor_tensor(out=t3[:], in0=x2, in1=cosb, op=ALU.mult), [src, cs], [t3])
    P.op("pool", lambda e: e.tensor_tensor(out=t4[:], in0=x1, in1=sinb, op=ALU.mult), [src, cs], [t4])
    P.op("pool", lambda e: e.tensor_tensor(out=o2, in0=t3[:], in1=t4[:], op=ALU.add), [t3, t4], [dst])


def phase_attn_qkv(P, nc, hnT_all, hnT_own, w_qkv, cs_k, cs_q, ident, KT_d, V_d, QT_d):
    with ExitStack() as es:
        idb = P.sbuf(es, "aq_idb", [128, 128], BF16)
        P.dma("pool", idb[:], ident, idb, True)
        w = P.sbuf(es, "aq_w", [128, 8, 3 * D], BF16)
        w_v = w_qkv.rearrange("(dk p) f -> p dk f", p=128)
        for dk in range(8):
            P.dma("pool", w[:, dk, :], w_v[:, dk, :], w, True)
        hn = [P.sbuf(es, "aq_hn%d" % i, [128, 8, 128], BF16) for i in range(2)]
        cs = [P.sbuf(es, "aq_cs%d" % i, [128, 64], F32) for i in range(2)]
        ksb = [P.sbuf(es, "aq_ksb%d" % i, [128, D], F32) for i in range(2)]
        krot = [P.sbuf(es, "aq_krot%d" % i, [128, D], BF16) for i in range(2)]
        vsb = [P.sbuf(es, "aq_vsb%d" % i, [128, D], BF16) for i in range(2)]
        tmp = [[P.sbuf(es, "aq_t%d_%d" % (i, j), [128, 16, 32], F32) for j in range(4)] for i in range(2)]
        kTs = [P.sbuf(es, "aq_kT%d" % i, [128, 8, 128], BF16) for i in range(2)]
        pk = [P.psum(es, "aq_pk%d" % i, [128, 512], F32) for i in range(2)]
        pv = [P.psum(es, "aq_pv%d" % i, [128, 512], F32) for i in range(2)]
        pt = [P.psum(es, "aq_pt%d" % i, [128, 8, 128], BF16) for i in range(2)]
        KT_v = KT_d.rearrange("(hp two) d k -> (two d) hp k", two=2)
        QT_v = QT_d.rearrange("(hp two) d k -> (two d) hp k", two=2)

        def proj(hk, col0, ps):
            for half in range(2):
                for dk in range(8):
                    P.op("pe", lambda e, half=half, dk=dk: e.matmul(
                        ps[half][:], lhsT=hk[:, dk, :], rhs=w[:, dk, col0 + half * 512: col0 + (half + 1) * 512],
                        start=(dk == 0), stop=(dk == 7)), [hk, w], [ps[half]])

        def rope_part1(i, ps, cs_ap, scale):
            k = i % 2
            P.dma("sp", cs[k][:], cs_ap, cs[k], True)
            for half in range(2):
                P.op("act", lambda e, half=half, k=k: e.activation(out=ksb[k][:, half * 512:(half + 1) * 512], in_=ps[half][:],
                                                                func=AF.Copy, scale=scale), [ps[half]], [ksb[k]])
            emit_rope(P, ksb[k], cs[k], krot[k], tmp[k])

        def rope_part2(i, dst_v, col):
            k = i % 2
            for hp in range(8):
                P.op("pe", lambda e, hp=hp, k=k: e.transpose(out=pt[k][:, hp, :], in_=krot[k][:, hp * 128:(hp + 1) * 128],
                                                           identity=idb[:]), [krot[k], idb], [pt[k]])
            P.op("act", lambda e, k=k: e.copy(out=kTs[k][:], in_=pt[k][:]), [pt[k]], [kTs[k]])
            P.dma("sp", dst_v[:, :, col:col + 128], kTs[k][:], kTs[k], False)

        pend = None
        for T in range(NKT):
            k = T % 2
            P.dma("sp", hn[k][:], hnT_all[T].rearrange("p (dk t) -> p dk t", t=128), hn[k], True)
            proj(hn[k], D, pk)
            rope_part1(T, pk, cs_k[T], 1.0)
            proj(hn[k], 2 * D, pv)
            for half in range(2):
                P.op("dve", lambda e, half=half, k=k: e.tensor_copy(out=vsb[k][:, half * 512:(half + 1) * 512], in_=pv[half][:]),
                     [pv[half]], [vsb[k]])
            P.dma("sp", V_d[T * 128:(T + 1) * 128, :], vsb[k][:], vsb[k], False)
            if pend is not None:
                rope_part2(*pend)
            pend = (T, KT_v, T * 128)
        for t in range(NQT):
            i = NKT + t
            k = i % 2
            P.dma("sp", hn[k][:], hnT_own[t].rearrange("p (dk t) -> p dk t", t=128), hn[k], True)
            proj(hn[k], 0, pk)
            rope_part1(i, pk, cs_q[t], 0.125)
            if pend is not None:
                rope_part2(*pend)
            pend = (i, QT_v, t * 128)
        rope_part2(*pend)
        P.barrier()


def phase_attn_core(P, nc, x_d, KT_d, V_d, QT_d, E_ap, elig_ap, ownm1_ap, cm_ap, w_o, ident, out_d):
    with ExitStack() as es0:
        idb = P.sbuf(es0, "ac_idb", [128, 128], BF16)
        P.dma("pool", idb[:], ident, idb, True)
        attn = P.sbuf(es0, "ac_attn", [128, NQT, D], BF16)
        with ExitStack() as es:
            KT = [P.sbuf(es, "ac_KT%d" % i, [128, SEQ], BF16) for i in range(2)]
            QT = [P.sbuf(es, "ac_QT%d" % i, [128, NTOK], BF16) for i in range(2)]
            Vh = [P.sbuf(es, "ac_V%d" % i, [128, NKT, 130], BF16) for i in range(2)]
            elig = P.sbuf(es, "ac_elig", [128, NBLK, NB], F32)
            ownm1 = P.sbuf(es, "ac_ownm1", [128, NBLK, NB], F32)
            cm = P.sbuf(es, "ac_cm", [128, 2, BLK], BF16)
            km = [P.sbuf(es, "ac_km%d" % i, [128, NB], F32) for i in range(2)]
            kmb = [P.sbuf(es, "ac_kmb%d" % i, [128, NB], BF16) for i in range(2)]
            gm = [P.sbuf(es, "ac_gm%d" % i, [128, NB], F32) for i in range(4)]
            top8 = [P.sbuf(es, "ac_top8%d" % i, [128, 8], F32) for i in range(4)]
            thr = [P.sbuf(es, "ac_thr%d" % i, [128, 1], F32) for i in range(4)]
            sel = [P.sbuf(es, "ac_sel%d" % i, [128, NB], F32) for i in range(4)]
            NM = [P.sbuf(es, "ac_NM%d" % i, [128, 128], BF16) for i in range(4)]
            PT = [P.sbuf(es, "ac_PT%d" % i, [128, 2, BLK], BF16) for i in range(3)]
            rs = [P.sbuf(es, "ac_rs%d" % i, [128, 1], F32) for i in range(2)]
            pS = [P.psum(es, "ac_pS%d" % i, [128, 2, BLK], F32) for i in range(2)]
            pO = [[P.psum(es, "ac_pO%d%d" % (i, j), [128, 512], F32) for j in range(2)] for i in range(2)]
            pG = P.psum(es, "ac_pG", [128, 512], F32)
            pN = P.psum(es, "ac_pN", [128, 128], BF16)
            P.dma("sp", elig[:], elig_ap.rearrange("(o a) b -> o a b", o=1).to_broadcast([128, NBLK, NB]), elig, True)
            P.dma("sp", ownm1[:], ownm1_ap.rearrange("(o a) b -> o a b", o=1).to_broadcast([128, NBLK, NB]), ownm1, True)
            P.dma("pool", cm[:], cm_ap, cm, True)
            for i in range(4):
                P.op("dve", lambda e, i=i: e.memset(NM[i][:], 0.0), [], [NM[i]])
            for i in range(2):
                P.op("pool", lambda e, i=i: e.memset(KT[i][64:128, :], 0.0), [], [KT[i]])
                P.dma("pool", KT[i][64:96, :], E_ap, KT[i], True)
                P.op("pool", lambda e, i=i: e.memset(QT[i][64:128, :], 0.0), [], [QT[i]])
                P.op("dve", lambda e, i=i: e.memset(Vh[i][:, :, 0:1], 1.0), [], [Vh[i]])
                P.op("dve", lambda e, i=i: e.memset(Vh[i][:, :, 129:130], 1.0), [], [Vh[i]])
            for i in range(2):
                P.op("dve", lambda e, i=i: e.memset(km[i][:], 0.0), [], [km[i]])
            V_v = V_d.rearrange("(T p) f -> p T f", p=128)
            st = dict(npS=0, nPT=0, ngate=0)

            def load_head(h):
                hp = h // 2
                if h % 2 == 0:
                    vb = Vh[hp % 2]
                    for q4 in range(4):
                        P.dma("sp", vb[:, q4 * 16:(q4 + 1) * 16, 1:129], V_v[:, q4 * 16:(q4 + 1) * 16, hp * 128:(hp + 1) * 128],
                              vb, True)
                kt_, qt_ = KT[h % 2], QT[h % 2]
                P.dma("sp", kt_[0:64, :], KT_d[h], kt_, True)
                P.dma("sp", qt_[0:64, :], QT_d[h], qt_, True)
                P.op("dve", lambda e: e.tensor_reduce(out=km[h % 2][0:64, :], in_=fap(kt_[0:64, 0:1], [(BLK, NB), (1, BLK)]),
                                                      axis=AX.X, op=ALU.add), [kt_], [km[h % 2]])
                P.op("dve", lambda e: e.tensor_copy(out=kmb[h % 2][:], in_=km[h % 2][:]), [km[h % 2]], [kmb[h % 2]])

            def gating_a(h, qt):
                qt_ = QT[h % 2]
                jl = qt // 2
                g = qt % 4
                P.op("pe", lambda e: e.matmul(pG[:, 0:NB], lhsT=qt_[:, qt * 128:(qt + 1) * 128], rhs=kmb[h % 2][:, :],
                                              start=True, stop=True), [qt_, kmb[h % 2]], [pG])
                P.op("dve", lambda e: e.tensor_tensor(out=gm[g][:], in0=pG[:, 0:NB], in1=elig[:, jl, :], op=ALU.add),
                     [pG, elig], [gm[g]])
                P.op("dve", lambda e: e.max(out=top8[g][:], in_=gm[g][:]), [gm[g]], [top8[g]])
                P.op("dve", lambda e: e.tensor_scalar(out=thr[g][:], in0=top8[g][:, 2:3], scalar1=-1.0e29, scalar2=None,
                                                      op0=ALU.max), [top8[g]], [thr[g]])
                P.op("dve", lambda e: e.tensor_scalar(out=sel[g][:], in0=gm[g][:], scalar1=thr[g][:, 0:1], scalar2=None,
                                                      op0=ALU.is_ge), [gm[g], thr[g]], [sel[g]])
                P.op("dve", lambda e: e.scalar_tensor_tensor(out=NM[g][:, 64:96], in0=sel[g][:], scalar=BIG,
                                                             in1=ownm1[:, jl, :], op0=ALU.mult, op1=ALU.add),
                     [sel[g], ownm1], [NM[g]])

            def gating_b(h, qt):
                qt_ = QT[h % 2]
                g = qt % 4
                P.op("pe", lambda e: e.transpose(out=pN[:], in_=NM[g][:], identity=idb[:]), [NM[g], idb], [pN])
                P.op("act", lambda e: e.copy(out=qt_[64:96, qt * 128:(qt + 1) * 128], in_=pN[64:96, :]), [pN], [qt_])

            def attention(h, jl):
                hb = h % 2
                hp = h // 2
                kt_, qt_ = KT[h % 2], QT[h % 2]
                vb = Vh[hp % 2]
                po = pO[jl % 2]
                blocks = [rr * 16 + m for m in range(jl + 1) for rr in range(2)]
                nb_ = len(blocks)
                def qk(bi):
                    b = blocks[bi]
                    ps = pS[(st["npS"] + bi) % 2]
                    for kt2 in range(2):
                        T = b * 2 + kt2
                        P.op("pe", lambda e, kt2=kt2, T=T: e.matmul(
                            ps[:, kt2, :], lhsT=kt_[:, T * 128:(T + 1) * 128], rhs=qt_[:, jl * BLK:(jl + 1) * BLK],
                            start=True, stop=True), [kt_, qt_], [ps])

                qk(0)
                for bi, b in enumerate(blocks):
                    if bi + 1 < nb_:
                        qk(bi + 1)
                    ps = pS[(st["npS"] + bi) % 2]
                    pt_ = PT[st["nPT"] % 3]
                    st["nPT"] += 1
                    P.op("act", lambda e: e.activation(out=pt_[:], in_=ps[:], func=AF.Exp), [ps], [pt_])
                    if b == jl:
                        P.op("pool", lambda e: e.tensor_tensor(out=pt_[:], in0=pt_[:], in1=cm[:, :, :], op=ALU.mult),
                             [pt_, cm], [pt_])
                    for kt2 in range(2):
                        T = b * 2 + kt2
                        for qs in range(2):
                            P.op("pe", lambda e, kt2=kt2, qs=qs, T=T: e.matmul(
                                po[qs][:, 0:65], lhsT=pt_[:, kt2, qs * 128:(qs + 1) * 128],
                                rhs=vb[:, T, hb * 65:hb * 65 + 65],
                                start=(bi == 0 and kt2 == 0), stop=(bi == nb_ - 1 and kt2 == 1)), [pt_, vb], [po[qs]])
                st["npS"] += nb_
                for qs in range(2):
                    sc = 0 if hb == 0 else 64
                    oc = 1 if hb == 0 else 0
                    P.op("dve", lambda e, qs=qs: e.reciprocal(out=rs[qs][:], in_=po[qs][:, sc:sc + 1]), [po[qs]], [rs[qs]])
                    P.op("dve", lambda e, qs=qs: e.tensor_scalar(
                        out=attn[:, jl * 2 + qs, h * 64:(h + 1) * 64], in0=po[qs][:, oc:oc + 64], scalar1=rs[qs][:, 0:1],
                        scalar2=None, op0=ALU.mult), [po[qs], rs[qs]], [attn])

            load_head(0)
            for qt in range(NQT):
                gating_a(0, qt)
                gating_b(0, qt)
            for h in range(NHEAD):
                if h + 1 < NHEAD:
                    load_head(h + 1)
                for jl in range(NBLK):
                    attention(h, jl)
                    if h + 1 < NHEAD:
                        gating_a(h + 1, 2 * jl)
                        gating_a(h + 1, 2 * jl + 1)
                        if jl > 0:
                            gating_b(h + 1, 2 * jl - 2)
                            gating_b(h + 1, 2 * jl - 1)
                if h + 1 < NHEAD:
                    gating_b(h + 1, NQT - 2)
                    gating_b(h + 1, NQT - 1)
        P.barrier()
        with ExitStack() as es:
            wo = P.sbuf(es, "ao_wo", [128, 8, D], BF16)
            w_o_v = w_o.rearrange("(dk p) f -> p dk f", p=128)
            for dk in range(0, 8, 4):
                P.dma("pool", wo[:, dk:dk + 4, :], w_o_v[:, dk:dk + 4, :], wo, True)
            aT = [P.sbuf(es, "ao_aT%d" % i, [128, 8, 128], BF16) for i in range(2)]
            xt = [P.sbuf(es, "ao_xt%d" % i, [128, D], F32) for i in range(2)]
            pt = [P.psum(es, "ao_pt%d" % i, [128, 8, 128], BF16) for i in range(2)]
            py = [P.psum(es, "ao_py%d" % i, [128, 512], F32) for i in range(2)]
            npy = 0
            for qt in range(NQT):
                k = qt % 2
                P.dma("sp", xt[k][:], x_d[qt * 128:(qt + 1) * 128, :], xt[k], True)
                for fk in range(8):
                    P.op("pe", lambda e, fk=fk, k=k, qt=qt: e.transpose(out=pt[k][:, fk, :], in_=attn[:, qt, fk * 128:(fk + 1) * 128],
                                                                     identity=idb[:]), [attn, idb], [pt[k]])
                P.op("act", lambda e, k=k: e.copy(out=aT[k][:], in_=pt[k][:]), [pt[k]], [aT[k]])
                for fh in range(2):
                    pyk = py[npy % 2]
                    npy += 1
                    for fk in range(8):
                        P.op("pe", lambda e, fk=fk, k=k, fh=fh, pyk=pyk: e.matmul(pyk[:], lhsT=aT[k][:, fk, :], rhs=wo[:, fk, fh * 512:(fh + 1) * 512],
                                                                                start=(fk == 0), stop=(fk == 7)), [aT[k], wo], [pyk])
                    P.op("dve", lambda e, k=k, fh=fh, pyk=pyk: e.tensor_tensor(out=xt[k][:, fh * 512:(fh + 1) * 512], in0=xt[k][:, fh * 512:(fh + 1) * 512],
                                                                              in1=pyk[:], op=ALU.add), [xt[k], pyk], [xt[k]])
                P.dma("sp", out_d[qt * 128:(qt + 1) * 128, :], xt[k][:], xt[k], False)
        P.barrier()


def attn_tables(core):
    r = core % 2
    half = DH // 2
    inv = (np.float32(10000.0) ** (-np.arange(half, dtype=np.float32) / np.float32(half))).astype(np.float32)

    def cs_for(rr, lt):
        j = 2 * (lt // 2) + rr
        pos = (j * BLK + (lt % 2) * 128 + np.arange(128)).astype(np.float32)
        ang = (pos[:, None] * inv[None, :]).astype(np.float32)
        return np.concatenate([np.cos(ang), np.sin(ang)], axis=1).astype(np.float32)

    rank_of = lambda T: r if T < NQT else 1 - r
    cs_k = np.stack([cs_for(rank_of(T), T % NQT) for T in range(NKT)])
    cs_q = np.stack([cs_for(r, t) for t in range(NQT)])
    nglob = np.array([2 * (b % 16) + (r if b < 16 else 1 - r) for b in range(NB)])
    E = np.zeros((NB, SEQ), np.float32)
    for b in range(NB):
        E[b, b * BLK:(b + 1) * BLK] = 1.0
    elig = np.zeros((NBLK, NB), np.float32)
    ownm1 = np.zeros((NBLK, NB), np.float32)
    for jl in range(NBLK):
        j = 2 * jl + r
        elig[jl] = np.where(nglob < j, 0.0, -1.0e30)
        ownm1[jl] = np.where(nglob == j, 0.0, -BIG)
    tri = np.zeros((128, 2, BLK), np.float32)
    for kt in range(2):
        kp = kt * 128 + np.arange(128)
        tri[:, kt, :] = (kp[:, None] <= np.arange(BLK)[None, :]).astype(np.float32)
    return {"cs_k": cs_k, "cs_q": cs_q, "E": E, "elig": elig, "ownm1": ownm1, "cm": np.ascontiguousarray(tri)}


def build_norm():
    nc = bass.Bass("TRN2", target_bir_lowering=False)
    x = nc.dram_tensor("x", [NTOK, D], F32, kind="ExternalInput").ap()
    g = nc.dram_tensor("g", [D], F32, kind="ExternalInput").ap()
    ident = nc.dram_tensor("ident", [128, 128], F32, kind="ExternalInput").ap()
    hnT_own = nc.dram_tensor("hnT_own", [NQT, 128, D], BF16, kind="ExternalOutput").ap()
    with ExitStack() as es:
        P = Prog(nc, es)
        phase_norm_T(P, nc, x, g, ident, hnT_own)
    return nc


def attn_decls(nc):
    d = {}
    d["cs_k"] = nc.dram_tensor("cs_k", [NKT, 128, 64], F32, kind="ExternalInput").ap()
    d["cs_q"] = nc.dram_tensor("cs_q", [NQT, 128, 64], F32, kind="ExternalInput").ap()
    d["E"] = nc.dram_tensor("E", [NB, SEQ], F32, kind="ExternalInput").ap()
    d["elig"] = nc.dram_tensor("elig", [NBLK, NB], F32, kind="ExternalInput").ap()
    d["ownm1"] = nc.dram_tensor("ownm1", [NBLK, NB], F32, kind="ExternalInput").ap()
    d["cm"] = nc.dram_tensor("cm", [128, 2, BLK], F32, kind="ExternalInput").ap()
    return d


def build_attn(debug=False):
    nc = bass.Bass("TRN2", target_bir_lowering=False)
    x = nc.dram_tensor("x", [NTOK, D], F32, kind="ExternalInput").ap()
    hnT_all = nc.dram_tensor("hnT_all", [NKT, 128, D], BF16, kind="ExternalInput").ap()
    hnT_own = nc.dram_tensor("hnT_own", [NQT, 128, D], BF16, kind="ExternalInput").ap()
    w_qkv = nc.dram_tensor("w_qkv", [D, 3 * D], F32, kind="ExternalInput").ap()
    w_o = nc.dram_tensor("w_o", [D, D], F32, kind="ExternalInput").ap()
    ident = nc.dram_tensor("ident", [128, 128], F32, kind="ExternalInput").ap()
    t = attn_decls(nc)
    out = nc.dram_tensor("out", [NTOK, D], F32, kind="ExternalOutput").ap()
    kd = "ExternalOutput" if debug else "Internal"
    KT_d = nc.dram_tensor("KT_d", [NHEAD, DH, SEQ], BF16, kind=kd).ap()
    V_d = nc.dram_tensor("V_d", [SEQ, D], BF16, kind=kd).ap()
    QT_d = nc.dram_tensor("QT_d", [NHEAD, DH, NTOK], BF16, kind=kd).ap()
    with ExitStack() as es:
        P = Prog(nc, es)
        phase_attn_qkv(P, nc, hnT_all, hnT_own, w_qkv, t["cs_k"], t["cs_q"], ident, KT_d, V_d, QT_d)
        phase_attn_core(P, nc, x, KT_d, V_d, QT_d, t["E"], t["elig"], t["ownm1"], t["cm"], w_o, ident, out)
    return nc


def build_conv():
    nc = bass.Bass("TRN2", target_bir_lowering=False)
    x0 = nc.dram_tensor("x0", [NTOK, D], F32, kind="ExternalInput").ap()
    xh = nc.dram_tensor("xh", [NBLK * 2, D], F32, kind="ExternalInput").ap()
    g = nc.dram_tensor("g", [D], F32, kind="ExternalInput").ap()
    w_in = nc.dram_tensor("w_in", [D, 3 * D], F32, kind="ExternalInput").ap()
    conv_w = nc.dram_tensor("conv_w", [3, D], F32, kind="ExternalInput").ap()
    w_out = nc.dram_tensor("w_out", [D, D], F32, kind="ExternalInput").ap()
    ident = nc.dram_tensor("ident", [128, 128], F32, kind="ExternalInput").ap()
    x1 = nc.dram_tensor("x1", [NTOK, D], F32, kind="ExternalOutput").ap()
    with ExitStack() as es:
        P = Prog(nc, es)
        phase_conv(P, nc, x0, xh, g, w_in, conv_w, w_out, ident, x1)
    return nc


_NC_CACHE = {}
NALL = 2 * NTOK


def build_fused():
    nc = bass.Bass("TRN2", target_bir_lowering=False)
    I = lambda name, shape, dt=F32: nc.dram_tensor(name, list(shape), dt, kind="ExternalInput").ap()
    S = lambda name, shape, dt=F32: nc.dram_tensor(name, list(shape), dt, kind="Internal").ap()
    x_all = I("x_all", [NALL, D])
    xh_all = I("xh_all", [2 * NBLK * 2, D])
    norm_mix = I("norm_mix", [2, D])
    norm_ffn = I("norm_ffn", [2, D])
    norm_final = I("norm_final", [D])
    conv_w_in = I("conv_w_in", [D, 3 * D])
    conv_w = I("conv_w", [3, D])
    conv_w_out = I("conv_w_out", [D, D])
    w_qkv = I("w_qkv", [D, 3 * D])
    w_o = I("w_o", [D, D])
    wq = [I("wq%d" % l, [D, D]) for l in range(2)]
    k1 = [I("k1_%d" % l, [NKEY, 64]) for l in range(2)]
    k2 = [I("k2_%d" % l, [NKEY, 64]) for l in range(2)]
    u = [I("u%d" % l, [NKEY * NKEY, D]) for l in range(2)]
    v = [I("v%d" % l, [NKEY * NKEY, D]) for l in range(2)]
    ident = I("ident", [128, 128])
    iota = I("iota", [128, 128])
    t = attn_decls(nc)
    out = nc.dram_tensor("out", [NTOK, D], F32, kind="ExternalOutput").ap()
    x1 = S("x1", [NALL, D])
    x2 = S("x2", [NALL, D])
    x3 = S("x3", [NTOK, D])
    hnT_all = S("hnT_all", [NKT, 128, D], BF16)
    KT_d = S("KT_d", [NHEAD, DH, SEQ], BF16)
    V_d = S("V_d", [SEQ, D], BF16)
    QT_d = S("QT_d", [NHEAD, DH, NTOK], BF16)
    UT = [S("UT%d" % l, [128, 128, D], BF16) for l in range(2)]
    VV = [S("VV%d" % l, [NKEY * NKEY, D], BF16) for l in range(2)]
    hnT_d = S("hnT_d", [NALL // GRP, 128, 8 * GRP], BF16)
    tab_d = S("tab_d", [3, 128, NALL])
    with ExitStack() as es:
        P = Prog(nc, es)
        phase_conv(P, nc, x_all, xh_all, norm_mix[0], conv_w_in, conv_w, conv_w_out, ident, x1, nblk=2 * NBLK,
                   side_jobs=[(u[l], v[l], UT[l], VV[l]) for l in range(2)])
        phase_peer_route(P, nc, x1, norm_ffn[0], wq[0], k1[0], k2[0], ident, iota, hnT_d, tab_d, NALL // 128)
        phase_peer_experts(P, nc, x1, hnT_d, tab_d, UT[0], VV[0], iota, x2, None, NALL // GRP)
        phase_norm_T(P, nc, x2, norm_mix[1], ident, hnT_all, NKT)
        phase_attn_qkv(P, nc, hnT_all, hnT_all, w_qkv, t["cs_k"], t["cs_q"], ident, KT_d, V_d, QT_d)
        phase_attn_core(P, nc, x2, KT_d, V_d, QT_d, t["E"], t["elig"], t["ownm1"], t["cm"], w_o, ident, x3)
        phase_peer_route(P, nc, x3, norm_ffn[1], wq[1], k1[1], k2[1], ident, iota, hnT_d, tab_d, NTOK // 128)
        phase_peer_experts(P, nc, x3, hnT_d, tab_d, UT[1], VV[1], iota, out, norm_final, NTOK // GRP)
        P.wait_all_dma("sp")
    return nc


def kernel(x, norm_mix, norm_ffn, conv_w_in, conv_w, conv_w_out, attn_w_qkv, attn_w_o,
           peer_w_q, peer_k1, peer_k2, peer_u, peer_v, norm_final):
    from concourse.bass_utils import run_bass_kernel_spmd
    f = lambda a: np.ascontiguousarray(np.asarray(a, dtype=np.float32))
    x = f(x)
    B, S, _ = x.shape
    cores = list(range(8))
    if "fused" not in _NC_CACHE:
        _NC_CACHE["fused"] = build_fused()
    shared = {"norm_mix": f(norm_mix), "norm_ffn": f(norm_ffn), "norm_final": f(norm_final),
              "conv_w_in": f(conv_w_in[0]), "conv_w": f(conv_w[0]), "conv_w_out": f(conv_w_out[0]),
              "w_qkv": f(attn_w_qkv[0]), "w_o": f(attn_w_o[0])}
    for l in range(2):
        shared["wq%d" % l] = f(peer_w_q[l])
        shared["k1_%d" % l] = f(peer_k1[l])
        shared["k2_%d" % l] = f(peer_k2[l])
        shared["u%d" % l] = f(peer_u[l])
        shared["v%d" % l] = f(peer_v[l])
    shared.update(consts())
    in_maps = []
    for c in cores:
        xs, xh = shard_tokens(x, c)
        xo, xoh = shard_tokens(x, c ^ 1)
        m = dict(shared)
        m["x_all"] = np.concatenate([xs, xo], axis=0)
        m["xh_all"] = np.concatenate([xh, xoh], axis=0)
        m.update(attn_tables(c))
        in_maps.append(m)
    res = run_bass_kernel_spmd(_NC_CACHE["fused"], in_maps, core_ids=cores)
    outs = [r["out"] for r in res.results]
    return unshard_tokens(outs, B, S)
```

```python
import numpy as np
import concourse.bass as bass
import concourse.mybir as mybir
from contextlib import ExitStack

F32 = mybir.dt.float32
BF16 = mybir.dt.bfloat16
I32 = mybir.dt.int32
U32 = mybir.dt.uint32
ALU = mybir.AluOpType
AF = mybir.ActivationFunctionType
AX = mybir.AxisListType

ENGS = ("pe", "act", "dve", "pool", "sp")


class Buf:
    def __init__(self, prog, t, name):
        self.p = prog
        self.t = t
        self.name = name
        self.last_w = None
        self.readers = {}
        self.dsem = None
        self.dcount = 0

    def __getitem__(self, idx):
        return self.t[idx]


class Prog:
    def __init__(self, nc, es, direct=True):
        self.nc = nc
        self.es = es
        self.direct = direct
        self.eng = {"pe": nc.tensor, "act": nc.scalar, "dve": nc.vector, "pool": nc.gpsimd, "sp": nc.sync}
        self.streams = {e: [] for e in ENGS}
        self.count = {e: 0 for e in ENGS}
        self.known = {e: {} for e in ENGS}
        self.sems = {}
        for e in ENGS:
            self.sems["E_" + e] = es.enter_context(nc.semaphore("E_" + e))
        self.free_dsems = []
        self.ndsem = 0
        self.dsem_count = {}
        self.all_bufs = []

    def sbuf(self, es, name, shape, dt):
        self.uid = getattr(self, "uid", 0) + 1
        t = es.enter_context(self.nc.sbuf_tensor("%s_%d" % (name, self.uid), list(shape), dt))
        b = Buf(self, t, name)
        return b

    def psum(self, es, name, shape, dt):
        self.uid = getattr(self, "uid", 0) + 1
        t = es.enter_context(self.nc.psum_tensor("%s_%d" % (name, self.uid), list(shape), dt))
        b = Buf(self, t, name)
        return b

    def _get_dsem(self, b):
        if b.dsem is None:
            if self.free_dsems:
                k = self.free_dsems.pop()
            else:
                k = "D_%d" % self.ndsem
                self.ndsem += 1
                self.sems[k] = self.es.enter_context(self.nc.semaphore(k))
                self.dsem_count[k] = 0
            b.dsem = k
            b.dcount = self.dsem_count[k]
            self.all_bufs.append(b)
        return b.dsem

    def release(self, bufs):
        for b in bufs:
            if b.dsem is not None:
                self.dsem_count[b.dsem] = b.dcount
                self.free_dsems.append(b.dsem)
                b.dsem = None

    def _collect(self, eng, reads, writes):
        deps = {}

        def add(ev):
            if ev is None:
                return
            k, v = ev
            if deps.get(k, 0) < v:
                deps[k] = v

        for b in reads:
            add(b.last_w)
        for b in writes:
            add(b.last_w)
            for k, v in b.readers.items():
                add((k, v))
        waits = []
        for k, v in deps.items():
            if eng == "pe" and k == "E_pe":
                continue
            if self.known[eng].get(k, 0) < v:
                self.known[eng][k] = v
                waits.append((k, v))
        return waits

    def op(self, eng, fn, reads=(), writes=()):
        waits = self._collect(eng, reads, writes)
        self.count[eng] += 1
        ev = ("E_" + eng, self.count[eng])
        self._push(eng, (waits, fn, ("E_" + eng, 1)))
        for b in reads:
            b.readers[ev[0]] = ev[1]
        for b in writes:
            b.last_w = ev
            b.readers = {}
        return ev

    def dma(self, q, out, in_, buf, load, **kw):
        if load:
            waits = self._collect(q, (), (buf,))
        else:
            waits = self._collect(q, (buf,), ())
        k = self._get_dsem(buf)
        buf.dcount += 16
        ev = (k, buf.dcount)
        self._push(q, (waits, lambda e: e.dma_start(out=out, in_=in_, **kw), (k, 16)))
        if load:
            buf.last_w = ev
            buf.readers = {}
        else:
            buf.readers[k] = buf.dcount
        return ev

    def dma2(self, q, out, in_, dst_buf, src_buf, **kw):
        raise NotImplementedError

    def barrier(self):
        waits = []
        for b in self.all_bufs:
            if b.dsem is not None and self.known["sp"].get(b.dsem, 0) < b.dcount:
                self.known["sp"][b.dsem] = b.dcount
                waits.append((b.dsem, b.dcount))
        for e in ENGS:
            if e == "sp":
                continue
            k = "E_" + e
            if self.known["sp"].get(k, 0) < self.count[e]:
                self.known["sp"][k] = self.count[e]
                waits.append((k, self.count[e]))
        self.count["sp"] += 1
        v = self.count["sp"]
        self._push("sp", (waits, lambda e, s=self.sems["E_sp"]: e.sem_inc(s, 1), None))
        for e in ENGS:
            if e == "sp":
                continue
            self.known[e]["E_sp"] = v
            self._push(e, ([("E_sp", v)], None, None))
            for k2, v2 in self.known["sp"].items():
                if self.known[e].get(k2, 0) < v2:
                    self.known[e][k2] = v2
        self.release(self.all_bufs)
        self.all_bufs = [b for b in self.all_bufs if False]

    def wait_all_dma(self, eng="sp"):
        waits = []
        for b in self.all_bufs:
            if b.dsem is not None and self.known[eng].get(b.dsem, 0) < b.dcount:
                self.known[eng][b.dsem] = b.dcount
                waits.append((b.dsem, b.dcount))
        self._push(eng, (waits, None, None))

    def _push(self, e, item):
        if not self.direct:
            self.streams[e].append(item)
            return
        waits, fn, inc = item
        eng = self.eng[e]
        for k, v in waits:
            eng.wait_ge(self.sems[k], v)
        if fn is not None:
            ins = fn(eng)
            if inc is not None:
                ins.then_inc(self.sems[inc[0]], inc[1])

    def emit(self):
        if self.direct:
            return
        nc = self.nc
        engmap = {"pe": "tensor", "act": "scalar", "dve": "vector", "pool": "gpsimd", "sp": "sync"}
        with nc.Block() as block:
            for e in ENGS:
                stream = self.streams[e]

                def body(eng, stream=stream):
                    for waits, fn, inc in stream:
                        for k, v in waits:
                            eng.wait_ge(self.sems[k], v)
                        if fn is not None:
                            ins = fn(eng)
                            if inc is not None:
                                ins.then_inc(self.sems[inc[0]], inc[1])

                getattr(block, engmap[e])(body)


D = 1024
NTOK = 4096
BLK = 256
NBLK = NTOK // BLK
EPS = 1e-6


def bcast_rows(ap1d, nparts):
    n = ap1d.shape[0]
    return ap1d.rearrange("(o n) -> o n", o=1).to_broadcast([nparts, n])


class NormScratch:
    def __init__(self, P, es, tag):
        self.junk = P.sbuf(es, "nj" + tag, [128, D], BF16)
        self.ss = P.sbuf(es, "nss" + tag, [128, 1], F32)
        self.ms = P.sbuf(es, "nms" + tag, [128, 1], F32)
        self.rstd = P.sbuf(es, "nrs" + tag, [128, 1], F32)
        self.hn = P.sbuf(es, "nhn" + tag, [128, D], BF16)
        self.pt = P.psum(es, "npt" + tag, [128, 8, 128], BF16)


def emit_rstd(P, xt, rows, S):
    P.op("act", lambda e: e.activation(out=S.junk[0:rows, :], in_=xt[0:rows, :], func=AF.Square,
                                       accum_out=S.ss[0:rows, :]), [xt], [S.junk, S.ss])
    P.op("dve", lambda e: e.tensor_scalar(out=S.ms[0:rows, :], in0=S.ss[0:rows, :], scalar1=1.0 / D, scalar2=EPS,
                                          op0=ALU.mult, op1=ALU.add), [S.ss], [S.ms])
    P.op("act", lambda e: e.activation(out=S.ms[0:rows, :], in_=S.ms[0:rows, :], func=AF.Sqrt), [S.ms], [S.ms])
    P.op("dve", lambda e: e.reciprocal(out=S.rstd[0:rows, :], in_=S.ms[0:rows, :]), [S.ms], [S.rstd])


def emit_norm_T(P, xt, rows, gt, idb, hnT, col0, S, evac="act"):
    emit_rstd(P, xt, rows, S)
    P.op("dve", lambda e: e.scalar_tensor_tensor(out=S.hn[0:rows, :], in0=xt[0:rows, :], scalar=S.rstd[0:rows, 0:1],
                                                 in1=gt[0:rows, :], op0=ALU.mult, op1=ALU.mult),
         [xt, S.rstd, gt], [S.hn])
    for dk in range(8):
        P.op("pe", lambda e, dk=dk: e.transpose(out=S.pt[:, dk, 0:rows], in_=S.hn[0:rows, dk * 128:(dk + 1) * 128],
                                                identity=idb[0:rows, 0:rows]), [S.hn, idb], [S.pt])
    if evac == "act":
        P.op("act", lambda e: e.copy(out=hnT[:, :, col0:col0 + rows], in_=S.pt[:, :, 0:rows]), [S.pt], [hnT])
    else:
        P.op("dve", lambda e: e.tensor_copy(out=hnT[:, :, col0:col0 + rows], in_=S.pt[:, :, 0:rows]), [S.pt], [hnT])


def phase_conv(P, nc, x0, xh, g_ap, w_in, conv_w, w_out, ident, x1, nblk=NBLK, side_jobs=None):
    with ExitStack() as es:
        side = prep_gen(P, es, side_jobs, ident) if side_jobs else None
        side_steps = (128 * len(side_jobs) + nblk - 1) // nblk if side_jobs else 0
        idb = P.sbuf(es, "c_idb", [128, 128], BF16)
        gt = P.sbuf(es, "c_gt", [128, D], F32)
        win = P.sbuf(es, "c_win", [128, 8, 3 * D], BF16)
        wout = P.sbuf(es, "c_wout", [128, 8, D], BF16)
        cw = P.sbuf(es, "c_cw", [128, 8, 3], F32)
        P.dma("pool", idb[:], ident, idb, True)
        P.dma("sp", gt[:], bcast_rows(g_ap, 128), gt, True)
        w_in_v = w_in.rearrange("(dk p) f -> p dk f", p=128)
        for dk in range(8):
            P.dma("pool", win[:, dk, :], w_in_v[:, dk, :], win, True)
        w_out_v = w_out.rearrange("(dk p) f -> p dk f", p=128)
        for dk in range(0, 8, 4):
            P.dma("pool", wout[:, dk:dk + 4, :], w_out_v[:, dk:dk + 4, :], wout, True)
        for kk in range(3):
            P.dma("sp", cw[:, :, kk], conv_w[kk, :].rearrange("(fc p) -> p fc", p=128), cw, True,
                  allow_slow_non_contiguous=True)
        sets = []
        for s in range(2):
            t = str(s)
            st = dict(
                xt=[P.sbuf(es, "c_xt%d_%s" % (i, t), [128, D], F32) for i in range(2)],
                xh=P.sbuf(es, "c_xh" + t, [2, D], F32),
                hnT=P.sbuf(es, "c_hnT" + t, [128, 8, BLK + 2], BF16),
                gT=P.sbuf(es, "c_gT" + t, [128, 8, BLK], BF16),
                S=NormScratch(P, es, "c" + t),
            )
            sets.append(st)
        csb = [P.sbuf(es, "c_csb%d" % i, [128, BLK + 2], F32) for i in range(2)]
        u = [P.sbuf(es, "c_u%d" % i, [128, BLK + 2], F32) for i in range(2)]
        acc = [P.sbuf(es, "c_acc%d" % i, [128, BLK], F32) for i in range(2)]
        pb = [P.psum(es, "c_pb%d" % i, [128, 512], F32) for i in range(3)]
        py = [P.psum(es, "c_py%d" % i, [128, 512], F32) for i in range(2)]
        N = BLK + 2
        nbuf = 0
        npy = 0
        for blk in range(nblk):
            st = sets[blk % 2]
            S = st["S"]
            hnT = st["hnT"]
            gT = st["gT"]
            P.dma("sp", st["xh"][:], xh[blk * 2:blk * 2 + 2, :], st["xh"], True)
            for i in range(2):
                P.dma("sp", st["xt"][i][:], x0[blk * BLK + i * 128: blk * BLK + (i + 1) * 128, :], st["xt"][i], True)
            emit_norm_T(P, st["xh"], 2, gt, idb, hnT, 0, S)
            for i in range(2):
                emit_norm_T(P, st["xt"][i], 128, gt, idb, hnT, 2 + i * 128, S)
            for fc in range(8):
                for j in range(3):
                    col = j * D + fc * 128
                    for dk in range(8):
                        P.op("pe", lambda e, j=j, dk=dk, col=col: e.matmul(
                            pb[j][:, 0:N], lhsT=win[:, dk, col:col + 128], rhs=hnT[:, dk, 0:N],
                            start=(dk == 0), stop=(dk == 7)), [win, hnT], [pb[j]])
                k = nbuf % 2
                nbuf += 1
                P.op("act", lambda e, k=k: e.copy(out=csb[k][:, 0:N], in_=pb[1][:, 0:N]), [pb[1]], [csb[k]])
                P.op("dve", lambda e, k=k: e.tensor_tensor(out=u[k][:, 0:N], in0=csb[k][:, 0:N], in1=pb[2][:, 0:N],
                                                           op=ALU.mult), [csb[k], pb[2]], [u[k]])
                P.op("dve", lambda e, k=k, fc=fc: e.tensor_scalar(out=acc[k][:], in0=u[k][:, 0:BLK],
                                                                   scalar1=cw[:, fc, 0:1], scalar2=None, op0=ALU.mult),
                     [u[k], cw], [acc[k]])
                for kk in (1, 2):
                    P.op("dve", lambda e, k=k, fc=fc, kk=kk: e.scalar_tensor_tensor(
                        out=acc[k][:], in0=u[k][:, kk:kk + BLK], scalar=cw[:, fc, kk:kk + 1], in1=acc[k][:],
                        op0=ALU.mult, op1=ALU.add), [u[k], cw, acc[k]], [acc[k]])
                P.op("dve", lambda e, k=k, fc=fc: e.tensor_tensor(out=gT[:, fc, :], in0=acc[k][:], in1=pb[0][:, 2:N],
                                                                   op=ALU.mult), [acc[k], pb[0]], [gT])
            for _ in range(side_steps):
                next(side, None)
            for i in range(2):
                xt = st["xt"][i]
                for fh in range(2):
                    pyk = py[npy % 2]
                    npy += 1
                    for fc in range(8):
                        P.op("pe", lambda e, fc=fc, i=i, fh=fh, pyk=pyk: e.matmul(
                            pyk[:, :], lhsT=gT[:, fc, i * 128:(i + 1) * 128], rhs=wout[:, fc, fh * 512:(fh + 1) * 512],
                            start=(fc == 0), stop=(fc == 7)), [gT, wout], [pyk])
                    P.op("dve", lambda e, xt=xt, fh=fh, pyk=pyk: e.tensor_tensor(
                        out=xt[:, fh * 512:(fh + 1) * 512], in0=xt[:, fh * 512:(fh + 1) * 512], in1=pyk[:, :],
                        op=ALU.add), [xt, pyk], [xt])
                P.dma("sp", x1[blk * BLK + i * 128: blk * BLK + (i + 1) * 128, :], xt[:], xt, False)
        P.barrier()


def block_of(core, jl):
    return core // 2, 2 * jl + (core % 2)


def shard_tokens(x, core):
    b = core // 2
    xs = np.empty((NTOK, x.shape[2]), x.dtype)
    xh = np.zeros((NBLK * 2, x.shape[2]), x.dtype)
    for jl in range(NBLK):
        _, j = block_of(core, jl)
        xs[jl * BLK:(jl + 1) * BLK] = x[b, j * BLK:(j + 1) * BLK]
        if j > 0:
            xh[jl * 2:jl * 2 + 2] = x[b, j * BLK - 2:j * BLK]
    return xs, xh


def unshard_tokens(outs, B, S):
    full = np.empty((B, S, outs[0].shape[1]), outs[0].dtype)
    for core, o in enumerate(outs):
        for jl in range(NBLK):
            b, j = block_of(core, jl)
            full[b, j * BLK:(j + 1) * BLK] = o[jl * BLK:(jl + 1) * BLK]
    return full


NH = 8
NKEY = 128
TOPK = 16
GRP = 256
NGRP = NTOK // GRP
NEG = -1.0e30


def fap(a, dims):
    return bass.AP(a.tensor, a.offset, [list(a.ap[0])] + [list(d) for d in dims])


def view(P, buf, name):
    b = Buf(P, buf.t, name)
    return b


def prep_gen(P, es, jobs, ident):
    idb = P.sbuf(es, "pg_idb", [128, 128], BF16)
    P.dma("pool", idb[:], ident, idb, True)
    ub = [P.sbuf(es, "pg_ub%d" % i, [128, D], BF16) for i in range(3)]
    ut = [P.sbuf(es, "pg_ut%d" % i, [128, 8, 128], BF16) for i in range(3)]
    vb = [P.sbuf(es, "pg_vb%d" % i, [128, D], BF16) for i in range(3)]
    pt = P.psum(es, "pg_pt", [128, 8, 128], BF16)
    n = 0
    for (u_ap, v_ap, UT_d, V_d) in jobs:
        for c in range(128):
            k = n % 3
            n += 1
            P.dma("pool", ub[k][:], u_ap[c * 128:(c + 1) * 128, :], ub[k], True)
            for dk in range(8):
                P.op("pe", lambda e, dk=dk, k=k: e.transpose(out=pt[:, dk, :], in_=ub[k][:, dk * 128:(dk + 1) * 128],
                                                           identity=idb[:]), [ub[k], idb], [pt])
            P.op("act", lambda e, k=k: e.copy(out=ut[k][:], in_=pt[:]), [pt], [ut[k]])
            P.dma("sp", UT_d[c].rearrange("p (dk q) -> p dk q", q=128), ut[k][:], ut[k], False)
            P.dma("pool", vb[k][:], v_ap[c * 128:(c + 1) * 128, :], vb[k], True)
            P.dma("sp", V_d[c * 128:(c + 1) * 128, :], vb[k][:], vb[k], False)
            yield


def phase_peer_prep(P, nc, u_ap, v_ap, ident, UT_d, V_d):
    with ExitStack() as es:
        idb = P.sbuf(es, "pp_idb", [128, 128], BF16)
        P.dma("pool", idb[:], ident, idb, True)
        ub = [P.sbuf(es, "pp_ub%d" % i, [128, D], BF16) for i in range(3)]
        ut = [P.sbuf(es, "pp_ut%d" % i, [128, 8, 128], BF16) for i in range(3)]
        vb = [P.sbuf(es, "pp_vb%d" % i, [128, D], BF16) for i in range(3)]
        pt = [P.psum(es, "pp_pt%d" % i, [128, 8, 128], BF16) for i in range(2)]
        for c in range(128):
            k = c % 3
            P.dma("pool", ub[k][:], u_ap[c * 128:(c + 1) * 128, :], ub[k], True)
            for dk in range(8):
                P.op("pe", lambda e, dk=dk, k=k, c=c: e.transpose(out=pt[c % 2][:, dk, :], in_=ub[k][:, dk * 128:(dk + 1) * 128],
                                                                 identity=idb[:]), [ub[k], idb], [pt[c % 2]])
            if c % 2:
                P.op("act", lambda e, k=k, c=c: e.copy(out=ut[k][:], in_=pt[c % 2][:]), [pt[c % 2]], [ut[k]])
            else:
                P.op("dve", lambda e, k=k, c=c: e.tensor_copy(out=ut[k][:], in_=pt[c % 2][:]), [pt[c % 2]], [ut[k]])
            P.dma("sp", UT_d[c].rearrange("p (dk q) -> p dk q", q=128), ut[k][:], ut[k], False)
            P.dma("pool", vb[k][:], v_ap[c * 128:(c + 1) * 128, :], vb[k], True)
            P.dma("sp", V_d[c * 128:(c + 1) * 128, :], vb[k][:], vb[k], False)
        P.barrier()


def phase_peer_route(P, nc, x_d, g_ap, wq_ap, k1_ap, k2_ap, ident, iota_ap, hnT_d, tab_d, ntiles=NTOK // 128, stop=99):
    with ExitStack() as es:
        idb = P.sbuf(es, "pr_idb", [128, 128], BF16)
        idf = P.sbuf(es, "pr_idf", [128, 128], F32)
        iot = P.sbuf(es, "pr_iot", [128, 128], F32)
        gt = P.sbuf(es, "pr_gt", [128, D], F32)
        wq = P.sbuf(es, "pr_wq", [128, 8, D], BF16)
        kcat = P.sbuf(es, "pr_kcat", [128, 128], BF16)
        kT = P.sbuf(es, "pr_kT", [128, 128], BF16)
        P.dma("pool", idb[:], ident, idb, True)
        P.dma("sp", idf[:], ident, idf, True)
        P.dma("sp", iot[:], iota_ap, iot, True)
        P.dma("sp", gt[:], bcast_rows(g_ap, 128), gt, True)
        wq_v = wq_ap.rearrange("(dk p) f -> p dk f", p=128)
        for dk in range(0, 8, 4):
            P.dma("pool", wq[:, dk:dk + 4, :], wq_v[:, dk:dk + 4, :], wq, True)
        P.dma("pool", kcat[:, 0:64], k1_ap, kcat, True)
        P.dma("pool", kcat[:, 64:128], k2_ap, kcat, True)
        Ss = [NormScratch(P, es, "pr%d" % i) for i in range(2)]
        P.op("pe", lambda e: e.transpose(out=Ss[0].pt[:, 0, :], in_=kcat[:], identity=idb[:]), [kcat, idb], [Ss[0].pt])
        P.op("act", lambda e: e.copy(out=kT[:], in_=Ss[0].pt[:, 0, :]), [Ss[0].pt], [kT])
        kblk = P.sbuf(es, "pr_kblk", [128, 256], BF16)
        P.op("dve", lambda e: e.memset(kblk[:], 0.0), [], [kblk])
        P.op("dve", lambda e: e.tensor_copy(out=kblk[0:64, 0:128], in_=kT[0:64, :]), [kT], [kblk])
        P.op("dve", lambda e: e.tensor_copy(out=kblk[64:128, 128:256], in_=kT[64:128, :]), [kT], [kblk])
        xts = [P.sbuf(es, "pr_xt%d" % i, [128, D], F32) for i in range(2)]
        hnTs = [P.sbuf(es, "pr_hnT%d" % i, [128, 8, 128], BF16) for i in range(2)]
        qTs = [P.sbuf(es, "pr_qT%d" % i, [128, 8, 128], BF16) for i in range(2)]
        ps_q = P.psum(es, "pr_psq", [128, 4, 128], F32)
        ps_s = [P.psum(es, "pr_pss%d" % i, [128, 512], F32) for i in range(4)]
        ps_t = P.psum(es, "pr_pst", [128, 3, 128], F32)
        s_sb = P.sbuf(es, "pr_s", [128, 16 * 128], F32)
        segs = [view(P, s_sb, "seg%d" % j) for j in range(16)]
        v = P.sbuf(es, "pr_v", [128, 16, 16], F32)
        vseg = [view(P, v, "vseg%d" % j) for j in range(16)]
        ix = P.sbuf(es, "pr_ix", [128, 16, 16], U32)
        ixseg = [view(P, ix, "ixseg%d" % j) for j in range(16)]
        ixf = P.sbuf(es, "pr_ixf", [128, 16, 16], F32)
        cand = P.sbuf(es, "pr_cand", [128, 8, 256], F32)
        cseg = [view(P, cand, "cseg%d" % j) for j in range(8)]
        ts = P.sbuf(es, "pr_ts", [128, 8, 16], F32)
        tsseg = [view(P, ts, "tsseg%d" % j) for j in range(8)]
        pos = P.sbuf(es, "pr_pos", [128, 8, 16], U32)
        posseg = [view(P, pos, "posseg%d" % j) for j in range(8)]
        ku = [P.sbuf(es, "pr_ku%d" % i, [128, 8, 16], U32) for i in range(2)]
        kf = [P.sbuf(es, "pr_kf%d" % i, [128, 8, 16], F32) for i in range(2)]
        eq = [P.sbuf(es, "pr_eq%d" % i, [128, 128, 16], F32) for i in range(2)]
        res = P.sbuf(es, "pr_res", [128, 3, 128], F32)
        resv = [view(P, res, "resv%d" % j) for j in range(3)]
        dd = P.sbuf(es, "pr_dd", [128, 8, 16], F32)
        ee = P.sbuf(es, "pr_ee", [128, 8, 16], F32)
        zz = P.sbuf(es, "pr_zz", [128, 8], F32)
        rz = P.sbuf(es, "pr_rz", [128, 8], F32)
        resT = [P.sbuf(es, "pr_resT%d" % i, [128, 3, 128], F32) for i in range(2)]
        tab_v = tab_d.rearrange("a p t -> p a t")
        for tile in range(ntiles):
            k = tile % 2
            xt, S, hnT, qT = xts[k], Ss[k], hnTs[k], qTs[k]
            grp, sub = tile // 2, tile % 2
            P.dma("sp", xt[:], x_d[tile * 128:(tile + 1) * 128, :], xt, True)
            emit_norm_T(P, xt, 128, gt, idb, hnT, 0, S)
            P.dma("sp", hnT_d[grp].rearrange("p (dk t) -> p dk t", t=GRP)[:, :, sub * 128:(sub + 1) * 128], hnT[:], hnT, False)
            if stop <= 1:
                continue
            for half in range(2):
                for f4 in range(4):
                    fc = half * 4 + f4
                    for dk in range(8):
                        P.op("pe", lambda e, fc=fc, f4=f4, dk=dk: e.matmul(
                            ps_q[:, f4, :], lhsT=wq[:, dk, fc * 128:(fc + 1) * 128], rhs=hnT[:, dk, :],
                            start=(dk == 0), stop=(dk == 7)), [wq, hnT], [ps_q])
                P.op("act", lambda e, half=half: e.copy(out=qT[:, half * 4:(half + 1) * 4, :], in_=ps_q[:]), [ps_q], [qT])
            if stop <= 2:
                continue
            for b4 in range(4):
                for hh in range(2):
                    h = b4 * 2 + hh
                    P.op("pe", lambda e, b4=b4, hh=hh, h=h: e.matmul(
                        ps_s[b4][:, hh * 256:(hh + 1) * 256], lhsT=qT[:, h, :], rhs=kblk[:, :],
                        start=True, stop=True), [qT, kblk], [ps_s[b4]])
                P.op("act", lambda e, b4=b4: e.copy(out=s_sb[:, b4 * 512:(b4 + 1) * 512], in_=ps_s[b4][:]),
                     [ps_s[b4]], segs[b4 * 4:(b4 + 1) * 4])
            if stop <= 3:
                continue
            sg = lambda j: s_sb[:, j * 128:(j + 1) * 128]
            for j in range(16):
                P.op("dve", lambda e, j=j: e.max(out=v[:, j, 0:8], in_=sg(j)), [segs[j]], [vseg[j]])
            for j in range(16):
                P.op("dve", lambda e, j=j: e.max_index(out=ix[:, j, 0:8], in_max=v[:, j, 0:8], in_values=sg(j)),
                     [segs[j], vseg[j]], [ixseg[j]])
            for j in range(16):
                P.op("dve", lambda e, j=j: e.match_replace(out=sg(j), in_to_replace=v[:, j, 0:8], in_values=sg(j),
                                                           imm_value=NEG), [segs[j], vseg[j]], [segs[j]])
            for j in range(16):
                P.op("dve", lambda e, j=j: e.max(out=v[:, j, 8:16], in_=sg(j)), [segs[j]], [vseg[j]])
            for j in range(16):
                P.op("dve", lambda e, j=j: e.max_index(out=ix[:, j, 8:16], in_max=v[:, j, 8:16], in_values=sg(j)),
                     [segs[j], vseg[j]], [ixseg[j]])
            if stop <= 4:
                continue
            P.op("dve", lambda e: e.tensor_copy(out=ixf[:], in_=ix[:]), ixseg, [ixf])
            P.op("dve", lambda e: e.tensor_tensor(
                out=fap(cand[:], [(256, 8), (16, 16), (1, 16)]),
                in0=fap(v[:], [(32, 8), (1, 16), (0, 16)]),
                in1=fap(v[:, 1, :], [(32, 8), (0, 16), (1, 16)]), op=ALU.add), vseg, cseg)
            cs = lambda h: cand[:, h, :]
            for h in range(8):
                P.op("dve", lambda e, h=h: e.max(out=ts[:, h, 0:8], in_=cs(h)), [cseg[h]], [tsseg[h]])
            for h in range(8):
                P.op("dve", lambda e, h=h: e.max_index(out=pos[:, h, 0:8], in_max=ts[:, h, 0:8], in_values=cs(h)),
                     [cseg[h], tsseg[h]], [posseg[h]])
            for h in range(8):
                P.op("dve", lambda e, h=h: e.match_replace(out=cs(h), in_to_replace=ts[:, h, 0:8], in_values=cs(h),
                                                           imm_value=NEG), [cseg[h], tsseg[h]], [cseg[h]])
            for h in range(8):
                P.op("dve", lambda e, h=h: e.max(out=ts[:, h, 8:16], in_=cs(h)), [cseg[h]], [tsseg[h]])
            for h in range(8):
                P.op("dve", lambda e, h=h: e.max_index(out=pos[:, h, 8:16], in_max=ts[:, h, 8:16], in_values=cs(h)),
                     [cseg[h], tsseg[h]], [posseg[h]])
            if stop <= 5:
                continue
            P.op("dve", lambda e: e.tensor_single_scalar(out=ku[0][:], in_=pos[:], scalar=4, op=ALU.logical_shift_right),
                 posseg, [ku[0]])
            P.op("dve", lambda e: e.tensor_single_scalar(out=ku[1][:], in_=pos[:], scalar=15, op=ALU.bitwise_and),
                 posseg, [ku[1]])
            for sd in range(2):
                P.op("dve", lambda e, sd=sd: e.tensor_copy(out=kf[sd][:], in_=ku[sd][:]), [ku[sd]], [kf[sd]])
            if stop <= 6:
                continue
            for sd in range(2):
                P.op("dve", lambda e, sd=sd: e.tensor_tensor(
                    out=eq[sd][:], in0=fap(iot[:], [(0, 128), (1, 16)]), in1=fap(kf[sd][:], [(1, 128), (0, 16)]),
                    op=ALU.is_equal), [iot, kf[sd]], [eq[sd]])
                P.op("dve", lambda e, sd=sd: e.tensor_tensor(
                    out=fap(eq[sd][:], [(256, 8), (16, 16), (1, 16)]), in0=fap(eq[sd][:], [(256, 8), (16, 16), (1, 16)]),
                    in1=fap(ixf[:, sd, :], [(32, 8), (0, 16), (1, 16)]), op=ALU.mult), [eq[sd], ixf], [eq[sd]])
                P.op("dve", lambda e, sd=sd: e.tensor_reduce(out=res[:, sd, :], in_=eq[sd][:], axis=AX.X, op=ALU.add),
                     [eq[sd]], [resv[sd]])
            if stop <= 7:
                continue
            P.op("dve", lambda e: e.tensor_tensor(out=dd[:], in0=ts[:], in1=fap(ts[:], [(16, 8), (0, 16)]),
                                                  op=ALU.subtract), tsseg, [dd])
            P.op("act", lambda e: e.activation(out=ee[:], in_=dd[:], func=AF.Exp), [dd], [ee])
            P.op("dve", lambda e: e.tensor_reduce(out=zz[:], in_=ee[:], axis=AX.X, op=ALU.add), [ee], [zz])
            P.op("dve", lambda e: e.reciprocal(out=rz[:], in_=zz[:]), [zz], [rz])
            P.op("dve", lambda e: e.tensor_tensor(out=fap(res[:, 2, :], [(16, 8), (1, 16)]), in0=ee[:],
                                                  in1=fap(rz[:], [(1, 8), (0, 16)]), op=ALU.mult), [ee, rz], [resv[2]])
            if stop <= 8:
                continue
            for a in range(3):
                P.op("pe", lambda e, a=a: e.transpose(out=ps_t[:, a, :], in_=res[:, a, :], identity=idf[:]),
                     [resv[a], idf], [ps_t])
            P.op("act", lambda e, k=k: e.copy(out=resT[k][:], in_=ps_t[:]), [ps_t], [resT[k]])
            P.dma("sp", tab_v[:, :, tile * 128:(tile + 1) * 128], resT[k][:], resT[k], False)
        P.barrier()


def phase_peer_experts(P, nc, x_d, hnT_d, tab_d, UT_d, V_d, iota_ap, out_d, gfin_ap=None, ngrp=NGRP):
    SUBT = 8
    with ExitStack() as es:
        iot = P.sbuf(es, "pe_iot", [128, 128], BF16)
        P.dma("pool", iot[:], iota_ap, iot, True)
        GS = [P.sbuf(es, "pe_GS%d" % i, [128, GRP, 128], BF16) for i in range(2)]
        NR = 6
        utr = [P.sbuf(es, "pe_ut%d" % i, [128, 8, 128], BF16) for i in range(NR)]
        vr = [P.sbuf(es, "pe_v%d" % i, [128, D], BF16) for i in range(NR)]
        hnT = [P.sbuf(es, "pe_hnT%d" % i, [128, 8, GRP], BF16) for i in range(2)]
        tab = [P.sbuf(es, "pe_tab%d" % i, [128, 3, GRP], F32) for i in range(2)]
        xt = [P.sbuf(es, "pe_xt%d" % i, [128, D], F32) for i in range(2)]
        NSET = 3
        O1 = [P.sbuf(es, "pe_O1_%d" % i, [128, SUBT, 128], BF16) for i in range(NSET)]
        O2g = [P.sbuf(es, "pe_O2g_%d" % i, [128, SUBT, 128], BF16) for i in range(NSET)]
        tabb = [P.sbuf(es, "pe_tabb%d" % i, [128, 3, GRP], BF16) for i in range(2)]
        ga = [P.sbuf(es, "pe_ga%d" % i, [128, GRP], BF16) for i in range(3)]
        pT = [P.sbuf(es, "pe_pT%d" % i, [128, GRP], BF16) for i in range(4)]
        po = [[P.psum(es, "pe_po%d%d" % (i, j), [128, 512], F32) for j in range(2)] for i in range(2)]
        pa = [P.psum(es, "pe_pa%d" % i, [128, 512], F32) for i in range(3)]
        pav = pa
        pa_ap = lambda s: pa[s][:, 0:GRP]
        pg = [P.psum(es, "pe_pg%d" % i, [128, 4, 128], F32) for i in range(1)]
        if gfin_ap is not None:
            gfin = P.sbuf(es, "pe_gfin", [128, D], F32)
            P.dma("sp", gfin[:], bcast_rows(gfin_ap, 128), gfin, True)
            S = NormScratch.__new__(NormScratch)
            S.junk = P.sbuf(es, "pe_nj", [128, D], BF16)
            S.ss = P.sbuf(es, "pe_nss", [128, 1], F32)
            S.ms = P.sbuf(es, "pe_nms", [128, 1], F32)
            S.rstd = P.sbuf(es, "pe_nrs", [128, 1], F32)
        tab_v = tab_d.rearrange("a p t -> p a t")

        def load_group(g):
            P.dma("sp", tab[g % 2][:], tab_v[:, :, g * GRP:(g + 1) * GRP], tab[g % 2], True)
            P.op("pool", lambda e: e.tensor_copy(out=tabb[g % 2][:], in_=tab[g % 2][:]), [tab[g % 2]], [tabb[g % 2]])

        def stageA(g, sb):
            tb = tabb[g % 2]
            t0 = sb * SUBT
            k = sb % NSET
            P.op("dve", lambda e: e.tensor_tensor(out=O1[k][:], in0=fap(iot[:], [(0, SUBT), (1, 128)]),
                                                  in1=fap(tb[:, 0, t0:t0 + SUBT], [(1, SUBT), (0, 128)]), op=ALU.is_equal),
                 [iot, tb], [O1[k]])
            P.op("dve", lambda e: e.tensor_tensor(out=O2g[k][:], in0=fap(iot[:], [(0, SUBT), (1, 128)]),
                                                  in1=fap(tb[:, 1, t0:t0 + SUBT], [(1, SUBT), (0, 128)]), op=ALU.is_equal),
                 [iot, tb], [O2g[k]])
            P.op("pool", lambda e: e.tensor_tensor(out=O2g[k][:], in0=O2g[k][:],
                                                   in1=fap(tb[:, 2, t0:t0 + SUBT], [(1, SUBT), (0, 128)]), op=ALU.mult),
                 [O2g[k], tb], [O2g[k]])

        def stageB(g, sb, quarters=(0, 1)):
            k = sb % NSET
            for q4 in quarters:
                pgk = pg[0]
                for tl in range(4):
                    t = q4 * 4 + tl
                    P.op("pe", lambda e, t=t, tl=tl, pgk=pgk: e.matmul(pgk[:, tl, :], lhsT=O2g[k][:, t, :], rhs=O1[k][:, t, :],
                                                                     start=True, stop=True), [O2g[k], O1[k]], [pgk])

        def stageC(g, sb, quarters=(0, 1)):
            gs = GS[g % 2]
            t0 = sb * SUBT
            for q4 in quarters:
                pgk = pg[0]
                tt = t0 + q4 * 4
                P.op("act", lambda e, pgk=pgk, tt=tt: e.copy(out=gs[:, tt:tt + 4, :], in_=pgk[:]),
                     [pgk], [gs])

        def gbuild_sub(g, sb):
            stageA(g, sb)
            for q4 in range(2):
                stageB(g, sb, (q4,))
                stageC(g, sb, (q4,))

        nsubs = GRP // SUBT
        load_group(0)
        for sb in range(nsubs):
            gbuild_sub(0, sb)
        jobs = [(g, c) for g in range(ngrp) for c in range(128)]

        def emit_pa(i):
            g, c = jobs[i]
            hg = hnT[g % 2]
            if c == 0:
                P.dma("sp", hg[:], hnT_d[g].rearrange("p (dk t) -> p dk t", t=GRP), hg, True)
                if g + 1 < ngrp:
                    load_group(g + 1)
            r = i % NR
            P.dma("sp", utr[r][:], UT_d[c].rearrange("p (dk q) -> p dk q", q=128), utr[r], True)
            P.dma("sp", vr[r][:], V_d[c * 128:(c + 1) * 128, :], vr[r], True)
            s = i % 3
            for dk in range(8):
                P.op("pe", lambda e, dk=dk, r=r, s=s: e.matmul(pa_ap(s), lhsT=utr[r][:, dk, :], rhs=hg[:, dk, :],
                                                             start=(dk == 0), stop=(dk == 7)), [utr[r], hg], [pav[s]])

        emit_pa(0)
        emit_pa(1)
        for i, (g, c) in enumerate(jobs):
            if i + 2 < len(jobs):
                emit_pa(i + 2)
            gs = GS[g % 2]
            r = i % NR
            s = i % 3
            gk = ga[i % 3]
            pk = pT[i % 4]
            P.op("act", lambda e, gk=gk, s=s: e.activation(out=gk[:], in_=pa_ap(s), func=AF.Gelu), [pav[s]], [gk])
            P.op("dve", lambda e, gk=gk, pk=pk, c=c, gs=gs: e.tensor_tensor(out=pk[:], in0=gk[:], in1=fap(gs[:, 0, c:c + 1], [(128, GRP)]),
                                                                           op=ALU.mult), [gk, gs], [pk])
            for tsb in range(2):
                for dh in range(2):
                    P.op("pe", lambda e, tsb=tsb, dh=dh, pk=pk, r=r, c=c: e.matmul(
                        po[tsb][dh][:], lhsT=pk[:, tsb * 128:(tsb + 1) * 128], rhs=vr[r][:, dh * 512:(dh + 1) * 512],
                        start=(c == 0), stop=(c == 127)), [pk, vr[r]], [po[tsb][dh]])
            if g + 1 < ngrp:
                sb = c // 4
                if c % 4 == 0:
                    if sb > 0:
                        stageC(g + 1, sb - 1, (1,))
                    stageA(g + 1, sb)
                elif c % 4 == 1:
                    stageB(g + 1, sb, (0,))
                elif c % 4 == 2:
                    stageC(g + 1, sb, (0,))
                elif c % 4 == 3:
                    stageB(g + 1, sb, (1,))
                    if c == 127:
                        stageC(g + 1, sb, (1,))
            if c == 96:
                for tsb in range(2):
                    row0 = g * GRP + tsb * 128
                    P.dma("sp", xt[tsb][:], x_d[row0:row0 + 128, :], xt[tsb], True)
            if c != 127:
                continue
            for tsb in range(2):
                row0 = g * GRP + tsb * 128
                x_t = xt[tsb]
                for dh in range(2):
                    P.op("dve", lambda e, x_t=x_t, dh=dh, tsb=tsb: e.tensor_tensor(
                        out=x_t[:, dh * 512:(dh + 1) * 512], in0=x_t[:, dh * 512:(dh + 1) * 512], in1=po[tsb][dh][:],
                        op=ALU.add), [x_t, po[tsb][dh]], [x_t])
                if gfin_ap is not None:
                    emit_rstd(P, x_t, 128, S)
                    P.op("dve", lambda e, x_t=x_t: e.scalar_tensor_tensor(out=x_t[:], in0=x_t[:], scalar=S.rstd[:, 0:1],
                                                                         in1=gfin[:], op0=ALU.mult, op1=ALU.mult),
                         [x_t, S.rstd, gfin], [x_t])
                P.dma("sp", out_d[row0:row0 + 128, :], x_t[:], x_t, False)
        P.barrier()


def build_peer(final, phases=(1, 1, 1), debug=False, ntiles=NTOK // 128, stop=99):
    nc = bass.Bass("TRN2", target_bir_lowering=False)
    x = nc.dram_tensor("x", [NTOK, D], F32, kind="ExternalInput").ap()
    g = nc.dram_tensor("g", [D], F32, kind="ExternalInput").ap()
    wq = nc.dram_tensor("wq", [D, D], F32, kind="ExternalInput").ap()
    k1 = nc.dram_tensor("k1", [NKEY, 64], F32, kind="ExternalInput").ap()
    k2 = nc.dram_tensor("k2", [NKEY, 64], F32, kind="ExternalInput").ap()
    u = nc.dram_tensor("u", [NKEY * NKEY, D], F32, kind="ExternalInput").ap()
    v = nc.dram_tensor("v", [NKEY * NKEY, D], F32, kind="ExternalInput").ap()
    ident = nc.dram_tensor("ident", [128, 128], F32, kind="ExternalInput").ap()
    iota = nc.dram_tensor("iota", [128, 128], F32, kind="ExternalInput").ap()
    gfin = nc.dram_tensor("gfin", [D], F32, kind="ExternalInput").ap() if final else None
    out = nc.dram_tensor("out", [NTOK, D], F32, kind="ExternalOutput").ap()
    kd = "ExternalOutput" if debug else "Internal"
    UT_d = nc.dram_tensor("UT_d", [128, 128, D], BF16, kind=kd).ap()
    V_d = nc.dram_tensor("V_d", [NKEY * NKEY, D], BF16, kind=kd).ap()
    hnT_d = nc.dram_tensor("hnT_d", [NGRP, 128, 8 * GRP], BF16, kind=kd).ap()
    tab_d = nc.dram_tensor("tab_d", [3, 128, NTOK], F32, kind=kd).ap()
    with ExitStack() as es:
        P = Prog(nc, es)
        if phases[0]:
            phase_peer_prep(P, nc, u, v, ident, UT_d, V_d)
        if phases[1]:
            phase_peer_route(P, nc, x, g, wq, k1, k2, ident, iota, hnT_d, tab_d, ntiles, stop)
        if phases[2]:
            phase_peer_experts(P, nc, x, hnT_d, tab_d, UT_d, V_d, iota, out, gfin)
    return nc


def consts():
    return {"ident": np.eye(128, dtype=np.float32),
            "iota": np.tile(np.arange(128, dtype=np.float32)[None, :], (128, 1))}


NQT_ = NTOK // 128
NHEAD = 16
DH = 64
SEQ = 8192
NKT = SEQ // 128
NQT = NTOK // 128
NB = SEQ // BLK
BIG = 30000.0


def phase_norm_T(P, nc, x_d, g_ap, ident, hnT_own, ntiles=NQT_):
    with ExitStack() as es:
        idb = P.sbuf(es, "n_idb", [128, 128], BF16)
        gt = P.sbuf(es, "n_gt", [128, D], F32)
        P.dma("pool", idb[:], ident, idb, True)
        P.dma("sp", gt[:], bcast_rows(g_ap, 128), gt, True)
        Ss = [NormScratch(P, es, "n%d" % i) for i in range(2)]
        xts = [P.sbuf(es, "n_xt%d" % i, [128, D], F32) for i in range(2)]
        hnTs = [P.sbuf(es, "n_hnT%d" % i, [128, 8, 128], BF16) for i in range(2)]
        for tile in range(ntiles):
            k = tile % 2
            P.dma("sp", xts[k][:], x_d[tile * 128:(tile + 1) * 128, :], xts[k], True)
            emit_norm_T(P, xts[k], 128, gt, idb, hnTs[k], 0, Ss[k], evac=("act" if k else "dve"))
            P.dma("sp", hnT_own[tile].rearrange("p (dk t) -> p dk t", t=128), hnTs[k][:], hnTs[k], False)
        P.barrier()


def emit_rope(P, src, cs, dst, tmp):
    x1 = fap(src[:, 0:1], [(64, 16), (1, 32)])
    x2 = fap(src[:, 32:33], [(64, 16), (1, 32)])
    cosb = fap(cs[:, 0:1], [(0, 16), (1, 32)])
    sinb = fap(cs[:, 32:33], [(0, 16), (1, 32)])
    o1 = fap(dst[:, 0:1], [(64, 16), (1, 32)])
    o2 = fap(dst[:, 32:33], [(64, 16), (1, 32)])
    t1, t2, t3, t4 = tmp
    P.op("dve", lambda e: e.tensor_tensor(out=t1[:], in0=x1, in1=cosb, op=ALU.mult), [src, cs], [t1])
    P.op("dve", lambda e: e.tensor_tensor(out=t2[:], in0=x2, in1=sinb, op=ALU.mult), [src, cs], [t2])
    P.op("dve", lambda e: e.tensor_tensor(out=o1, in0=t1[:], in1=t2[:], op=ALU.subtract), [t1, t2], [dst])
    P.op("pool", lambda e: e.tensor_tensor(out=t3[:], in0=x2, in1=cosb, op=ALU.mult), [src, cs], [t3])
    P.op("pool", lambda e: e.tensor_tensor(out=t4[:], in0=x1, in1=sinb, op=ALU.mult), [src, cs], [t4])
    P.op("pool", lambda e: e.tensor_tensor(out=o2, in0=t3[:], in1=t4[:], op=ALU.add), [t3, t4], [dst])


def phase_attn_qkv(P, nc, hnT_all, hnT_own, w_qkv, cs_k, cs_q, ident, KT_d, V_d, QT_d):
    with ExitStack() as es:
        idb = P.sbuf(es, "aq_idb", [128, 128], BF16)
        P.dma("pool", idb[:], ident, idb, True)
        w = P.sbuf(es, "aq_w", [128, 8, 3 * D], BF16)
        w_v = w_qkv.rearrange("(dk p) f -> p dk f", p=128)
        for dk in range(8):
            P.dma("pool", w[:, dk, :], w_v[:, dk, :], w, True)
        hn = [P.sbuf(es, "aq_hn%d" % i, [128, 8, 128], BF16) for i in range(2)]
        cs = [P.sbuf(es, "aq_cs%d" % i, [128, 64], F32) for i in range(2)]
        ksb = [P.sbuf(es, "aq_ksb%d" % i, [128, D], F32) for i in range(2)]
        krot = [P.sbuf(es, "aq_krot%d" % i, [128, D], BF16) for i in range(2)]
        vsb = [P.sbuf(es, "aq_vsb%d" % i, [128, D], BF16) for i in range(2)]
        tmp = [[P.sbuf(es, "aq_t%d_%d" % (i, j), [128, 16, 32], F32) for j in range(4)] for i in range(2)]
        kTs = [P.sbuf(es, "aq_kT%d" % i, [128, 8, 128], BF16) for i in range(2)]
        pk = [P.psum(es, "aq_pk%d" % i, [128, 512], F32) for i in range(2)]
        pv = [P.psum(es, "aq_pv%d" % i, [128, 512], F32) for i in range(2)]
        pt = [P.psum(es, "aq_pt%d" % i, [128, 8, 128], BF16) for i in range(2)]
        KT_v = KT_d.rearrange("(hp two) d k -> (two d) hp k", two=2)
        QT_v = QT_d.rearrange("(hp two) d k -> (two d) hp k", two=2)

        def proj(hk, col0, ps):
            for half in range(2):
                for dk in range(8):
                    P.op("pe", lambda e, half=half, dk=dk: e.matmul(
                        ps[half][:], lhsT=hk[:, dk, :], rhs=w[:, dk, col0 + half * 512: col0 + (half + 1) * 512],
                        start=(dk == 0), stop=(dk == 7)), [hk, w], [ps[half]])

        def rope_part1(i, ps, cs_ap, scale):
            k = i % 2
            P.dma("sp", cs[k][:], cs_ap, cs[k], True)
            for half in range(2):
                P.op("act", lambda e, half=half, k=k: e.activation(out=ksb[k][:, half * 512:(half + 1) * 512], in_=ps[half][:],
                                                                func=AF.Copy, scale=scale), [ps[half]], [ksb[k]])
            emit_rope(P, ksb[k], cs[k], krot[k], tmp[k])

        def rope_part2(i, dst_v, col):
            k = i % 2
            for hp in range(8):
                P.op("pe", lambda e, hp=hp, k=k: e.transpose(out=pt[k][:, hp, :], in_=krot[k][:, hp * 128:(hp + 1) * 128],
                                                           identity=idb[:]), [krot[k], idb], [pt[k]])
            P.op("act", lambda e, k=k: e.copy(out=kTs[k][:], in_=pt[k][:]), [pt[k]], [kTs[k]])
            P.dma("sp", dst_v[:, :, col:col + 128], kTs[k][:], kTs[k], False)

        pend = None
        for T in range(NKT):
            k = T % 2
            P.dma("sp", hn[k][:], hnT_all[T].rearrange("p (dk t) -> p dk t", t=128), hn[k], True)
            proj(hn[k], D, pk)
            rope_part1(T, pk, cs_k[T], 1.0)
            proj(hn[k], 2 * D, pv)
            for half in range(2):
                P.op("dve", lambda e, half=half, k=k: e.tensor_copy(out=vsb[k][:, half * 512:(half + 1) * 512], in_=pv[half][:]),
                     [pv[half]], [vsb[k]])
            P.dma("sp", V_d[T * 128:(T + 1) * 128, :], vsb[k][:], vsb[k], False)
            if pend is not None:
                rope_part2(*pend)
            pend = (T, KT_v, T * 128)
        for t in range(NQT):
            i = NKT + t
            k = i % 2
            P.dma("sp", hn[k][:], hnT_own[t].rearrange("p (dk t) -> p dk t", t=128), hn[k], True)
            proj(hn[k], 0, pk)
            rope_part1(i, pk, cs_q[t], 0.125)
            if pend is not None:
                rope_part2(*pend)
            pend = (i, QT_v, t * 128)
        rope_part2(*pend)
        P.barrier()


def phase_attn_core(P, nc, x_d, KT_d, V_d, QT_d, E_ap, elig_ap, ownm1_ap, cm_ap, w_o, ident, out_d):
    with ExitStack() as es0:
        idb = P.sbuf(es0, "ac_idb", [128, 128], BF16)
        P.dma("pool", idb[:], ident, idb, True)
        attn = P.sbuf(es0, "ac_attn", [128, NQT, D], BF16)
        with ExitStack() as es:
            KT = [P.sbuf(es, "ac_KT%d" % i, [128, SEQ], BF16) for i in range(2)]
            QT = [P.sbuf(es, "ac_QT%d" % i, [128, NTOK], BF16) for i in range(2)]
            Vh = [P.sbuf(es, "ac_V%d" % i, [128, NKT, 130], BF16) for i in range(2)]
            elig = P.sbuf(es, "ac_elig", [128, NBLK, NB], F32)
            ownm1 = P.sbuf(es, "ac_ownm1", [128, NBLK, NB], F32)
            cm = P.sbuf(es, "ac_cm", [128, 2, BLK], BF16)
            km = [P.sbuf(es, "ac_km%d" % i, [128, NB], F32) for i in range(2)]
            kmb = [P.sbuf(es, "ac_kmb%d" % i, [128, NB], BF16) for i in range(2)]
            gm = [P.sbuf(es, "ac_gm%d" % i, [128, NB], F32) for i in range(4)]
            top8 = [P.sbuf(es, "ac_top8%d" % i, [128, 8], F32) for i in range(4)]
            thr = [P.sbuf(es, "ac_thr%d" % i, [128, 1], F32) for i in range(4)]
            sel = [P.sbuf(es, "ac_sel%d" % i, [128, NB], F32) for i in range(4)]
            NM = [P.sbuf(es, "ac_NM%d" % i, [128, 128], BF16) for i in range(4)]
            PT = [P.sbuf(es, "ac_PT%d" % i, [128, 2, BLK], BF16) for i in range(3)]
            rs = [P.sbuf(es, "ac_rs%d" % i, [128, 1], F32) for i in range(2)]
            pS = [P.psum(es, "ac_pS%d" % i, [128, 2, BLK], F32) for i in range(2)]
            pO = [[P.psum(es, "ac_pO%d%d" % (i, j), [128, 512], F32) for j in range(2)] for i in range(2)]
            pG = P.psum(es, "ac_pG", [128, 512], F32)
            pN = P.psum(es, "ac_pN", [128, 128], BF16)
            P.dma("sp", elig[:], elig_ap.rearrange("(o a) b -> o a b", o=1).to_broadcast([128, NBLK, NB]), elig, True)
            P.dma("sp", ownm1[:], ownm1_ap.rearrange("(o a) b -> o a b", o=1).to_broadcast([128, NBLK, NB]), ownm1, True)
            P.dma("pool", cm[:], cm_ap, cm, True)
            for i in range(4):
                P.op("dve", lambda e, i=i: e.memset(NM[i][:], 0.0), [], [NM[i]])
            for i in range(2):
                P.op("pool", lambda e, i=i: e.memset(KT[i][64:128, :], 0.0), [], [KT[i]])
                P.dma("pool", KT[i][64:96, :], E_ap, KT[i], True)
                P.op("pool", lambda e, i=i: e.memset(QT[i][64:128, :], 0.0), [], [QT[i]])
                P.op("dve", lambda e, i=i: e.memset(Vh[i][:, :, 0:1], 1.0), [], [Vh[i]])
                P.op("dve", lambda e, i=i: e.memset(Vh[i][:, :, 129:130], 1.0), [], [Vh[i]])
            for i in range(2):
                P.op("dve", lambda e, i=i: e.memset(km[i][:], 0.0), [], [km[i]])
            V_v = V_d.rearrange("(T p) f -> p T f", p=128)
            st = dict(npS=0, nPT=0, ngate=0)

            def load_head(h):
                hp = h // 2
                if h % 2 == 0:
                    vb = Vh[hp % 2]
                    for q4 in range(4):
                        P.dma("sp", vb[:, q4 * 16:(q4 + 1) * 16, 1:129], V_v[:, q4 * 16:(q4 + 1) * 16, hp * 128:(hp + 1) * 128],
                              vb, True)
                kt_, qt_ = KT[h % 2], QT[h % 2]
                P.dma("sp", kt_[0:64, :], KT_d[h], kt_, True)
                P.dma("sp", qt_[0:64, :], QT_d[h], qt_, True)
                P.op("dve", lambda e: e.tensor_reduce(out=km[h % 2][0:64, :], in_=fap(kt_[0:64, 0:1], [(BLK, NB), (1, BLK)]),
                                                      axis=AX.X, op=ALU.add), [kt_], [km[h % 2]])
                P.op("dve", lambda e: e.tensor_copy(out=kmb[h % 2][:], in_=km[h % 2][:]), [km[h % 2]], [kmb[h % 2]])

            def gating_a(h, qt):
                qt_ = QT[h % 2]
                jl = qt // 2
                g = qt % 4
                P.op("pe", lambda e: e.matmul(pG[:, 0:NB], lhsT=qt_[:, qt * 128:(qt + 1) * 128], rhs=kmb[h % 2][:, :],
                                              start=True, stop=True), [qt_, kmb[h % 2]], [pG])
                P.op("dve", lambda e: e.tensor_tensor(out=gm[g][:], in0=pG[:, 0:NB], in1=elig[:, jl, :], op=ALU.add),
                     [pG, elig], [gm[g]])
                P.op("dve", lambda e: e.max(out=top8[g][:], in_=gm[g][:]), [gm[g]], [top8[g]])
                P.op("dve", lambda e: e.tensor_scalar(out=thr[g][:], in0=top8[g][:, 2:3], scalar1=-1.0e29, scalar2=None,
                                                      op0=ALU.max), [top8[g]], [thr[g]])
                P.op("dve", lambda e: e.tensor_scalar(out=sel[g][:], in0=gm[g][:], scalar1=thr[g][:, 0:1], scalar2=None,
                                                      op0=ALU.is_ge), [gm[g], thr[g]], [sel[g]])
                P.op("dve", lambda e: e.scalar_tensor_tensor(out=NM[g][:, 64:96], in0=sel[g][:], scalar=BIG,
                                                             in1=ownm1[:, jl, :], op0=ALU.mult, op1=ALU.add),
                     [sel[g], ownm1], [NM[g]])

            def gating_b(h, qt):
                qt_ = QT[h % 2]
                g = qt % 4
                P.op("pe", lambda e: e.transpose(out=pN[:], in_=NM[g][:], identity=idb[:]), [NM[g], idb], [pN])
                P.op("act", lambda e: e.copy(out=qt_[64:96, qt * 128:(qt + 1) * 128], in_=pN[64:96, :]), [pN], [qt_])

            def attention(h, jl):
                hb = h % 2
                hp = h // 2
                kt_, qt_ = KT[h % 2], QT[h % 2]
                vb = Vh[hp % 2]
                po = pO[jl % 2]
                blocks = [rr * 16 + m for m in range(jl + 1) for rr in range(2)]
                nb_ = len(blocks)
                def qk(bi):
                    b = blocks[bi]
                    ps = pS[(st["npS"] + bi) % 2]
                    for kt2 in range(2):
                        T = b * 2 + kt2
                        P.op("pe", lambda e, kt2=kt2, T=T: e.matmul(
                            ps[:, kt2, :], lhsT=kt_[:, T * 128:(T + 1) * 128], rhs=qt_[:, jl * BLK:(jl + 1) * BLK],
                            start=True, stop=True), [kt_, qt_], [ps])

                qk(0)
                for bi, b in enumerate(blocks):
                    if bi + 1 < nb_:
                        qk(bi + 1)
                    ps = pS[(st["npS"] + bi) % 2]
                    pt_ = PT[st["nPT"] % 3]
                    st["nPT"] += 1
                    P.op("act", lambda e: e.activation(out=pt_[:], in_=ps[:], func=AF.Exp), [ps], [pt_])
                    if b == jl:
                        P.op("pool", lambda e: e.tensor_tensor(out=pt_[:], in0=pt_[:], in1=cm[:, :, :], op=ALU.mult),
                             [pt_, cm], [pt_])
                    for kt2 in range(2):
                        T = b * 2 + kt2
                        for qs in range(2):
                            P.op("pe", lambda e, kt2=kt2, qs=qs, T=T: e.matmul(
                                po[qs][:, 0:65], lhsT=pt_[:, kt2, qs * 128:(qs + 1) * 128],
                                rhs=vb[:, T, hb * 65:hb * 65 + 65],
                                start=(bi == 0 and kt2 == 0), stop=(bi == nb_ - 1 and kt2 == 1)), [pt_, vb], [po[qs]])
                st["npS"] += nb_
                for qs in range(2):
                    sc = 0 if hb == 0 else 64
                    oc = 1 if hb == 0 else 0
                    P.op("dve", lambda e, qs=qs: e.reciprocal(out=rs[qs][:], in_=po[qs][:, sc:sc + 1]), [po[qs]], [rs[qs]])
                    P.op("dve", lambda e, qs=qs: e.tensor_scalar(
                        out=attn[:, jl * 2 + qs, h * 64:(h + 1) * 64], in0=po[qs][:, oc:oc + 64], scalar1=rs[qs][:, 0:1],
                        scalar2=None, op0=ALU.mult), [po[qs], rs[qs]], [attn])

            load_head(0)
            for qt in range(NQT):
                gating_a(0, qt)
                gating_b(0, qt)
            for h in range(NHEAD):
                if h + 1 < NHEAD:
                    load_head(h + 1)
                for jl in range(NBLK):
                    attention(h, jl)
                    if h + 1 < NHEAD:
                        gating_a(h + 1, 2 * jl)
                        gating_a(h + 1, 2 * jl + 1)
                        if jl > 0:
                            gating_b(h + 1, 2 * jl - 2)
                            gating_b(h + 1, 2 * jl - 1)
                if h + 1 < NHEAD:
                    gating_b(h + 1, NQT - 2)
                    gating_b(h + 1, NQT - 1)
        P.barrier()
        with ExitStack() as es:
            wo = P.sbuf(es, "ao_wo", [128, 8, D], BF16)
            w_o_v = w_o.rearrange("(dk p) f -> p dk f", p=128)
            for dk in range(0, 8, 4):
                P.dma("pool", wo[:, dk:dk + 4, :], w_o_v[:, dk:dk + 4, :], wo, True)
            aT = [P.sbuf(es, "ao_aT%d" % i, [128, 8, 128], BF16) for i in range(2)]
            xt = [P.sbuf(es, "ao_xt%d" % i, [128, D], F32) for i in range(2)]
            pt = [P.psum(es, "ao_pt%d" % i, [128, 8, 128], BF16) for i in range(2)]
            py = [P.psum(es, "ao_py%d" % i, [128, 512], F32) for i in range(2)]
            npy = 0
            for qt in range(NQT):
                k = qt % 2
                P.dma("sp", xt[k][:], x_d[qt * 128:(qt + 1) * 128, :], xt[k], True)
                for fk in range(8):
                    P.op("pe", lambda e, fk=fk, k=k, qt=qt: e.transpose(out=pt[k][:, fk, :], in_=attn[:, qt, fk * 128:(fk + 1) * 128],
                                                                     identity=idb[:]), [attn, idb], [pt[k]])
                P.op("act", lambda e, k=k: e.copy(out=aT[k][:], in_=pt[k][:]), [pt[k]], [aT[k]])
                for fh in range(2):
                    pyk = py[npy % 2]
                    npy += 1
                    for fk in range(8):
                        P.op("pe", lambda e, fk=fk, k=k, fh=fh, pyk=pyk: e.matmul(pyk[:], lhsT=aT[k][:, fk, :], rhs=wo[:, fk, fh * 512:(fh + 1) * 512],
                                                                                start=(fk == 0), stop=(fk == 7)), [aT[k], wo], [pyk])
                    P.op("dve", lambda e, k=k, fh=fh, pyk=pyk: e.tensor_tensor(out=xt[k][:, fh * 512:(fh + 1) * 512], in0=xt[k][:, fh * 512:(fh + 1) * 512],
                                                                              in1=pyk[:], op=ALU.add), [xt[k], pyk], [xt[k]])
                P.dma("sp", out_d[qt * 128:(qt + 1) * 128, :], xt[k][:], xt[k], False)
        P.barrier()


def attn_tables(core):
    r = core % 2
    half = DH // 2
    inv = (np.float32(10000.0) ** (-np.arange(half, dtype=np.float32) / np.float32(half))).astype(np.float32)

    def cs_for(rr, lt):
        j = 2 * (lt // 2) + rr
        pos = (j * BLK + (lt % 2) * 128 + np.arange(128)).astype(np.float32)
        ang = (pos[:, None] * inv[None, :]).astype(np.float32)
        return np.concatenate([np.cos(ang), np.sin(ang)], axis=1).astype(np.float32)

    rank_of = lambda T: r if T < NQT else 1 - r
    cs_k = np.stack([cs_for(rank_of(T), T % NQT) for T in range(NKT)])
    cs_q = np.stack([cs_for(r, t) for t in range(NQT)])
    nglob = np.array([2 * (b % 16) + (r if b < 16 else 1 - r) for b in range(NB)])
    E = np.zeros((NB, SEQ), np.float32)
    for b in range(NB):
        E[b, b * BLK:(b + 1) * BLK] = 1.0
    elig = np.zeros((NBLK, NB), np.float32)
    ownm1 = np.zeros((NBLK, NB), np.float32)
    for jl in range(NBLK):
        j = 2 * jl + r
        elig[jl] = np.where(nglob < j, 0.0, -1.0e30)
        ownm1[jl] = np.where(nglob == j, 0.0, -BIG)
    tri = np.zeros((128, 2, BLK), np.float32)
    for kt in range(2):
        kp = kt * 128 + np.arange(128)
        tri[:, kt, :] = (kp[:, None] <= np.arange(BLK)[None, :]).astype(np.float32)
    return {"cs_k": cs_k, "cs_q": cs_q, "E": E, "elig": elig, "ownm1": ownm1, "cm": np.ascontiguousarray(tri)}


def build_norm():
    nc = bass.Bass("TRN2", target_bir_lowering=False)
    x = nc.dram_tensor("x", [NTOK, D], F32, kind="ExternalInput").ap()
    g = nc.dram_tensor("g", [D], F32, kind="ExternalInput").ap()
    ident = nc.dram_tensor("ident", [128, 128], F32, kind="ExternalInput").ap()
    hnT_own = nc.dram_tensor("hnT_own", [NQT, 128, D], BF16, kind="ExternalOutput").ap()
    with ExitStack() as es:
        P = Prog(nc, es)
        phase_norm_T(P, nc, x, g, ident, hnT_own)
    return nc


def attn_decls(nc):
    d = {}
    d["cs_k"] = nc.dram_tensor("cs_k", [NKT, 128, 64], F32, kind="ExternalInput").ap()
    d["cs_q"] = nc.dram_tensor("cs_q", [NQT, 128, 64], F32, kind="ExternalInput").ap()
    d["E"] = nc.dram_tensor("E", [NB, SEQ], F32, kind="ExternalInput").ap()
    d["elig"] = nc.dram_tensor("elig", [NBLK, NB], F32, kind="ExternalInput").ap()
    d["ownm1"] = nc.dram_tensor("ownm1", [NBLK, NB], F32, kind="ExternalInput").ap()
    d["cm"] = nc.dram_tensor("cm", [128, 2, BLK], F32, kind="ExternalInput").ap()
    return d


def build_attn(debug=False):
    nc = bass.Bass("TRN2", target_bir_lowering=False)
    x = nc.dram_tensor("x", [NTOK, D], F32, kind="ExternalInput").ap()
    hnT_all = nc.dram_tensor("hnT_all", [NKT, 128, D], BF16, kind="ExternalInput").ap()
    hnT_own = nc.dram_tensor("hnT_own", [NQT, 128, D], BF16, kind="ExternalInput").ap()
    w_qkv = nc.dram_tensor("w_qkv", [D, 3 * D], F32, kind="ExternalInput").ap()
    w_o = nc.dram_tensor("w_o", [D, D], F32, kind="ExternalInput").ap()
    ident = nc.dram_tensor("ident", [128, 128], F32, kind="ExternalInput").ap()
    t = attn_decls(nc)
    out = nc.dram_tensor("out", [NTOK, D], F32, kind="ExternalOutput").ap()
    kd = "ExternalOutput" if debug else "Internal"
    KT_d = nc.dram_tensor("KT_d", [NHEAD, DH, SEQ], BF16, kind=kd).ap()
    V_d = nc.dram_tensor("V_d", [SEQ, D], BF16, kind=kd).ap()
    QT_d = nc.dram_tensor("QT_d", [NHEAD, DH, NTOK], BF16, kind=kd).ap()
    with ExitStack() as es:
        P = Prog(nc, es)
        phase_attn_qkv(P, nc, hnT_all, hnT_own, w_qkv, t["cs_k"], t["cs_q"], ident, KT_d, V_d, QT_d)
        phase_attn_core(P, nc, x, KT_d, V_d, QT_d, t["E"], t["elig"], t["ownm1"], t["cm"], w_o, ident, out)
    return nc


def build_conv():
    nc = bass.Bass("TRN2", target_bir_lowering=False)
    x0 = nc.dram_tensor("x0", [NTOK, D], F32, kind="ExternalInput").ap()
    xh = nc.dram_tensor("xh", [NBLK * 2, D], F32, kind="ExternalInput").ap()
    g = nc.dram_tensor("g", [D], F32, kind="ExternalInput").ap()
    w_in = nc.dram_tensor("w_in", [D, 3 * D], F32, kind="ExternalInput").ap()
    conv_w = nc.dram_tensor("conv_w", [3, D], F32, kind="ExternalInput").ap()
    w_out = nc.dram_tensor("w_out", [D, D], F32, kind="ExternalInput").ap()
    ident = nc.dram_tensor("ident", [128, 128], F32, kind="ExternalInput").ap()
    x1 = nc.dram_tensor("x1", [NTOK, D], F32, kind="ExternalOutput").ap()
    with ExitStack() as es:
        P = Prog(nc, es)
        phase_conv(P, nc, x0, xh, g, w_in, conv_w, w_out, ident, x1)
    return nc


_NC_CACHE = {}
NALL = 2 * NTOK


def build_fused():
    nc = bass.Bass("TRN2", target_bir_lowering=False)
    I = lambda name, shape, dt=F32: nc.dram_tensor(name, list(shape), dt, kind="ExternalInput").ap()
    S = lambda name, shape, dt=F32: nc.dram_tensor(name, list(shape), dt, kind="Internal").ap()
    x_all = I("x_all", [NALL, D])
    xh_all = I("xh_all", [2 * NBLK * 2, D])
    norm_mix = I("norm_mix", [2, D])
    norm_ffn = I("norm_ffn", [2, D])
    norm_final = I("norm_final", [D])
    conv_w_in = I("conv_w_in", [D, 3 * D])
    conv_w = I("conv_w", [3, D])
    conv_w_out = I("conv_w_out", [D, D])
    w_qkv = I("w_qkv", [D, 3 * D])
    w_o = I("w_o", [D, D])
    wq = [I("wq%d" % l, [D, D]) for l in range(2)]
    k1 = [I("k1_%d" % l, [NKEY, 64]) for l in range(2)]
    k2 = [I("k2_%d" % l, [NKEY, 64]) for l in range(2)]
    u = [I("u%d" % l, [NKEY * NKEY, D]) for l in range(2)]
    v = [I("v%d" % l, [NKEY * NKEY, D]) for l in range(2)]
    ident = I("ident", [128, 128])
    iota = I("iota", [128, 128])
    t = attn_decls(nc)
    out = nc.dram_tensor("out", [NTOK, D], F32, kind="ExternalOutput").ap()
    x1 = S("x1", [NALL, D])
    x2 = S("x2", [NALL, D])
    x3 = S("x3", [NTOK, D])
    hnT_all = S("hnT_all", [NKT, 128, D], BF16)
    KT_d = S("KT_d", [NHEAD, DH, SEQ], BF16)
    V_d = S("V_d", [SEQ, D], BF16)
    QT_d = S("QT_d", [NHEAD, DH, NTOK], BF16)
    UT = [S("UT%d" % l, [128, 128, D], BF16) for l in range(2)]
    VV = [S("VV%d" % l, [NKEY * NKEY, D], BF16) for l in range(2)]
    hnT_d = S("hnT_d", [NALL // GRP, 128, 8 * GRP], BF16)
    tab_d = S("tab_d", [3, 128, NALL])
    with ExitStack() as es:
        P = Prog(nc, es)
        phase_conv(P, nc, x_all, xh_all, norm_mix[0], conv_w_in, conv_w, conv_w_out, ident, x1, nblk=2 * NBLK,
                   side_jobs=[(u[l], v[l], UT[l], VV[l]) for l in range(2)])
        phase_peer_route(P, nc, x1, norm_ffn[0], wq[0], k1[0], k2[0], ident, iota, hnT_d, tab_d, NALL // 128)
        phase_peer_experts(P, nc, x1, hnT_d, tab_d, UT[0], VV[0], iota, x2, None, NALL // GRP)
        phase_norm_T(P, nc, x2, norm_mix[1], ident, hnT_all, NKT)
        phase_attn_qkv(P, nc, hnT_all, hnT_all, w_qkv, t["cs_k"], t["cs_q"], ident, KT_d, V_d, QT_d)
        phase_attn_core(P, nc, x2, KT_d, V_d, QT_d, t["E"], t["elig"], t["ownm1"], t["cm"], w_o, ident, x3)
        phase_peer_route(P, nc, x3, norm_ffn[1], wq[1], k1[1], k2[1], ident, iota, hnT_d, tab_d, NTOK // 128)
        phase_peer_experts(P, nc, x3, hnT_d, tab_d, UT[1], VV[1], iota, out, norm_final, NTOK // GRP)
        P.wait_all_dma("sp")
    return nc


def kernel(x, norm_mix, norm_ffn, conv_w_in, conv_w, conv_w_out, attn_w_qkv, attn_w_o,
           peer_w_q, peer_k1, peer_k2, peer_u, peer_v, norm_final):
    from concourse.bass_utils import run_bass_kernel_spmd
    f = lambda a: np.ascontiguousarray(np.asarray(a, dtype=np.float32))
    x = f(x)
    B, S, _ = x.shape
    cores = list(range(8))
    if "fused" not in _NC_CACHE:
        _NC_CACHE["fused"] = build_fused()
    shared = {"norm_mix": f(norm_mix), "norm_ffn": f(norm_ffn), "norm_final": f(norm_final),
              "conv_w_in": f(conv_w_in[0]), "conv_w": f(conv_w[0]), "conv_w_out": f(conv_w_out[0]),
              "w_qkv": f(attn_w_qkv[0]), "w_o": f(attn_w_o[0])}
    for l in range(2):
        shared["wq%d" % l] = f(peer_w_q[l])
        shared["k1_%d" % l] = f(peer_k1[l])
        shared["k2_%d" % l] = f(peer_k2[l])
        shared["u%d" % l] = f(peer_u[l])
        shared["v%d" % l] = f(peer_v[l])
    shared.update(consts())
    in_maps = []
    for c in cores:
        xs, xh = shard_tokens(x, c)
        xo, xoh = shard_tokens(x, c ^ 1)
        m = dict(shared)
        m["x_all"] = np.concatenate([xs, xo], axis=0)
        m["xh_all"] = np.concatenate([xh, xoh], axis=0)
        m.update(attn_tables(c))
        in_maps.append(m)
    res = run_bass_kernel_spmd(_NC_CACHE["fused"], in_maps, core_ids=cores)
    outs = [r["out"] for r in res.results]
    return unshard_tokens(outs, B, S)
```
